# Optimizing a Trainium2 kernel written in Bass

```python
import jax, jax.numpy as jnp
from jax import lax
import numpy as np

D_MODEL = 1024
BATCH = 2
SEQ = 8192
DEPTH = 2

ML_HEADS = 4
ML_HEAD_DIM = 256
ML_W = ML_HEADS * ML_HEAD_DIM
ML_CHUNK = 64
SC_W = D_MODEL
CONV_K = 3
SB_HEADS = 16
SB_HEAD_DIM = 64
SB_W = SB_HEADS * SB_HEAD_DIM
SB_BLOCK = 128
N_BRANCH = 3
D_FF = 4 * D_MODEL
EPS = 1e-6
SPLIT_SIZES = (ML_W, ML_W, ML_W, ML_W, ML_HEADS, ML_HEADS,
               SC_W, SC_W, SC_W,
               SB_W, SB_W, SB_W,
               D_MODEL, D_MODEL, D_MODEL)
N_IN = sum(SPLIT_SIZES)

kernel_name = "hybrid_mlstm_shortconv_stickbreaking_block"


def rmsnorm(x, g):
    xf = x.astype(jnp.float32)
    y = xf * lax.rsqrt(jnp.mean(xf * xf, axis=-1, keepdims=True) + EPS)
    return (y * g.astype(jnp.float32)).astype(x.dtype)


def _to_chunks(a, nc, L):
    a = a.reshape((a.shape[0], nc, L) + a.shape[2:])
    return jnp.swapaxes(jnp.moveaxis(a, 1, 0), 2, 3)


def mlstm(q, k, v, i_pre, f_pre):
    B_, S_, H, dh = q.shape
    L = ML_CHUNK
    nc = S_ // L
    f32 = jnp.float32
    qc = _to_chunks(q.astype(f32), nc, L)
    kc = _to_chunks(k.astype(f32) * (dh ** -0.5), nc, L)
    vc = _to_chunks(v.astype(f32), nc, L)
    ic = _to_chunks(i_pre.astype(f32), nc, L)
    lfc = _to_chunks(jax.nn.log_sigmoid(f_pre.astype(f32)), nc, L)
    causal = jnp.tril(jnp.ones((L, L), dtype=bool))

    def step(carry, xs):
        C, n, m = carry
        qb, kb, vb, ib, lfb = xs
        b = jnp.cumsum(lfb, axis=-1)
        dmat = b[..., :, None] - b[..., None, :] + ib[..., None, :]
        dmat = jnp.where(causal, dmat, -jnp.inf)
        inter = b + m[..., None]
        m_t = jnp.maximum(inter, jnp.max(dmat, axis=-1))
        w_intra = jnp.exp(dmat - m_t[..., None])
        a_inter = jnp.exp(inter - m_t)
        s = jnp.einsum('bhtd,bhsd->bhts', qb, kb) * w_intra
        num = (jnp.einsum('bhts,bhsd->bhtd', s, vb)
               + a_inter[..., None] * jnp.einsum('bhvk,bhtk->bhtv', C, qb))
        den = jnp.sum(s, axis=-1) + a_inter * jnp.einsum('bhk,bhtk->bht', n, qb)
        h = num / jnp.maximum(jnp.abs(den), jnp.exp(-m_t))[..., None]
        b_last = b[..., -1]
        g = b_last[..., None] - b + ib
        m_new = jnp.maximum(b_last + m, jnp.max(g, axis=-1))
        decay = jnp.exp(b_last + m - m_new)
        w_state = jnp.exp(g - m_new[..., None])
        C_new = decay[..., None, None] * C + jnp.einsum('bhs,bhsv,bhsk->bhvk', w_state, vb, kb)
        n_new = decay[..., None] * n + jnp.einsum('bhs,bhsk->bhk', w_state, kb)
        return (C_new, n_new, m_new), h

    init = (jnp.zeros((B_, H, dh, dh), f32), jnp.zeros((B_, H, dh), f32), jnp.zeros((B_, H), f32))
    _, hs = lax.scan(step, init, (qc, kc, vc, ic, lfc))
    hs = jnp.moveaxis(jnp.swapaxes(hs, 2, 3), 0, 1)
    return hs.reshape(B_, S_, H, dh)


def short_conv(gate_b, gate_c, u, w):
    z = gate_c * u
    y = lax.conv_general_dilated(
        z, w[:, None, :].astype(z.dtype), window_strides=(1,),
        padding=[(CONV_K - 1, 0)], dimension_numbers=('NWC', 'WIO', 'NWC'),
        feature_group_count=z.shape[-1])
    return gate_b * y


def stick_breaking(q, k, v):
    B_, S_, H, d = q.shape
    nb = S_ // SB_BLOCK
    f32 = jnp.float32
    qf = q.astype(f32) * (d ** -0.5)
    kf = k.astype(f32)
    vf = v.astype(f32)
    q_blocks = jnp.transpose(qf.reshape(B_, nb, SB_BLOCK, H, d), (1, 0, 3, 2, 4))
    starts = jnp.arange(nb, dtype=jnp.int32) * SB_BLOCK
    key_pos = jnp.arange(S_, dtype=jnp.int32)

    def block(args):
        qblk, start = args
        q_pos = start + jnp.arange(SB_BLOCK, dtype=jnp.int32)
        mask = key_pos[None, :] < q_pos[:, None]
        z = jnp.einsum('bhqd,bshd->bhqs', qblk, kf)
        log_beta = jax.nn.log_sigmoid(z)
        log_1m = jnp.where(mask, log_beta - z, 0.0)
        cum = jnp.cumsum(log_1m, axis=-1)
        log_a = log_beta + cum[..., -1:] - cum
        a = jnp.where(mask, jnp.exp(log_a), 0.0)
        return jnp.einsum('bhqs,bshd->bqhd', a, vf)

    out = lax.map(block, (q_blocks, starts))
    return jnp.moveaxis(out, 0, 1).reshape(B_, S_, H * d)


def head_rmsnorm(h, g, n_heads):
    B_, S_, W = h.shape
    hh = h.reshape(B_, S_, n_heads, W // n_heads)
    hh = hh * lax.rsqrt(jnp.mean(hh * hh, axis=-1, keepdims=True) + EPS)
    return hh.reshape(B_, S_, W) * g.astype(jnp.float32)


def setup_inputs(seed: int = 0) -> dict:
    key = jax.random.key(seed)
    ks = jax.random.split(key, 16)
    nrm = jax.random.normal
    x = nrm(ks[0], (BATCH, SEQ, D_MODEL), jnp.float32)
    norm_mix_g = 1.0 + 0.02 * nrm(ks[1], (DEPTH, D_MODEL), jnp.float32)
    w_in = nrm(ks[2], (DEPTH, D_MODEL, N_IN), jnp.float32) * D_MODEL ** -0.5
    b_if = jnp.concatenate([
        0.1 * nrm(ks[3], (DEPTH, ML_HEADS), jnp.float32),
        jnp.linspace(3.0, 6.0, ML_HEADS, dtype=jnp.float32)[None, :]
        + 0.1 * nrm(ks[4], (DEPTH, ML_HEADS), jnp.float32)], axis=-1)
    ml_norm_g = 1.0 + 0.02 * nrm(ks[5], (DEPTH, ML_W), jnp.float32)
    conv_w = nrm(ks[6], (DEPTH, CONV_K, SC_W), jnp.float32) * CONV_K ** -0.5
    w_ml_proj = nrm(ks[7], (DEPTH, ML_W, D_MODEL), jnp.float32) * ML_W ** -0.5
    w_sc_proj = nrm(ks[8], (DEPTH, SC_W, D_MODEL), jnp.float32) * SC_W ** -0.5
    w_sb_proj = nrm(ks[9], (DEPTH, SB_W, D_MODEL), jnp.float32) * SB_W ** -0.5
    w_out = nrm(ks[10], (DEPTH, D_MODEL, D_MODEL), jnp.float32) * D_MODEL ** -0.5
    norm_mlp_g = 1.0 + 0.02 * nrm(ks[11], (DEPTH, D_MODEL), jnp.float32)
    w_up = nrm(ks[12], (DEPTH, D_MODEL, D_FF), jnp.float32) * D_MODEL ** -0.5
    w_down = nrm(ks[13], (DEPTH, D_FF, D_MODEL), jnp.float32) * D_FF ** -0.5
    norm_final_g = 1.0 + 0.02 * nrm(ks[14], (D_MODEL,), jnp.float32)
    return {"x": x, "norm_mix_g": norm_mix_g, "w_in": w_in, "b_if": b_if,
            "ml_norm_g": ml_norm_g, "conv_w": conv_w, "w_ml_proj": w_ml_proj,
            "w_sc_proj": w_sc_proj, "w_sb_proj": w_sb_proj, "w_out": w_out,
            "norm_mlp_g": norm_mlp_g, "w_up": w_up, "w_down": w_down,
            "norm_final_g": norm_final_g}


def reference(x, norm_mix_g, w_in, b_if, ml_norm_g, conv_w, w_ml_proj, w_sc_proj,
              w_sb_proj, w_out, norm_mlp_g, w_up, w_down, norm_final_g):
    B_, S_, _ = x.shape
    offsets = []
    acc = 0
    for sz in SPLIT_SIZES[:-1]:
        acc += sz
        offsets.append(acc)
    for l in range(DEPTH):
        h = rmsnorm(x, norm_mix_g[l])
        proj = h @ w_in[l]
        (ml_q, ml_k, ml_v, ml_o, ml_i, ml_f,
         sc_b, sc_c, sc_u,
         sb_q, sb_k, sb_v,
         g_ml, g_sc, g_sb) = jnp.split(proj, offsets, axis=-1)
        i_pre = ml_i + b_if[l, :ML_HEADS]
        f_pre = ml_f + b_if[l, ML_HEADS:]
        h_tilde = mlstm(ml_q.reshape(B_, S_, ML_HEADS, ML_HEAD_DIM),
                        ml_k.reshape(B_, S_, ML_HEADS, ML_HEAD_DIM),
                        ml_v.reshape(B_, S_, ML_HEADS, ML_HEAD_DIM),
                        i_pre, f_pre).reshape(B_, S_, ML_W)
        y_ml = head_rmsnorm(jax.nn.sigmoid(ml_o.astype(jnp.float32)) * h_tilde,
                            ml_norm_g[l], ML_HEADS).astype(x.dtype)
        y_sc = short_conv(sc_b, sc_c, sc_u, conv_w[l])
        y_sb = stick_breaking(sb_q.reshape(B_, S_, SB_HEADS, SB_HEAD_DIM),
                              sb_k.reshape(B_, S_, SB_HEADS, SB_HEAD_DIM),
                              sb_v.reshape(B_, S_, SB_HEADS, SB_HEAD_DIM)).astype(x.dtype)
        merged = (jax.nn.sigmoid(g_ml) * (y_ml @ w_ml_proj[l])
                  + jax.nn.sigmoid(g_sc) * (y_sc @ w_sc_proj[l])
                  + jax.nn.sigmoid(g_sb) * (y_sb @ w_sb_proj[l]))
        x = x + merged @ w_out[l]
        h2 = rmsnorm(x, norm_mlp_g[l])
        x = x + jnp.square(jax.nn.relu(h2 @ w_up[l])) @ w_down[l]
    return rmsnorm(x, norm_final_g)
```

```python
import contextlib
import numpy as np
import ml_dtypes
import concourse.bass as bass
import concourse.mybir as mybir
from concourse.bass_utils import run_bass_kernel_spmd

F32 = mybir.dt.float32
BF16 = mybir.dt.bfloat16
AF = mybir.ActivationFunctionType
ALU = mybir.AluOpType
AX = mybir.AxisListType
NPBF = ml_dtypes.bfloat16

D = 1024
NCORE = 8
SEQ = 8192
TOK = 2048
ST = 1024
EPS = 1e-6
NEG = -30000.0


class Op:
    __slots__ = ("eng", "fn", "dma", "deps", "signal", "sem", "target", "prev", "idx", "cc", "nobar")


class Sched:
    NDMA = 8

    def __init__(self, nc):
        self.nc = nc
        self.ops = []
        self.last_write = {}
        self.readers = {}
        self.bar = None
        self.bar_seen = set()
        self.bar_start = 0

    def barrier(self, fn):
        prev = [o for o in self.ops[self.bar_start:] if not o.nobar]
        b = self.add("dve", fn)
        seen = {d.idx for d in b.deps}
        b.deps += [o for o in prev if o.idx not in seen]
        self.bar = b
        self.bar_seen = {"dve"}
        self.bar_start = b.idx
        return b

    def add(self, eng, fn, reads=(), writes=(), dma=False, cc=False, nobar=False):
        op = Op()
        op.nobar = nobar
        op.eng, op.fn, op.dma = eng, fn, dma
        op.cc = cc
        op.signal = dma or cc
        op.sem = None
        op.target = 0
        op.prev = None
        op.idx = len(self.ops)
        deps = {}
        for k in reads:
            w = self.last_write.get(k)
            if w is not None:
                deps[w.idx] = w
        for k in writes:
            w = self.last_write.get(k)
            if w is not None:
                deps[w.idx] = w
            for r in self.readers.get(k, ()):
                deps[r.idx] = r
        for k in reads:
            self.readers.setdefault(k, []).append(op)
        for k in writes:
            self.last_write[k] = op
            self.readers[k] = []
        deps.pop(op.idx, None)
        if self.bar is not None and eng not in self.bar_seen:
            deps[self.bar.idx] = self.bar
            self.bar_seen.add(eng)
        op.deps = list(deps.values())
        self.ops.append(op)
        return op

    @staticmethod
    def _skip(d, op):
        return d.eng == "pe" and op.eng == "pe" and not d.dma and not op.dma and not d.cc and not op.cc

    def finalize(self):
        nc = self.nc
        engs = {"pe": nc.tensor, "act": nc.scalar, "dve": nc.vector, "pool": nc.gpsimd, "sp": nc.sync}
        for op in self.ops:
            for d in op.deps:
                if not self._skip(d, op):
                    d.signal = True
        with contextlib.ExitStack() as es:
            esem = {e: es.enter_context(nc.semaphore("s_" + e)) for e in ("pe", "act", "dve", "pool")}
            dsem = {e: [es.enter_context(nc.semaphore("d_%s%d" % (e, i))) for i in range(self.NDMA)]
                    for e in ("sp", "pool")}
            semid = {}
            for s in list(esem.values()) + [x for v in dsem.values() for x in v]:
                semid[id(s)] = len(semid)
            ccsem = es.enter_context(nc.semaphore("s_cc"))
            semid[id(ccsem)] = len(semid)
            cccnt = 0
            cnt = {e: 0 for e in esem}
            dcnt = {e: 0 for e in dsem}
            dlist = {e: [] for e in dsem}
            per = {e: [] for e in engs}
            for op in self.ops:
                per[op.eng].append(op)
                if op.cc:
                    cccnt += 1
                    op.sem = ccsem
                    op.target = cccnt
                elif op.dma:
                    j = dcnt[op.eng]
                    dcnt[op.eng] += 1
                    op.sem = dsem[op.eng][j % self.NDMA]
                    op.target = 16 * (j // self.NDMA + 1)
                    op.prev = dlist[op.eng][j - self.NDMA] if j >= self.NDMA else None
                    dlist[op.eng].append(op)
                elif op.signal:
                    cnt[op.eng] += 1
                    op.sem = esem[op.eng]
                    op.target = cnt[op.eng]
            self.counts = dict(cnt)
            self.dcounts = dict(dcnt)

            def emit(name, e):
                waited = {}

                def need(d):
                    key = semid[id(d.sem)]
                    if waited.get(key, 0) < d.target:
                        e.wait_ge(d.sem, d.target)
                        waited[key] = d.target

                for op in per[name]:
                    for d in op.deps:
                        if not self._skip(d, op):
                            need(d)
                    if op.prev is not None:
                        need(op.prev)
                    ins = op.fn(e)
                    if op.signal:
                        ins.then_inc(op.sem, 16 if op.dma else 1)
                if name in dlist:
                    for d in dlist[name][-self.NDMA:]:
                        need(d)

            with nc.Block() as block:
                @block.tensor
                def _(e):
                    emit("pe", e)

                @block.scalar
                def _(e):
                    emit("act", e)

                @block.vector
                def _(e):
                    emit("dve", e)

                @block.gpsimd
                def _(e):
                    emit("pool", e)

                @block.sync
                def _(e):
                    emit("sp", e)


OFF_Q, OFF_K, OFF_V, OFF_O, OFF_IF = 0, 1024, 2048, 3072, 4096
OFF_SC = 4104
OFF_SBQ, OFF_SBK, OFF_SBV = 7176, 8200, 9224
OFF_G = 10248
N_IN = 13320


def build_ts(do_c, do_a, do_final):
    nc = bass.Bass("TRN2", target_bir_lowering=False)
    S = Sched(nc)

    def din(name, shape, dt=F32):
        return nc.dram_tensor(name, list(shape), dt, kind="ExternalInput").ap()

    def dout(name, shape, dt=F32):
        return nc.dram_tensor(name, list(shape), dt, kind="ExternalOutput").ap()

    xT = din("xT", [D, TOK])
    if do_c:
        yT = din("yT", [3 * D, TOK], BF16)
        gmix_c = din("gmix_c", [128, 8])
        w_g = din("w_g", [D, 3 * D])
        w_br = [din("w_ml", [D, D]), din("w_sc", [D, D]), din("w_sb", [D, D])]
        w_out = din("w_out", [D, D])
        gmlp = din("gmlp", [128, 8])
        w_up = din("w_up", [D, 4 * D])
        w_down = din("w_down", [4 * D, D])
    if do_final:
        gfin = din("gfin", [128, 8])
        outT = dout("outT", [D, TOK])
    if do_a:
        gmix_a = din("gmix_a", [128, 8])
        w_mix = din("w_mix", [D, OFF_G])
        if do_c:
            xoT = dout("xoT", [D, TOK])
        e_qT = dout("e_qT", [4, 256, TOK], BF16)
        e_kT = dout("e_kT", [4, 256, TOK], BF16)
        e_k = dout("e_k", [4, TOK, 256], BF16)
        e_v = dout("e_v", [4, TOK, 256], BF16)
        e_o = dout("e_o", [4, TOK, 256], BF16)
        e_if = dout("e_if", [8, TOK], F32)
        e_sc = dout("e_sc", [4, 3, 256, TOK], BF16)
        e_sbqk = dout("e_sbqk", [4, 2, 256, TOK], BF16)
        e_sbv = dout("e_sbv", [4, TOK, 256], BF16)

    with contextlib.ExitStack() as es:
        def sb(name, shape, dt):
            return es.enter_context(nc.sbuf_tensor(name, list(shape), dt))

        xs = sb("xs", [128, 8, ST], F32)
        hT = sb("hT", [128, 8, ST], BF16)
        sq = [sb("sq%d" % i, [128, ST], F32) for i in range(2)]
        rstd = sb("rstd", [128, ST], F32)
        big = sb("big", [128, 24, ST], BF16)
        mT = sb("mT", [128, 8, ST], BF16)
        NW = 3
        wsl = [sb("wsl%d" % i, [128, 8192], BF16) for i in range(NW)]
        ost = [sb("ost%d" % i, [128, ST], BF16) for i in range(4)]
        ostf = sb("ostf", [128, ST], F32)
        sgt = [sb("sgt%d" % i, [128, ST], F32) for i in range(2)]
        mac = sb("mac", [128, ST], F32)
        gv = sb("gv", [128, 4, 8], F32)
        ones = sb("ones", [128, 128], F32)
        ps = [es.enter_context(nc.psum_tensor("ps%d" % i, [128, ST], F32)) for i in range(4)]

        S.add("pool", lambda e: e.memset(ones[:], 1.0), writes=["ones"])
        gi = 0
        gidx = {}
        for nm, ap in (("gmix_c", gmix_c if do_c else None), ("gmlp", gmlp if do_c else None),
                       ("gfin", gfin if do_final else None), ("gmix_a", gmix_a if do_a else None)):
            if ap is not None:
                S.add("sp", lambda e, ap=ap, gi=gi: e.dma_start(out=gv[:, gi, :], in_=ap), writes=[("gv", gi)], dma=True)
                gidx[nm] = gi
                gi += 1

        st_ctr = {"w": 0, "ps": 0, "ost": 0, "sq": 0, "sg": 0}

        def next_w():
            i = st_ctr["w"] % NW
            st_ctr["w"] += 1
            return i

        def next_ps():
            i = st_ctr["ps"] % 4
            st_ctr["ps"] += 1
            return i

        def load_w(slot, pieces):
            for (src, off, a, b) in pieces:
                dst = wsl[slot][:, off:off + a * b].rearrange("p (a b) -> p a b", a=a)
                S.add("pool", lambda e, dst=dst, src=src: e.dma_start(out=dst, in_=src),
                      writes=[("w", slot)], dma=True)

        def wview(w, c0, ncols):
            return w.rearrange("(kc p) n -> p kc n", p=128)[:, :, c0:c0 + ncols]

        def rmsnorm_to_hT(gkey):
            p = next_ps()
            for fc in range(8):
                q = st_ctr["sq"] % 2
                st_ctr["sq"] += 1
                S.add("act", lambda e, q=q, fc=fc: e.activation(out=sq[q][:], in_=xs[:, fc, :], func=AF.Square),
                      reads=[("xs", fc)], writes=[("sq", q)])
                for h in range(2):
                    S.add("pe", lambda e, q=q, fc=fc, h=h, p=p: e.matmul(
                        ps[p][:, h * 512:(h + 1) * 512], lhsT=ones[:], rhs=sq[q][:, h * 512:(h + 1) * 512],
                        start=(fc == 0), stop=(fc == 7)),
                        reads=[("sq", q), "ones"], writes=[("ps", p)])
            S.add("act", lambda e, p=p: e.activation(out=rstd[:], in_=ps[p][:], func=AF.Sqrt, bias=EPS, scale=1.0 / D),
                  reads=[("ps", p)], writes=["rstd"])
            S.add("dve", lambda e: e.reciprocal(out=rstd[:], in_=rstd[:]), reads=["rstd"], writes=["rstd"])
            g = gidx[gkey]
            for fc in range(8):
                S.add("dve", lambda e, fc=fc, g=g: e.scalar_tensor_tensor(
                    out=hT[:, fc, :], in0=xs[:, fc, :], scalar=gv[:, g, fc:fc + 1], in1=rstd[:],
                    op0=ALU.mult, op1=ALU.mult),
                    reads=[("xs", fc), "rstd", ("gv", g)], writes=[("hT", fc)])

        def proj_fm(p, wslot, wcol, ncols_m, act_tile, act_key, nk, kbase=0, wstride=None):
            for h in range(2):
                for kc in range(nk):
                    S.add("pe", lambda e, h=h, kc=kc: e.matmul(
                        ps[p][0:ncols_m, h * 512:(h + 1) * 512],
                        lhsT=wsl[wslot][:, kc * wstride + wcol: kc * wstride + wcol + ncols_m],
                        rhs=act_tile[:, kbase + kc, h * 512:(h + 1) * 512],
                        start=(kc == 0), stop=(kc == nk - 1)),
                        reads=[("w", wslot), (act_key, kbase + kc)], writes=[("ps", p)])

        for st in range(TOK // ST):
            t0 = st * ST
            for fc in range(8):
                S.add("sp", lambda e, fc=fc, t0=t0: e.dma_start(out=xs[:, fc, :], in_=xT[fc * 128:(fc + 1) * 128, t0:t0 + ST]),
                      writes=[("xs", fc)], dma=True)
            if do_c:
                for kc in range(24):
                    S.add("sp", lambda e, kc=kc, t0=t0: e.dma_start(out=big[:, kc, :], in_=yT[kc * 128:(kc + 1) * 128, t0:t0 + ST]),
                          writes=[("big", kc)], dma=True)
                rmsnorm_to_hT("gmix_c")
                for n in range(8):
                    slot = next_w()
                    pieces = []
                    for b in range(3):
                        pieces.append((wview(w_br[b], n * 128, 128), b * 1024, 8, 128))
                        pieces.append((wview(w_g, b * D + n * 128, 128), (3 + b) * 1024, 8, 128))
                    load_w(slot, pieces)
                    for b in range(3):
                        pg = next_ps()
                        proj_fm(pg, slot, (3 + b) * 1024, 128, hT, "hT", 8, wstride=128)
                        q = st_ctr["sg"] % 2
                        st_ctr["sg"] += 1
                        S.add("act", lambda e, q=q, pg=pg: e.activation(out=sgt[q][:], in_=ps[pg][:], func=AF.Sigmoid),
                              reads=[("ps", pg)], writes=[("sg", q)])
                        pb = next_ps()
                        proj_fm(pb, slot, b * 1024, 128, big, "big", 8, kbase=b * 8, wstride=128)
                        if b == 0:
                            S.add("dve", lambda e, q=q, pb=pb: e.tensor_tensor(out=mac[:], in0=ps[pb][:], in1=sgt[q][:], op=ALU.mult),
                                  reads=[("ps", pb), ("sg", q)], writes=["mac"])
                        else:
                            S.add("dve", lambda e, q=q, pb=pb: e.tensor_tensor(out=sgt[q][:], in0=ps[pb][:], in1=sgt[q][:], op=ALU.mult),
                                  reads=[("ps", pb), ("sg", q)], writes=[("sg", q)])
                            if b == 1:
                                S.add("pool", lambda e, q=q: e.tensor_tensor(out=mac[:], in0=mac[:], in1=sgt[q][:], op=ALU.add),
                                      reads=["mac", ("sg", q)], writes=["mac"])
                            else:
                                S.add("pool", lambda e, q=q, n=n: e.tensor_tensor(out=mT[:, n, :], in0=mac[:], in1=sgt[q][:], op=ALU.add),
                                      reads=["mac", ("sg", q)], writes=[("mT", n), "mac"])
                slot = next_w()
                load_w(slot, [(wview(w_out, 0, 1024), 0, 8, 1024)])
                for n in range(8):
                    p = next_ps()
                    proj_fm(p, slot, n * 128, 128, mT, "mT", 8, wstride=1024)
                    S.add("dve", lambda e, n=n, p=p: e.tensor_tensor(out=xs[:, n, :], in0=xs[:, n, :], in1=ps[p][:], op=ALU.add),
                          reads=[("xs", n), ("ps", p)], writes=[("xs", n)])
                rmsnorm_to_hT("gmlp")
                for hf in range(2):
                    for fg in range(4):
                        slot = next_w()
                        c0 = hf * 2048 + fg * 512
                        load_w(slot, [(wview(w_up, c0, 512), 0, 8, 512)])
                        for j in range(4):
                            f = fg * 4 + j
                            p = next_ps()
                            proj_fm(p, slot, j * 128, 128, hT, "hT", 8, wstride=512)
                            q = st_ctr["sg"] % 2
                            st_ctr["sg"] += 1
                            S.add("act", lambda e, q=q, p=p: e.activation(out=sgt[q][:], in_=ps[p][:], func=AF.Relu),
                                  reads=[("ps", p)], writes=[("sg", q)])
                            eng = "dve" if f % 2 == 0 else "pool"
                            S.add(eng, lambda e, q=q, f=f: e.tensor_tensor(out=big[:, f, :], in0=sgt[q][:], in1=sgt[q][:], op=ALU.mult),
                                  reads=[("sg", q)], writes=[("big", f)])
                    for ng in range(4):
                        slot = next_w()
                        src = w_down[hf * 2048:(hf + 1) * 2048, :].rearrange("(kc p) n -> p kc n", p=128)[:, :, ng * 256:(ng + 1) * 256]
                        load_w(slot, [(src, 0, 16, 256)])
                        for j in range(2):
                            n = ng * 2 + j
                            p = next_ps()
                            proj_fm(p, slot, j * 128, 128, big, "big", 16, wstride=256)
                            S.add("dve", lambda e, n=n, p=p: e.tensor_tensor(out=xs[:, n, :], in0=xs[:, n, :], in1=ps[p][:], op=ALU.add),
                                  reads=[("xs", n), ("ps", p)], writes=[("xs", n)])
                if do_a:
                    for fc in range(8):
                        S.add("sp", lambda e, fc=fc, t0=t0: e.dma_start(out=xoT[fc * 128:(fc + 1) * 128, t0:t0 + ST], in_=xs[:, fc, :]),
                              reads=[("xs", fc)], dma=True)
            if do_final:
                p = next_ps()
                for fc in range(8):
                    q = st_ctr["sq"] % 2
                    st_ctr["sq"] += 1
                    S.add("act", lambda e, q=q, fc=fc: e.activation(out=sq[q][:], in_=xs[:, fc, :], func=AF.Square),
                          reads=[("xs", fc)], writes=[("sq", q)])
                    for h in range(2):
                        S.add("pe", lambda e, q=q, fc=fc, h=h, p=p: e.matmul(
                            ps[p][:, h * 512:(h + 1) * 512], lhsT=ones[:], rhs=sq[q][:, h * 512:(h + 1) * 512],
                            start=(fc == 0), stop=(fc == 7)),
                            reads=[("sq", q), "ones"], writes=[("ps", p)])
                S.add("act", lambda e, p=p: e.activation(out=rstd[:], in_=ps[p][:], func=AF.Sqrt, bias=EPS, scale=1.0 / D),
                      reads=[("ps", p)], writes=["rstd"])
                S.add("dve", lambda e: e.reciprocal(out=rstd[:], in_=rstd[:]), reads=["rstd"], writes=["rstd"])
                g = gidx["gfin"]
                for fc in range(8):
                    q = st_ctr["sg"] % 2
                    st_ctr["sg"] += 1
                    S.add("dve", lambda e, fc=fc, g=g, q=q: e.scalar_tensor_tensor(
                        out=sgt[q][:], in0=xs[:, fc, :], scalar=gv[:, g, fc:fc + 1], in1=rstd[:],
                        op0=ALU.mult, op1=ALU.mult),
                        reads=[("xs", fc), "rstd", ("gv", g)], writes=[("sg", q)])
                    S.add("sp", lambda e, fc=fc, q=q, t0=t0: e.dma_start(out=outT[fc * 128:(fc + 1) * 128, t0:t0 + ST], in_=sgt[q][:]),
                          reads=[("sg", q)], dma=True)
            if do_a:
                rmsnorm_to_hT("gmix_a")

                def evac(p, m, dst_fn, f32=False):
                    if f32:
                        S.add("dve", lambda e: e.tensor_copy(out=ostf[0:m, :], in_=ps[p][0:m, :]),
                              reads=[("ps", p)], writes=["ostf"])
                        for (da, sa) in dst_fn(ostf):
                            S.add("sp", lambda e, da=da, sa=sa: e.dma_start(out=da, in_=sa), reads=["ostf"], dma=True)
                        return
                    o = st_ctr["ost"] % 4
                    st_ctr["ost"] += 1
                    if o % 2 == 0:
                        S.add("dve", lambda e: e.tensor_copy(out=ost[o][0:m, :], in_=ps[p][0:m, :]),
                              reads=[("ps", p)], writes=[("ost", o)])
                    else:
                        S.add("act", lambda e: e.activation(out=ost[o][0:m, :], in_=ps[p][0:m, :], func=AF.Copy),
                              reads=[("ps", p)], writes=[("ost", o)])
                    for (da, sa) in dst_fn(ost[o]):
                        S.add("sp", lambda e, da=da, sa=sa: e.dma_start(out=da, in_=sa), reads=[("ost", o)], dma=True)

                def fm_group(c0, ncols, dst_of_block):
                    slot = next_w()
                    load_w(slot, [(wview(w_mix, c0, ncols), 0, 8, ncols)])
                    nb = (ncols + 127) // 128
                    for j in range(nb):
                        m = min(128, ncols - j * 128)
                        p = next_ps()
                        proj_fm(p, slot, j * 128, m, hT, "hT", 8, wstride=ncols)
                        dst_of_block(p, m, c0 + j * 128)

                def dst_q(p, m, col):
                    g, r = divmod(col - OFF_Q, 256)
                    evac(p, m, lambda stg: [(e_qT[g, r:r + 128, t0:t0 + ST], stg[:, :])])

                def dst_kT(p, m, col):
                    g, r = divmod(col - OFF_K, 256)
                    evac(p, m, lambda stg: [(e_kT[g, r:r + 128, t0:t0 + ST], stg[:, :])])

                def dst_if(p, m, col):
                    evac(p, m, lambda stg: [(e_if[:, t0:t0 + ST], stg[0:8, :])], f32=True)

                def dst_sc(p, m, col):
                    w, r = divmod(col - OFF_SC, 1024)
                    g, r2 = divmod(r, 256)
                    evac(p, m, lambda stg: [(e_sc[g, w, r2:r2 + 128, t0:t0 + ST], stg[:, :])])

                def dst_sbqk(p, m, col):
                    w, r = divmod(col - OFF_SBQ, 1024)
                    g, r2 = divmod(r, 256)
                    evac(p, m, lambda stg: [(e_sbqk[g, w, r2:r2 + 128, t0:t0 + ST], stg[:, :])])

                for c0 in (0, 512):
                    fm_group(OFF_Q + c0, 512, dst_q)
                for c0 in (0, 512):
                    fm_group(OFF_K + c0, 512, dst_kT)
                fm_group(OFF_IF, 8, dst_if)
                for c0 in range(0, 3072, 512):
                    fm_group(OFF_SC + c0, 512, dst_sc)
                for c0 in range(0, 2048, 512):
                    fm_group(OFF_SBQ + c0, 512, dst_sbqk)

                def tm_group(base, coff, dst):
                    c0 = base + coff
                    slot = next_w()
                    load_w(slot, [(wview(w_mix, c0, 512), 0, 8, 512)])
                    for tp in range(ST // 256):
                        p = next_ps()
                        for h in range(2):
                            tt = tp * 2 + h
                            for kc in range(8):
                                S.add("pe", lambda e, h=h, kc=kc, tt=tt, p=p: e.matmul(
                                    ps[p][:, h * 512:(h + 1) * 512],
                                    lhsT=hT[:, kc, tt * 128:(tt + 1) * 128],
                                    rhs=wsl[slot][:, kc * 512:(kc + 1) * 512],
                                    start=(kc == 0), stop=(kc == 7)),
                                    reads=[("w", slot), ("hT", kc)], writes=[("ps", p)])
                        g0 = coff // 256

                        def dfn(stg, tp=tp, g0=g0):
                            res = []
                            for h in range(2):
                                tok = t0 + (tp * 2 + h) * 128
                                for gg in range(2):
                                    res.append((dst[g0 + gg, tok:tok + 128, :],
                                                stg[:, h * 512 + gg * 256: h * 512 + (gg + 1) * 256]))
                            return res
                        evac(p, 128, dfn)

                for c0 in (0, 512):
                    tm_group(OFF_K, c0, e_k)
                for c0 in (0, 512):
                    tm_group(OFF_V, c0, e_v)
                for c0 in (0, 512):
                    tm_group(OFF_O, c0, e_o)
                for c0 in (0, 512):
                    tm_group(OFF_SBV, c0, e_sbv)

        S.finalize()
    return nc


def _g8(v):
    return np.ascontiguousarray(np.asarray(v, np.float32).reshape(8, 128).T)


def _mix_consts(nc, S, sb0):
    K = {}
    cf = K["cf"] = sb0("cf", [128, 128], F32)
    identf = K["identf"] = sb0("identf", [128, 128], F32)
    identb = K["identb"] = sb0("identb", [128, 128], BF16)
    negtri = K["negtri"] = sb0("negtri", [128, 128], BF16)
    mstrict = K["mstrict"] = sb0("mstrict", [128, 128], BF16)
    mincl8 = K["mincl8"] = sb0("mincl8", [64, 8, 64], BF16)
    triu64 = K["triu64"] = sb0("triu64", [64, 64], F32)
    striu = K["striu"] = sb0("striu", [128, 128], F32)
    onesf = K["onesf"] = sb0("onesf", [128, 128], F32)
    onesb = K["onesb"] = sb0("onesb", [128, 128], BF16)
    negonesb = K["negonesb"] = sb0("negonesb", [128, 1], BF16)
    K["bart"] = sb0("bart", [128, 1], F32)

    def P(fn, reads=(), writes=()):
        return S.add("pool", fn, reads=reads, writes=writes)

    def V(fn, reads=(), writes=()):
        return S.add("dve", fn, reads=reads, writes=writes)

    def asel(out, in_, cmp, fill, cm, pat):
        return lambda e: e.affine_select(out=out, in_=in_, compare_op=cmp, fill=fill, base=0,
                                         pattern=pat, channel_multiplier=cm)
    P(lambda e: e.memset(onesf[:], 1.0), writes=["onesf"])
    P(lambda e: e.memset(onesb[:], 1.0), writes=["onesb"])
    P(lambda e: e.memset(negonesb[:], -1.0), writes=["negonesb"])
    P(asel(identf[:], onesf[:], ALU.is_equal, 0.0, 1, [[-1, 128]]), reads=["onesf"], writes=["identf"])
    V(lambda e: e.tensor_copy(out=identb[:], in_=identf[:]), reads=["identf"], writes=["identb"])
    P(lambda e: e.memset(cf[:], -1.0), writes=["cf"])
    P(asel(cf[:], cf[:], ALU.is_ge, 0.0, 1, [[-1, 128]]), reads=["cf"], writes=["cf"])
    V(lambda e: e.tensor_copy(out=negtri[:], in_=cf[:]), reads=["cf"], writes=["negtri"])
    P(lambda e: e.memset(cf[:], 0.0), reads=["cf"], writes=["cf"])
    P(asel(cf[:], cf[:], ALU.is_gt, NEG, -1, [[1, 128]]), reads=["cf"], writes=["cf"])
    V(lambda e: e.tensor_copy(out=mstrict[:], in_=cf[:]), reads=["cf"], writes=["mstrict"])
    P(lambda e: e.memset(cf[:], 0.0), reads=["cf"], writes=["cf"])
    P(asel(cf[:], cf[:], ALU.is_ge, NEG, -1, [[1, 128]]), reads=["cf"], writes=["cf"])
    for j in range(8):
        V(lambda e, j=j: e.tensor_copy(out=mincl8[:, j, :], in_=cf[0:64, 0:64]), reads=["cf"], writes=["mincl8"])
    P(asel(striu[:], onesf[:], ALU.is_ge, 0.0, -1, [[1, 128]]), reads=["onesf"], writes=["striu"])
    V(lambda e: e.tensor_copy(out=triu64[:], in_=striu[0:64, 0:64]), reads=["striu"], writes=["triu64"])
    P(asel(striu[:], onesf[:], ALU.is_gt, 0.0, -1, [[1, 128]]), reads=["onesf", "triu64"], writes=["striu"])

    return K


def build_mix(parts=("conv", "ml", "sb"), ctx=None, tag=""):
    if ctx is None:
        nc = bass.Bass("TRN2", target_bir_lowering=False)
        S = Sched(nc)

        def din(name, shape, dt=F32):
            return nc.dram_tensor(name, list(shape), dt, kind="ExternalInput").ap()

        m_qT = din("m_qT", [256, SEQ], BF16)
        m_kT = din("m_kT", [256, SEQ], BF16)
        m_k = din("m_k", [SEQ, 256], BF16)
        m_v = din("m_v", [SEQ, 256], BF16)
        m_o = din("m_o", [SEQ, 256], BF16)
        m_if = din("m_if", [2, SEQ], F32)
        bif = din("bif", [128, 2], F32)
        mlg = din("mlg", [64, 256], F32)
        scin = din("scin", [3, 256, SEQ], BF16)
        convw = din("convw", [128, 2, 3], F32)
        sbqk = din("sbqk", [2, 256, SEQ], BF16)
        sbv = din("sbv", [SEQ, 256], BF16)
        yT = nc.dram_tensor("yT", [768, SEQ], BF16, kind="ExternalOutput").ap()
        scr = nc.dram_tensor("scr", [2, SEQ], F32).ap()
    else:
        nc, S = ctx["nc"], ctx["S"]
        m_qT, m_kT, m_k, m_v, m_o, m_if = (ctx[k] for k in ("m_qT", "m_kT", "m_k", "m_v", "m_o", "m_if"))
        scin, sbqk, sbv, yT, scr = (ctx[k] for k in ("scin", "sbqk", "sbv", "yT", "scr"))
        bif, mlg, convw = ctx["bif" + tag], ctx["mlg" + tag], ctx["convw" + tag]

    def ysplit(r0, nrows, c0, c1):
        if ctx is None:
            return [(yT[r0:r0 + nrows, c0:c1], 0, nrows)]
        res = []
        for a in range(0, nrows, 64):
            j, i = divmod(r0 + a, 64)
            res.append((yT[j][i:i + 64, c0:c1], a, 64))
        return res

    def ykeys(r0, a):
        return [("yl", (r0 + a) // 64)]

    def hook(which):
        if ctx is not None and "hook" in ctx:
            ctx["hook"](which)

    with contextlib.ExitStack() as es0:
        def sb0(name, shape, dt):
            return es0.enter_context(nc.sbuf_tensor(tag + name, list(shape), dt))

        if ctx is None:
            pall = es0.enter_context(nc.psum_tensor("pall", [128, 4096], F32))
            pb = [pall[:, i * 512:(i + 1) * 512] for i in range(8)]
            pb7b = pall.bitcast(BF16)[:, 7 * 1024:8 * 1024]
            K = _mix_consts(nc, S, sb0)
        else:
            pb, pb7b, K = ctx["pb"], ctx["pb7b"], ctx["consts"]
        identf, identb, negtri, mstrict, mincl8 = K["identf"], K["identb"], K["negtri"], K["mstrict"], K["mincl8"]
        triu64, striu, onesf, onesb, negonesb, bart = K["triu64"], K["striu"], K["onesf"], K["onesb"], K["negonesb"], K["bart"]

        def P(fn, reads=(), writes=()):
            return S.add("pool", fn, reads=reads, writes=writes)

        def V(fn, reads=(), writes=()):
            return S.add("dve", fn, reads=reads, writes=writes)

        def A(fn, reads=(), writes=()):
            return S.add("act", fn, reads=reads, writes=writes)

        def T(fn, reads=(), writes=()):
            return S.add("pe", fn, reads=reads, writes=writes)

        def DMA(fn, reads=(), writes=()):
            return S.add("sp", fn, reads=reads, writes=writes, dma=True)

        def barrier():
            S.barrier(lambda e: e.memset(bart[:], 0.0))

        if "conv" in parts:
            with contextlib.ExitStack() as es:
                def sb(name, shape, dt):
                    return es.enter_context(nc.sbuf_tensor(tag + name, list(shape), dt))
                cw = sb("cw", [128, 2, 3], F32)
                tb = sb("c_b", [128, SEQ], BF16)
                tc_ = sb("c_c", [128, SEQ], BF16)
                tu = sb("c_u", [128, SEQ], BF16)
                z = sb("c_z", [128, SEQ + 2], F32)
                yv = sb("c_y", [128, SEQ], F32)
                ob = sb("c_o", [128, SEQ], BF16)
                DMA(lambda e: e.dma_start(out=cw[:], in_=convw), writes=["cw"])
                V(lambda e: e.memset(z[:, 0:2], 0.0), writes=["zpad"])
                for cc in range(2):
                    r0 = cc * 128
                    DMA(lambda e, r0=r0: e.dma_start(out=tb[:], in_=scin[0, r0:r0 + 128, :]), writes=["c_b"])
                    DMA(lambda e, r0=r0: e.dma_start(out=tc_[:], in_=scin[1, r0:r0 + 128, :]), writes=["c_c"])
                    DMA(lambda e, r0=r0: e.dma_start(out=tu[:], in_=scin[2, r0:r0 + 128, :]), writes=["c_u"])
                    V(lambda e: e.tensor_tensor(out=z[:, 2:SEQ + 2], in0=tc_[:], in1=tu[:], op=ALU.mult),
                      reads=["c_c", "c_u"], writes=["c_z"])
                    V(lambda e, cc=cc: e.tensor_scalar(out=yv[:], in0=z[:, 2:SEQ + 2], scalar1=cw[:, cc, 2:3], scalar2=None, op0=ALU.mult),
                      reads=["c_z", "cw"], writes=["c_y"])
                    V(lambda e, cc=cc: e.scalar_tensor_tensor(out=yv[:], in0=z[:, 1:SEQ + 1], scalar=cw[:, cc, 1:2], in1=yv[:],
                                                              op0=ALU.mult, op1=ALU.add),
                      reads=["c_z", "zpad", "cw", "c_y"], writes=["c_y"])
                    V(lambda e, cc=cc: e.scalar_tensor_tensor(out=yv[:], in0=z[:, 0:SEQ], scalar=cw[:, cc, 0:1], in1=yv[:],
                                                              op0=ALU.mult, op1=ALU.add),
                      reads=["c_z", "zpad", "cw", "c_y"], writes=["c_y"])
                    V(lambda e: e.tensor_tensor(out=ob[:], in0=yv[:], in1=tb[:], op=ALU.mult),
                      reads=["c_y", "c_b"], writes=["c_o"])
                    for (dst, a, n_) in ysplit(256 + r0, 128, 0, SEQ):
                        DMA(lambda e, dst=dst, a=a, n_=n_: e.dma_start(out=dst, in_=ob[a:a + n_, :]), reads=["c_o"], writes=ykeys(256 + r0, a))
                barrier()
                hook("conv")

        if "ml" in parts:
            with contextlib.ExitStack() as es:
                def sb(name, shape, dt):
                    return es.enter_context(nc.sbuf_tensor(tag + name, list(shape), dt))
                qT = sb("qT", [128, 2, SEQ], BF16)
                kT = sb("kT", [128, 2, SEQ], BF16)
                ifr = sb("ifr", [128, 2, 64], F32)
                bif_s = sb("bif_s", [128, 2], F32)
                mlg_s = sb("mlg_s", [64, 256], F32)
                G = {n: sb("g_" + n, [128, 64], F32) for n in
                     ("ipre", "l1", "bneg", "u", "pmA", "pmB", "Pm", "ai", "em", "ws", "negP", "tmp")}
                col = {n: sb("c_" + n, [128, 1], F32) for n in ("tot", "off", "cmax", "mprev", "M", "negM", "dec")}
                l1T = sb("l1T", [64, 128], F32)
                uT = sb("uT", [64, 128], F32)
                emT = sb("emT", [64, 128], F32)
                wsT = sb("wsT", [64, 128], F32)
                rowA = sb("rowA", [1, 130], F32)
                rowB = sb("rowB", [1, 130], F32)
                decrow = sb("decrow", [1, 128], F32)
                decbc = sb("decbc", [128, 128], F32)
                DMA(lambda e: e.dma_start(out=ifr[:], in_=m_if.rearrange("r (c t) -> c r t", c=128)), writes=["ifr"])
                DMA(lambda e: e.dma_start(out=bif_s[:], in_=bif), writes=["bif"])
                DMA(lambda e: e.dma_start(out=mlg_s[:], in_=mlg), writes=["mlg"])
                for kc in range(2):
                    DMA(lambda e, kc=kc: e.dma_start(out=qT[:, kc, :], in_=m_qT[kc * 128:(kc + 1) * 128, :]), writes=[("qT", kc)])
                    DMA(lambda e, kc=kc: e.dma_start(out=kT[:, kc, :], in_=m_kT[kc * 128:(kc + 1) * 128, :]), writes=[("kT", kc)])
                V(lambda e: e.tensor_scalar(out=G["ipre"][:], in0=ifr[:, 0, :], scalar1=bif_s[:, 0:1], scalar2=None, op0=ALU.add),
                  reads=["ifr", "bif"], writes=["ipre"])
                V(lambda e: e.tensor_scalar(out=G["tmp"][:], in0=ifr[:, 1, :], scalar1=bif_s[:, 1:2], scalar2=None, op0=ALU.add),
                  reads=["ifr", "bif"], writes=["tmp"])
                A(lambda e: e.activation(out=G["l1"][:], in_=G["tmp"][:], func=AF.Exp, scale=-1.0), reads=["tmp"], writes=["l1"])
                A(lambda e: e.activation(out=G["l1"][:], in_=G["l1"][:], func=AF.Ln, bias=1.0), reads=["l1"], writes=["l1"])
                T(lambda e: e.transpose(out=pb[0][0:64, 0:128], in_=G["l1"][:], identity=identf[:]), reads=["l1", "identf"], writes=[("pb", 0)])
                V(lambda e: e.tensor_copy(out=l1T[:], in_=pb[0][0:64, 0:128]), reads=[("pb", 0)], writes=["l1T"])
                T(lambda e: e.matmul(pb[1][:, 0:64], lhsT=l1T[:], rhs=triu64[:], start=True, stop=True), reads=["l1T", "triu64"], writes=[("pb", 1)])
                V(lambda e: e.tensor_copy(out=col["tot"][:], in_=pb[1][:, 63:64]), reads=[("pb", 1)], writes=["tot"])
                T(lambda e: e.matmul(pb[2][:, 0:1], lhsT=striu[:], rhs=col["tot"][:], start=True, stop=True), reads=["tot", "striu"], writes=[("pb", 2)])
                V(lambda e: e.tensor_copy(out=col["off"][:], in_=pb[2][:, 0:1]), reads=[("pb", 2)], writes=["off"])
                V(lambda e: e.tensor_scalar(out=G["bneg"][:], in0=pb[1][:, 0:64], scalar1=col["off"][:, 0:1], scalar2=None, op0=ALU.add),
                  reads=[("pb", 1), "off"], writes=["bneg"])
                V(lambda e: e.tensor_tensor(out=G["u"][:], in0=G["ipre"][:], in1=G["bneg"][:], op=ALU.add), reads=["ipre", "bneg"], writes=["u"])
                src, dst = "u", "pmA"
                for sh in (1, 2, 4, 8, 16, 32):
                    V(lambda e, s=src, d=dst, sh=sh: e.tensor_copy(out=G[d][:, 0:sh], in_=G[s][:, 0:sh]), reads=[src], writes=[dst])
                    V(lambda e, s=src, d=dst, sh=sh: e.tensor_tensor(out=G[d][:, sh:64], in0=G[s][:, sh:64], in1=G[s][:, 0:64 - sh], op=ALU.max),
                      reads=[src], writes=[dst])
                    src, dst = dst, ("pmB" if dst == "pmA" else "pmA")
                pm = src
                V(lambda e: e.tensor_copy(out=col["cmax"][:], in_=G[pm][:, 63:64]), reads=[pm], writes=["cmax"])
                T(lambda e: e.transpose(out=pb[3][0:1, 0:128], in_=col["cmax"][:], identity=identf[:]), reads=["cmax", "identf"], writes=[("pb", 3)])
                V(lambda e: e.memset(rowA[:], 0.0), writes=["rowA"])
                V(lambda e: e.tensor_copy(out=rowA[0:1, 1:128], in_=pb[3][0:1, 0:127]), reads=[("pb", 3), "rowA"], writes=["rowA"])
                rs, rd = rowA, rowB
                rsn, rdn = "rowA", "rowB"
                for sh in (1, 2, 4, 8, 16, 32, 64):
                    V(lambda e, s=rs, d=rd, sh=sh: e.tensor_copy(out=d[0:1, 0:sh], in_=s[0:1, 0:sh]), reads=[rsn], writes=[rdn])
                    V(lambda e, s=rs, d=rd, sh=sh: e.tensor_tensor(out=d[0:1, sh:128], in0=s[0:1, sh:128], in1=s[0:1, 0:128 - sh], op=ALU.max),
                      reads=[rsn], writes=[rdn])
                    rs, rd, rsn, rdn = rd, rs, rdn, rsn
                T(lambda e, r=rs: e.matmul(pb[4][:, 0:1], lhsT=r[0:1, 0:128], rhs=onesf[0:1, 0:1], start=True, stop=True),
                  reads=[rsn, "onesf"], writes=[("pb", 4)])
                V(lambda e: e.tensor_copy(out=col["mprev"][:], in_=pb[4][:, 0:1]), reads=[("pb", 4)], writes=["mprev"])
                V(lambda e: e.tensor_scalar(out=G["Pm"][:], in0=G[pm][:], scalar1=col["mprev"][:, 0:1], scalar2=None, op0=ALU.max),
                  reads=[pm, "mprev"], writes=["Pm"])
                V(lambda e: e.tensor_copy(out=col["M"][:], in_=G["Pm"][:, 63:64]), reads=["Pm"], writes=["M"])
                V(lambda e: e.tensor_scalar(out=col["negM"][:], in0=col["M"][:], scalar1=-1.0, scalar2=None, op0=ALU.mult), reads=["M"], writes=["negM"])
                V(lambda e: e.tensor_scalar(out=G["negP"][:], in0=G["Pm"][:], scalar1=-1.0, scalar2=None, op0=ALU.mult), reads=["Pm"], writes=["negP"])
                A(lambda e: e.activation(out=G["ai"][:], in_=G["Pm"][:], func=AF.Exp, scale=-1.0, bias=col["mprev"][:, 0:1]),
                  reads=["Pm", "mprev"], writes=["ai"])
                V(lambda e: e.tensor_tensor(out=G["tmp"][:], in0=G["bneg"][:], in1=G["Pm"][:], op=ALU.subtract), reads=["bneg", "Pm", "tmp"], writes=["tmp"])
                A(lambda e: e.activation(out=G["em"][:], in_=G["tmp"][:], func=AF.Exp), reads=["tmp"], writes=["em"])
                A(lambda e: e.activation(out=G["ws"][:], in_=G["u"][:], func=AF.Exp, bias=col["negM"][:, 0:1]), reads=["u", "negM"], writes=["ws"])
                V(lambda e: e.tensor_scalar(out=G["ws"][:], in0=G["ws"][:], scalar1=1.0 / 16, scalar2=None, op0=ALU.mult), reads=["ws"], writes=["ws"])
                A(lambda e: e.activation(out=col["dec"][:], in_=col["mprev"][:], func=AF.Exp, bias=col["negM"][:, 0:1]),
                  reads=["mprev", "negM"], writes=["dec"])
                for (srcn, dstt, dstn, bank) in (("u", uT, "uT", 0), ("em", emT, "emT", 1), ("ws", wsT, "wsT", 2)):
                    T(lambda e, s=srcn, b=bank: e.transpose(out=pb[b][0:64, 0:128], in_=G[s][:], identity=identf[:]),
                      reads=[srcn, "identf"], writes=[("pb", bank)])
                    if dstn == "uT":
                        V(lambda e, d=dstt, b=bank: e.tensor_scalar(out=d[:], in0=pb[b][0:64, 0:128], scalar1=-float(np.log(16.0)), scalar2=None, op0=ALU.add),
                          reads=[("pb", bank)], writes=[dstn])
                    else:
                        V(lambda e, d=dstt, b=bank: e.tensor_copy(out=d[:], in_=pb[b][0:64, 0:128]), reads=[("pb", bank)], writes=[dstn])
                T(lambda e: e.transpose(out=pb[3][0:1, 0:128], in_=col["dec"][:], identity=identf[:]), reads=["dec", "identf"], writes=[("pb", 3)])
                V(lambda e: e.tensor_copy(out=decrow[:], in_=pb[3][0:1, 0:128]), reads=[("pb", 3)], writes=["decrow"])
                T(lambda e: e.matmul(pb[4][:, 0:128], lhsT=onesf[0:1, 0:128], rhs=decrow[0:1, 0:128], start=True, stop=True),
                  reads=["decrow", "onesf"], writes=[("pb", 4)])
                V(lambda e: e.tensor_copy(out=decbc[:], in_=pb[4][:, 0:128]), reads=[("pb", 4)], writes=["decbc"])
                DMA(lambda e: e.dma_start(out=scr[0, :].rearrange("(c t) -> c t", c=128), in_=G["negP"][:]), reads=["negP"], writes=["scr0"])
                DMA(lambda e: e.dma_start(out=scr[1, :].rearrange("(c t) -> c t", c=128), in_=G["ai"][:]), reads=["ai"], writes=["scr1"])

                NB = 2
                kg = [sb("kg%d" % i, [64, 8, 256], BF16) for i in range(NB)]
                vg = [sb("vg%d" % i, [64, 8, 257], BF16) for i in range(NB)]
                og = [sb("og%d" % i, [64, 8, 256], BF16) for i in range(NB)]
                prow = [sb("prow%d" % i, [1, 2, 512], F32) for i in range(NB)]
                sgo = sb("sgo", [64, 8, 256], F32)
                yg = sb("yg", [64, 8, 256], F32)
                yng = sb("yng", [64, 8, 256], BF16)
                nraw = sb("nraw", [64, 8, 257], F32)
                sqs = sb("sqs", [64, 256], F32)
                qsT = [sb("qsT%d" % i, [128, 2, 512], BF16) for i in range(NB)]
                ymT = [sb("ymT%d" % i, [128, 2, 512], BF16) for i in range(NB)]
                wtg = [sb("wtg%d" % i, [64, 512], F32) for i in range(NB)]
                swT = [sb("swT%d" % i, [64, 64], BF16) for i in range(16)]
                kw = [sb("kw%d" % i, [64, 256], BF16) for i in range(16)]
                Cf = sb("Cf", [128, 2, 257], F32)
                Cb = [sb("Cb%d" % i, [128, 2, 257], BF16) for i in range(2)]
                ss = sb("ss", [64, 8], F32)
                rsd = sb("rsd", [64, 8], F32)
                dm1 = sb("dm1", [64, 8], F32)
                dm2 = sb("dm2", [64, 8], F32)
                pb0b = ctx["pb0b"] if ctx is not None else pall.bitcast(BF16)[:, 0:1024]
                for i in range(NB):
                    V(lambda e, i=i: e.memset(vg[i][:, :, 256:257], 1.0), writes=[("vg1", i)])
                V(lambda e: e.memset(Cf[:], 0.0), writes=[("Cf", 0), ("Cf", 1)])
                V(lambda e: e.memset(Cb[0][:], 0.0), writes=[("Cb", 0, 0), ("Cb", 0, 1)])
                for gi in range(16):
                    b = gi % NB
                    tok0 = gi * 512
                    DMA(lambda e, b=b, tok0=tok0: e.dma_start(out=kg[b][:], in_=m_k[tok0:tok0 + 512, :].rearrange("(j s) d -> s j d", j=8)), writes=[("kg", b)])
                    DMA(lambda e, b=b, tok0=tok0: e.dma_start(out=vg[b][:, :, 0:256], in_=m_v[tok0:tok0 + 512, :].rearrange("(j s) d -> s j d", j=8)), writes=[("vg", b)])
                    DMA(lambda e, b=b, tok0=tok0: e.dma_start(out=og[b][:], in_=m_o[tok0:tok0 + 512, :].rearrange("(j s) d -> s j d", j=8)), writes=[("og", b)])
                    DMA(lambda e, b=b, tok0=tok0: e.dma_start(out=prow[b][0:1, :, :], in_=scr[:, tok0:tok0 + 512]),
                        reads=["scr0", "scr1"], writes=[("prow", b)])
                    T(lambda e, b=b: e.matmul(pb[0][0:64, :], lhsT=onesf[0:1, 0:64], rhs=prow[b][0:1, 0, :], start=True, stop=False),
                      reads=[("prow", b), "onesf"], writes=[("pb", 0)])
                    T(lambda e: e.matmul(pb[0][0:64, :], lhsT=identb[0:64, 0:64], rhs=mincl8[:].rearrange("s j t -> s (j t)"), start=False, stop=True),
                      reads=["identb", "mincl8"], writes=[("pb", 0)])
                    for j in range(8):
                        c = gi * 8 + j
                        A(lambda e, j=j, c=c, b=b: e.activation(out=wtg[b][:, j * 64:(j + 1) * 64], in_=pb[0][0:64, j * 64:(j + 1) * 64], func=AF.Exp, bias=uT[:, c:c + 1]),
                          reads=[("pb", 0), "uT"], writes=[("wtg", b, j)])
                    T(lambda e, b=b: e.matmul(pb[1][:, :], lhsT=onesf[0:1, 0:128], rhs=prow[b][0:1, 1, :], start=True, stop=True),
                      reads=[("prow", b), "onesf"], writes=[("pb", 1)])
                    for kc in range(2):
                        V(lambda e, b=b, kc=kc, tok0=tok0: e.tensor_tensor(out=qsT[b][:, kc, :], in0=qT[:, kc, tok0:tok0 + 512], in1=pb[1][:, :], op=ALU.mult),
                          reads=[("qT", kc), ("pb", 1)], writes=[("qsT", b, kc)])
                    A(lambda e, b=b: e.activation(out=sgo[:], in_=og[b][:], func=AF.Sigmoid), reads=[("og", b)], writes=["sgo"])
                    V(lambda e: e.memset(ss[:], 0.0), reads=["ss"], writes=["ss"])
                    for j in range(8):
                        c = gi * 8 + j
                        w16 = c % 16
                        cs = slice(c * 64, (c + 1) * 64)
                        sbk = 1 + (j % 2)
                        for kc in range(2):
                            T(lambda e, kc=kc, cs=cs, sbk=sbk: e.matmul(pb[sbk][0:64, 0:64], lhsT=kT[:, kc, cs], rhs=qT[:, kc, cs], start=(kc == 0), stop=(kc == 1)),
                              reads=[("kT", kc), ("qT", kc)], writes=[("pb", sbk)])
                        V(lambda e, w16=w16, sbk=sbk, b=b, j=j: e.tensor_tensor(out=swT[w16][:], in0=pb[sbk][0:64, 0:64], in1=wtg[b][:, j * 64:(j + 1) * 64], op=ALU.mult),
                          reads=[("pb", sbk), ("wtg", b, j)], writes=[("swT", w16)])
                        A(lambda e, w16=w16, b=b, j=j, c=c: e.activation(out=kw[w16][:], in_=kg[b][:, j, :], func=AF.Copy, scale=wsT[:, c:c + 1]),
                          reads=[("kg", b), "wsT"], writes=[("kw", w16)])
                    for j in range(8):
                        c = gi * 8 + j
                        w = c % 2
                        w16 = c % 16
                        js = slice(j * 64, (j + 1) * 64)
                        ub = 4 + 2 * w
                        nb = 3 if j % 2 == 0 else 0
                        for kc in range(2):
                            T(lambda e, w16=w16, b=b, j=j, kc=kc, ub=ub: e.matmul(pb[ub + kc][:, 0:257], lhsT=kw[w16][:, kc * 128:(kc + 1) * 128], rhs=vg[b][:, j, :],
                                                                               start=True, stop=True),
                              reads=[("kw", w16), ("vg", b), ("vg1", b)], writes=[("pb", ub + kc)])
                        T(lambda e, w16=w16, b=b, j=j, nb=nb: e.matmul(pb[nb][0:64, 0:257], lhsT=swT[w16][:], rhs=vg[b][:, j, :], start=True, stop=False),
                          reads=[("swT", w16), ("vg", b), ("vg1", b)], writes=[("pb", nb)])
                        for kc in range(2):
                            T(lambda e, w=w, b=b, kc=kc, js=js, nb=nb: e.matmul(pb[nb][0:64, 0:257], lhsT=qsT[b][:, kc, js], rhs=Cb[w][:, kc, :],
                                                                             start=False, stop=(kc == 1)),
                              reads=[("qsT", b, kc), ("Cb", w, kc)], writes=[("pb", nb)])
                        for kc in range(2):
                            V(lambda e, kc=kc, c=c, w=w, ub=ub: e.scalar_tensor_tensor(out=Cb[1 - w][:, kc, :], in0=Cf[:, kc, :], scalar=decbc[:, c:c + 1], in1=pb[ub + kc][:, 0:257],
                                                                                   op0=ALU.mult, op1=ALU.add),
                              reads=[("Cf", kc), "decbc", ("pb", ub + kc)], writes=[("Cb", 1 - w, kc)])
                        A(lambda e, j=j, nb=nb: e.activation(out=nraw[:, j, :], in_=pb[nb][0:64, 0:257], func=AF.Copy), reads=[("pb", nb)], writes=[("nraw", j)])
                        for kc in range(2):
                            V(lambda e, kc=kc, c=c, ub=ub: e.scalar_tensor_tensor(out=Cf[:, kc, :], in0=Cf[:, kc, :], scalar=decbc[:, c:c + 1], in1=pb[ub + kc][:, 0:257],
                                                                              op0=ALU.mult, op1=ALU.add),
                              reads=[("Cf", kc), "decbc", ("pb", ub + kc)], writes=[("Cf", kc)])
                    nr = [("nraw", j) for j in range(8)]
                    c0 = gi * 8
                    V(lambda e: e.tensor_scalar(out=dm1[:], in0=nraw[:, :, 256], scalar1=-1.0, scalar2=None, op0=ALU.mult), reads=nr, writes=["dm1"])
                    V(lambda e: e.tensor_tensor(out=dm1[:], in0=dm1[:], in1=nraw[:, :, 256], op=ALU.max), reads=nr + ["dm1"], writes=["dm1"])
                    V(lambda e, c0=c0: e.tensor_tensor(out=dm1[:], in0=dm1[:], in1=emT[:, c0:c0 + 8], op=ALU.max), reads=["dm1", "emT"], writes=["dm1"])
                    V(lambda e: e.reciprocal(out=dm2[:], in_=dm1[:]), reads=["dm1"], writes=["dm2"])
                    for j in range(8):
                        V(lambda e, j=j: e.scalar_tensor_tensor(out=yg[:, j, :], in0=nraw[:, j, 0:256], scalar=dm2[:, j:j + 1], in1=sgo[:, j, :],
                                                                op0=ALU.mult, op1=ALU.mult),
                          reads=[("nraw", j), "dm2", "sgo"], writes=[("yg", j)])
                        A(lambda e, j=j: e.activation(out=sqs[:], in_=yg[:, j, :], func=AF.Square, accum_out=ss[:, j:j + 1]),
                          reads=[("yg", j), "ss"], writes=["sqs", "ss"])
                    A(lambda e: e.activation(out=rsd[:], in_=ss[:], func=AF.Ln, scale=1.0 / 256, bias=EPS), reads=["ss"], writes=["rsd"])
                    A(lambda e: e.activation(out=rsd[:], in_=rsd[:], func=AF.Exp, scale=-0.5), reads=["rsd"], writes=["rsd"])
                    for j in range(8):
                        V(lambda e, j=j: e.scalar_tensor_tensor(out=yng[:, j, :], in0=yg[:, j, :], scalar=rsd[:, j:j + 1], in1=mlg_s[:],
                                                                op0=ALU.mult, op1=ALU.mult),
                          reads=[("yg", j), "rsd", "mlg"], writes=[("yng", j)])
                        for vc in range(2):
                            T(lambda e, j=j, vc=vc: e.transpose(out=pb0b[:, vc * 512 + j * 64: vc * 512 + (j + 1) * 64], in_=yng[:, j, vc * 128:(vc + 1) * 128],
                                                                identity=identb[0:64, 0:64]),
                              reads=[("yng", j), "identb"], writes=[("pb", 0)])
                    A(lambda e, b=b: e.activation(out=ymT[b][:].rearrange("p a t -> p (a t)"), in_=pb0b[:, :], func=AF.Copy), reads=[("pb", 0)], writes=[("ymT", b)])
                    for vc in range(2):
                        for (dst, a, n_) in ysplit(vc * 128, 128, tok0, tok0 + 512):
                            DMA(lambda e, b=b, vc=vc, dst=dst, a=a, n_=n_: e.dma_start(out=dst, in_=ymT[b][a:a + n_, vc, :]),
                                reads=[("ymT", b)], writes=ykeys(vc * 128, a))
                barrier()
                hook("ml")

        if "sb" in parts:
            with contextlib.ExitStack() as es:
                def sb(name, shape, dt):
                    return es.enter_context(nc.sbuf_tensor(tag + name, list(shape), dt))
                sq_ = sb("sq_", [128, 2, SEQ], BF16)
                sk_ = sb("sk_", [128, 2, SEQ], BF16)
                sv_ = sb("sv_", [128, 64, 256], BF16)
                for hp in range(2):
                    DMA(lambda e, hp=hp: e.dma_start(out=sq_[:, hp, :], in_=sbqk[0, hp * 128:(hp + 1) * 128, :]), writes=[("sq", hp)])
                    DMA(lambda e, hp=hp: e.dma_start(out=sk_[:, hp, :], in_=sbqk[1, hp * 128:(hp + 1) * 128, :]), writes=[("sk", hp)])
                    V(lambda e, hp=hp: e.tensor_scalar(out=sq_[:, hp, :], in0=sq_[:, hp, :], scalar1=0.125, scalar2=None, op0=ALU.mult),
                      reads=[("sq", hp)], writes=[("sq", hp)])
                for q4 in range(4):
                    DMA(lambda e, q4=q4: e.dma_start(out=sv_[:, q4 * 16:(q4 + 1) * 16, :],
                                                     in_=sbv[q4 * 2048:(q4 + 1) * 2048, :].rearrange("(kb s) d -> s kb d", s=128)),
                        writes=[("sv", q4)])
                if ctx is not None and "pre_sb" in ctx:
                    ctx["pre_sb"](sb, tag, [("sq", 0), ("sq", 1), ("sk", 0), ("sk", 1)] + [("sv", q4) for q4 in range(4)])
                e_sb = [sb("e_sb%d" % i, [128, 1024], F32) for i in range(2)]
                l_sb = [sb("l_sb%d" % i, [128, 1024], BF16) for i in range(2)]
                a_sb = [sb("a_sb%d" % i, [128, 1024], BF16) for i in range(2)]
                lacc = [sb("lacc%d" % i, [128, 512], F32) for i in range(2)]
                laccb = [[sb("laccb%d_%d" % (i, k), [128, 512], BF16) for k in range(2)] for i in range(2)]
                negones = sb("negones", [128, 128], BF16)
                osb = [[sb("osb%d_%d" % (i, k), [64, 512], BF16) for k in range(2)] for i in range(2)]
                V(lambda e: e.memset(negones[:], -1.0), writes=["negones"])
                pallv = ctx["pall"] if ctx is not None else pall
                steps = []
                for hp in range(2):
                    for qt in range(16):
                        for kb in range(4 * qt + 3, -1, -1):
                            for hh in range(2):
                                steps.append((2 * hp + hh, qt, kb))
                n = len(steps)
                ng = n // 2

                def info(i):
                    h, qt, kb = steps[i]
                    j = kb - 4 * qt
                    c0 = 128 * j if j > 0 else 0
                    first = (kb == 4 * qt + 3)
                    last = (kb == 0)
                    g, k = divmod(i, 2)
                    zb = (g % 3) * 2 + k
                    off = k * 512
                    seq = 4 * qt + 3 - kb
                    return h, qt, kb, j, c0, first, last, g, zb, off, seq

                def gcols(g):
                    return info(2 * g)[4], 1024

                def zgrp(g):
                    base = (g % 3) * 1024
                    return pallv[:, base:base + 1024]

                def s1(i):
                    h, qt, kb, j, c0, first, last, g, zb, off, seq = info(i)
                    hp, hh = divmod(h, 2)
                    pr = slice(hh * 64, hh * 64 + 64)
                    T(lambda e: e.matmul(pb[zb][:, c0:512], lhsT=sk_[pr, hp, kb * 128:(kb + 1) * 128], rhs=sq_[pr, hp, qt * 512 + c0:(qt + 1) * 512],
                                         start=True, stop=True),
                      reads=[("sk", hp), ("sq", hp)], writes=[("zg", g % 3)])
                    if j >= 0:
                        T(lambda e: e.matmul(pb[zb][:, c0:c0 + 128], lhsT=identb[:], rhs=mstrict[:], start=False, stop=True, skip_group_check=True),
                          reads=["identb", "mstrict"], writes=[("zg", g % 3)])

                def v2(ap, lo):
                    return ap.rearrange("p (k c) -> p k c", k=2)[:, :, lo:512]

                def ga_e(g):
                    lo, hi = gcols(g)
                    A(lambda e: e.activation(out=v2(e_sb[g % 2][:], lo), in_=v2(zgrp(g), lo), func=AF.Exp),
                      reads=[("zg", g % 3)], writes=[("e_sb", g % 2)])

                def ga_l(g):
                    lo, hi = gcols(g)
                    A(lambda e: e.activation(out=v2(l_sb[g % 2][:], lo), in_=v2(e_sb[g % 2][:], lo), func=AF.Ln, bias=1.0),
                      reads=[("e_sb", g % 2)], writes=[("l_sb", g % 2)])

                def ga_a(g):
                    lo, hi = gcols(g)
                    A(lambda e: e.activation(out=v2(a_sb[g % 2][:], lo), in_=v2(zgrp(g), lo), func=AF.Exp),
                      reads=[("zg", g % 3)], writes=[("a_sb", g % 2)])

                def s3(i):
                    h, qt, kb, j, c0, first, last, g, zb, off, seq = info(i)
                    st = h % 2
                    lt = l_sb[g % 2]
                    T(lambda e: e.matmul(pb[zb][:, c0:512], lhsT=negtri[:], rhs=lt[:, off + c0:off + 512], start=False, stop=first, skip_group_check=True),
                      reads=["negtri", ("l_sb", g % 2)], writes=[("zg", g % 3)])
                    pc0 = 128 * (j + 1) if j >= 0 else 0
                    if not first:
                        prevb = laccb[st][(seq - 1) % 2]
                        T(lambda e: e.matmul(pb[zb][:, pc0:512], lhsT=negones[:], rhs=prevb[:, pc0:512], start=False, stop=True, skip_group_check=True),
                          reads=["negones", ("laccb", st, (seq - 1) % 2)], writes=[("zg", g % 3)])
                    if not last:
                        if first:
                            V(lambda e: e.tensor_copy(out=lacc[st][:, c0:512], in_=lt[:, off + c0:off + 512]),
                              reads=[("l_sb", g % 2)], writes=[("lacc", st)])
                        else:
                            if pc0 > c0:
                                V(lambda e: e.tensor_copy(out=lacc[st][:, c0:pc0], in_=lt[:, off + c0:off + pc0]),
                                  reads=[("l_sb", g % 2)], writes=[("lacc", st)])
                            V(lambda e: e.tensor_tensor(out=lacc[st][:, pc0:512], in0=lacc[st][:, pc0:512], in1=lt[:, off + pc0:off + 512], op=ALU.add),
                              reads=[("l_sb", g % 2), ("lacc", st)], writes=[("lacc", st)])
                        curb = laccb[st][seq % 2]
                        V(lambda e: e.tensor_copy(out=curb[:, c0:512], in_=lacc[st][:, c0:512]),
                          reads=[("lacc", st)], writes=[("laccb", st, seq % 2)])

                def s5(i):
                    h, qt, kb, j, c0, first, last, g, zb, off, seq = info(i)
                    st = h % 2
                    T(lambda e: e.matmul(pb[6 + st][0:64, c0:512], lhsT=sv_[:, kb, h * 64:(h + 1) * 64], rhs=a_sb[g % 2][:, off + c0:off + 512],
                                         start=first, stop=last, skip_group_check=True),
                      reads=[("sv", kb // 16), ("a_sb", g % 2)], writes=[("pb", 6 + st)])
                    if last:
                        o = qt % 2
                        V(lambda e: e.tensor_copy(out=osb[st][o][:], in_=pb[6 + st][0:64, :]), reads=[("pb", 6 + st)], writes=[("osb", st, o)])
                        for (dst, a, n_) in ysplit(512 + h * 64, 64, qt * 512, (qt + 1) * 512):
                            DMA(lambda e, dst=dst, a=a, n_=n_: e.dma_start(out=dst, in_=osb[st][o][a:a + n_, :]),
                                reads=[("osb", st, o)], writes=ykeys(512 + h * 64, a))
                        if qt == 15:
                            hook(("sb", h))

                def steps_of(g):
                    return [2 * g, 2 * g + 1]

                for i in steps_of(0):
                    s1(i)
                for g in range(ng + 1):
                    if g + 1 < ng:
                        for i in steps_of(g + 1):
                            s1(i)
                    if g < ng:
                        ga_e(g)
                        ga_l(g)
                    if 0 <= g - 1 < ng:
                        ga_a(g - 1)
                    if g < ng:
                        for i in steps_of(g):
                            s3(i)
                    if 0 <= g - 1 < ng:
                        for i in steps_of(g - 1):
                            s5(i)
        if ctx is None:
            S.finalize()
    return nc


A_Q, A_K, A_V, A_O, A_SBV, A_IF, A_SC, A_SBQ, A_SBK, A_N = 0, 256, 512, 768, 1024, 1280, 1282, 2050, 2306, 2562
GROUPS4 = [[0, 1, 2, 3], [4, 5, 6, 7]]
DEPTH = 2


def build_fused(phases=("h0", "ag", "a2", "mix", "c")):
    nc = bass.Bass("TRN2", target_bir_lowering=False)
    S = Sched(nc)

    def din(name, shape, dt=F32):
        return nc.dram_tensor(name, list(shape), dt, kind="ExternalInput").ap()

    def dint(name, shape, dt=BF16):
        return nc.dram_tensor(name, list(shape), dt)

    xT = din("xT", [D, TOK])
    yidx = din("yidx", [128, 48], mybir.dt.int32)
    gfin = din("gfin", [128, 8])
    outT = nc.dram_tensor("outT", [D, TOK], F32, kind="ExternalOutput").ap()
    W = []
    for l in range(DEPTH):
        t = "%d" % l
        W.append(dict(
            gmix=din("gmix" + t, [128, 8]), wA=din("wA" + t, [D, A_N]), w_g=din("w_g" + t, [D, 3 * D]),
            w_br=[din("w_ml" + t, [D, D]), din("w_sc" + t, [D, D]), din("w_sb" + t, [D, D])],
            w_out=din("w_out" + t, [D, D]), gmlp=din("gmlp" + t, [128, 8]),
            w_up=din("w_up" + t, [D, 4 * D]), w_down=din("w_down" + t, [4 * D, D]),
            bif=din("bif" + t, [128, 2]), mlg=din("mlg" + t, [64, 256]), convw=din("convw" + t, [128, 2, 3])))
    hl_t = [dint("hl%d" % j, [256, TOK]) for j in range(4)]
    ha_t = [dint("ha%d" % j, [4 * 256, TOK]) for j in range(4)]
    yl_t = [dint("yl%d" % j, [64, SEQ]) for j in range(12)]
    ya_t = [dint("ya%d" % j, [4 * 64, SEQ]) for j in range(12)]
    yall_t = dint("yall", [4 * 768 * 8, ST])
    yall = yall_t.ap()
    yloc = [t_.ap() for t_ in yl_t]
    xcur = dint("xcur", [D, TOK], F32).ap()
    ctx = {"nc": nc, "S": S}
    ctx["m_qT"] = dint("m_qT", [256, SEQ]).ap()
    ctx["m_kT"] = dint("m_kT", [256, SEQ]).ap()
    ctx["m_k"] = dint("m_k", [SEQ, 256]).ap()
    ctx["m_v"] = dint("m_v", [SEQ, 256]).ap()
    ctx["m_o"] = dint("m_o", [SEQ, 256]).ap()
    ctx["m_if"] = dint("m_if", [2, SEQ], F32).ap()
    ctx["scin"] = dint("scin", [3, 256, SEQ]).ap()
    ctx["sbqk"] = dint("sbqk", [2, 256, SEQ]).ap()
    ctx["sbv"] = dint("sbv", [SEQ, 256]).ap()
    ctx["scr"] = dint("scr", [2, SEQ], F32).ap()
    ctx["yT"] = yloc
    for l in range(DEPTH):
        for k in ("bif", "mlg", "convw"):
            ctx[k + "L%d" % l] = W[l][k]

    with contextlib.ExitStack() as es0:
        def sb0(name, shape, dt):
            return es0.enter_context(nc.sbuf_tensor(name, list(shape), dt))

        pall = es0.enter_context(nc.psum_tensor("pall", [128, 4096], F32))
        pb = [pall[:, i * 512:(i + 1) * 512] for i in range(8)]
        ps = [pall[:, i * 1024:(i + 1) * 1024] for i in range(4)]
        ctx["pb"] = pb
        ctx["pb7b"] = pall.bitcast(BF16)[:, 7 * 1024:8 * 1024]
        ctx["pb0b"] = pall.bitcast(BF16)[:, 0:1024]
        ctx["pall"] = pall
        K = ctx["consts"] = _mix_consts(nc, S, sb0)
        onesf = K["onesf"]
        bart = K["bart"]
        gv = sb0("gv", [128, 2 * DEPTH + 1, 8], F32)
        yix = sb0("yix", [128, 48], mybir.dt.int32)
        GI = {}
        for i, (nm, ap) in enumerate([("gmix0", W[0]["gmix"]), ("gmlp0", W[0]["gmlp"]), ("gmix1", W[1]["gmix"]),
                                      ("gmlp1", W[1]["gmlp"]), ("gfin", gfin)]):
            S.add("sp", lambda e, ap=ap, i=i: e.dma_start(out=gv[:, i, :], in_=ap), writes=[("gv", i)], dma=True)
            GI[nm] = i
        S.add("sp", lambda e: e.dma_start(out=yix[:], in_=yidx), writes=["yix"], dma=True)

        def barrier():
            S.barrier(lambda e: e.memset(bart[:], 0.0))

        ctr = {"w": 0, "ps": 0, "ost": 0, "sq": 0, "sg": 0}

        def next_ps():
            i = ctr["ps"] % 4
            ctr["ps"] += 1
            return i

        def wview(w, c0, ncols):
            return w.rearrange("(kc p) n -> p kc n", p=128)[:, :, c0:c0 + ncols]

        def allgather(src_t, dst_t, rkeys, wkey, nobar=False):
            S.add("pool", lambda e: e.collective_compute("AllGather", ALU.bypass, replica_groups=GROUPS4,
                                                         ins=[src_t.ap().opt()], outs=[dst_t.ap().opt()]),
                  reads=rkeys, writes=[wkey], cc=True, nobar=nobar)

        def gather_h():
            for j in range(4):
                allgather(hl_t[j], ha_t[j], ["hloc"], ("ha", j))

        yv = yall.rearrange("(g r) t -> g r t", g=4)
        issued = []

        def y_hook(which):
            pcs = {"conv": [4, 5, 6, 7], "ml": [0, 1, 2, 3]}.get(which) or [8 + which[1]]
            for j in pcs:
                allgather(yl_t[j], ya_t[j], [("yl", j)], ("ya", j), nobar=True)
                issued.append(j)
            for j in pcs:
                S.add("pool", lambda e, j=j: e.dma_start(
                    out=yv[:, j * 512:(j + 1) * 512, :],
                    in_=ya_t[j].ap().rearrange("(g f) (e t) -> g (f e) t", g=4, t=ST)),
                    reads=[("ya", jj) for jj in issued], writes=[("yall", j)], dma=True, nobar=True)

        ctx["hook"] = y_hook

        def wgroups(l):
            Wl = W[l]
            gl = []
            for n in range(8):
                pieces = []
                for b in range(3):
                    pieces.append((wview(Wl["w_br"][b], n * 128, 128), b * 1024, 8, 128))
                    pieces.append((wview(Wl["w_g"], b * D + n * 128, 128), (3 + b) * 1024, 8, 128))
                gl.append((pieces, 6144))
            gl.append(([(wview(Wl["w_out"], 0, 1024), 0, 8, 1024)], 8192))
            for hf in range(2):
                for fg in range(4):
                    gl.append(([(wview(Wl["w_up"], hf * 2048 + fg * 512, 512), 0, 8, 512)], 4096))
                for ng in range(4):
                    src = Wl["w_down"][hf * 2048:(hf + 1) * 2048, :].rearrange("(kc p) n -> p kc n", p=128)[:, :, ng * 256:(ng + 1) * 256]
                    gl.append(([(src, 0, 16, 256)], 4096))
            return gl

        NG = 25
        wsc = [dint("wsc%d" % l, [NG, 128, 8192]).ap() for l in range(DEPTH)]
        cur_layer = {"l": 0}

        def pre_sb(sb, tag, after):
            l = cur_layer["l"]
            stg = [sb("wstg%d" % i, [128, 8192], BF16) for i in range(2)]
            for k, (pieces, used) in enumerate(wgroups(l)):
                i = k % 2
                for (src, off, a, b) in pieces:
                    dst = stg[i][:, off:off + a * b].rearrange("p (a b) -> p a b", a=a)
                    S.add("pool", lambda e, dst=dst, src=src: e.dma_start(out=dst, in_=src), reads=after, writes=[("wstg", i)], dma=True)
                S.add("pool", lambda e, i=i, k=k, used=used, l=l: e.dma_start(out=wsc[l][k, :, 0:used], in_=stg[i][:, 0:used]),
                      reads=[("wstg", i)], writes=[("wsc", l, k)], dma=True)

        ctx["pre_sb"] = pre_sb

        def emit_ts(tag, l, mode):
            last = (l == DEPTH - 1)
            with contextlib.ExitStack() as es:
                def sb(name, shape, dt):
                    return es.enter_context(nc.sbuf_tensor(tag + name, list(shape), dt))
                xs = sb("xs", [128, 8, ST], F32)
                hT = sb("hT", [128, 8, ST], BF16)
                sq = [sb("sq%d" % i, [128, ST], F32) for i in range(2)]
                rstd = sb("rstd", [128, ST], F32)
                sgt = [sb("sgt%d" % i, [128, ST], F32) for i in range(2)]
                if mode == "c":
                    big = sb("big", [128, 24, ST], BF16)
                    mT = sb("mT", [128, 8, ST], BF16)
                    NW = 3
                    wsl = [sb("wsl%d" % i, [128, 8192], BF16) for i in range(NW)]
                    mac = sb("mac", [128, ST], F32)
                    Wl = W[l]

                def next_w():
                    i = ctr["w"] % NW
                    ctr["w"] += 1
                    return i

                wk = {"k": 0}

                def load_w(slot, pieces):
                    k = wk["k"] % NG
                    wk["k"] += 1
                    used = sum(a * b for (_, _, a, b) in pieces)
                    S.add("sp", lambda e, k=k, used=used: e.dma_start(out=wsl[slot][:, 0:used], in_=wsc[l][k, :, 0:used]),
                          reads=[("wsc", l, k)], writes=[("w", slot)], dma=True)

                def stats():
                    p = next_ps()
                    for fc in range(8):
                        q = ctr["sq"] % 2
                        ctr["sq"] += 1
                        S.add("act", lambda e, q=q, fc=fc: e.activation(out=sq[q][:], in_=xs[:, fc, :], func=AF.Square),
                              reads=[("xs", fc)], writes=[("sq", q)])
                        for h in range(2):
                            S.add("pe", lambda e, q=q, fc=fc, h=h, p=p: e.matmul(
                                ps[p][:, h * 512:(h + 1) * 512], lhsT=onesf[:], rhs=sq[q][:, h * 512:(h + 1) * 512],
                                start=(fc == 0), stop=(fc == 7)),
                                reads=[("sq", q), "onesf"], writes=[("ps", p)])
                    S.add("act", lambda e, p=p: e.activation(out=rstd[:], in_=ps[p][:], func=AF.Sqrt, bias=EPS, scale=1.0 / D),
                          reads=[("ps", p)], writes=["rstd"])
                    S.add("dve", lambda e: e.reciprocal(out=rstd[:], in_=rstd[:]), reads=["rstd"], writes=["rstd"])

                def rmsnorm_to_hT(gkey):
                    stats()
                    g = GI[gkey]
                    for fc in range(8):
                        S.add("dve", lambda e, fc=fc, g=g: e.scalar_tensor_tensor(
                            out=hT[:, fc, :], in0=xs[:, fc, :], scalar=gv[:, g, fc:fc + 1], in1=rstd[:],
                            op0=ALU.mult, op1=ALU.mult),
                            reads=[("xs", fc), "rstd", ("gv", g)], writes=[("hT", fc)])

                def proj_fm(p, wslot, wcol, act_tile, act_key, nk, kbase, wstride):
                    for kc in range(nk):
                        for h in range(2):
                            S.add("pe", lambda e, h=h, kc=kc: e.matmul(
                                ps[p][:, h * 512:(h + 1) * 512],
                                lhsT=wsl[wslot][:, kc * wstride + wcol: kc * wstride + wcol + 128],
                                rhs=act_tile[:, kbase + kc, h * 512:(h + 1) * 512],
                                start=(kc == 0), stop=(kc == nk - 1)),
                                reads=[("w", wslot), (act_key, kbase + kc)], writes=[("ps", p)])

                xsrc = xT if (mode == "h0" or l == 0) else xcur
                for st in range(TOK // ST):
                    t0 = st * ST
                    for fc in range(8):
                        S.add("sp", lambda e, fc=fc, t0=t0: e.dma_start(out=xs[:, fc, :], in_=xsrc[fc * 128:(fc + 1) * 128, t0:t0 + ST]),
                              reads=["xcur"], writes=[("xs", fc)], dma=True)
                    if mode == "h0":
                        rmsnorm_to_hT("gmix0")
                        for fc in range(8):
                            S.add("sp", lambda e, fc=fc, t0=t0: e.dma_start(out=hl_t[fc // 2].ap()[(fc % 2) * 128:(fc % 2 + 1) * 128, t0:t0 + ST], in_=hT[:, fc, :]),
                                  reads=[("hT", fc)], writes=["hloc"], dma=True)
                        continue
                    yrows = yall[:, :]
                    for kc in range(24):
                        S.add("pool", lambda e, kc=kc, st=st: e.indirect_dma_start(
                            out=big[:, kc, :], out_offset=None, in_=yrows,
                            in_offset=bass.IndirectOffsetOnAxis(ap=yix[:, kc * 2 + st: kc * 2 + st + 1], axis=0)),
                            reads=[("yall", jj) for jj in range(12)] + ["yix"], writes=[("big", kc)], dma=True)
                    rmsnorm_to_hT("gmix%d" % l)
                    for n in range(8):
                        slot = next_w()
                        pieces = []
                        for b in range(3):
                            pieces.append((wview(Wl["w_br"][b], n * 128, 128), b * 1024, 8, 128))
                            pieces.append((wview(Wl["w_g"], b * D + n * 128, 128), (3 + b) * 1024, 8, 128))
                        load_w(slot, pieces)
                        for b in range(3):
                            pg = next_ps()
                            proj_fm(pg, slot, (3 + b) * 1024, hT, "hT", 8, 0, 128)
                            q = ctr["sg"] % 2
                            ctr["sg"] += 1
                            S.add("act", lambda e, q=q, pg=pg: e.activation(out=sgt[q][:], in_=ps[pg][:], func=AF.Sigmoid),
                                  reads=[("ps", pg)], writes=[("sg", q)])
                            pb_ = next_ps()
                            proj_fm(pb_, slot, b * 1024, big, "big", 8, b * 8, 128)
                            if b == 0:
                                S.add("dve", lambda e, q=q, pb_=pb_: e.tensor_tensor(out=mac[:], in0=ps[pb_][:], in1=sgt[q][:], op=ALU.mult),
                                      reads=[("ps", pb_), ("sg", q)], writes=["mac"])
                            else:
                                S.add("dve", lambda e, q=q, pb_=pb_: e.tensor_tensor(out=sgt[q][:], in0=ps[pb_][:], in1=sgt[q][:], op=ALU.mult),
                                      reads=[("ps", pb_), ("sg", q)], writes=[("sg", q)])
                                if b == 1:
                                    S.add("dve", lambda e, q=q: e.tensor_tensor(out=mac[:], in0=mac[:], in1=sgt[q][:], op=ALU.add),
                                          reads=["mac", ("sg", q)], writes=["mac"])
                                else:
                                    S.add("dve", lambda e, q=q, n=n: e.tensor_tensor(out=mT[:, n, :], in0=mac[:], in1=sgt[q][:], op=ALU.add),
                                          reads=["mac", ("sg", q)], writes=[("mT", n), "mac"])
                    slot = next_w()
                    load_w(slot, [(wview(Wl["w_out"], 0, 1024), 0, 8, 1024)])
                    for n in range(8):
                        p = next_ps()
                        proj_fm(p, slot, n * 128, mT, "mT", 8, 0, 1024)
                        S.add("dve", lambda e, n=n, p=p: e.tensor_tensor(out=xs[:, n, :], in0=xs[:, n, :], in1=ps[p][:], op=ALU.add),
                              reads=[("xs", n), ("ps", p)], writes=[("xs", n)])
                    rmsnorm_to_hT("gmlp%d" % l)
                    for hf in range(2):
                        for fg in range(4):
                            slot = next_w()
                            c0 = hf * 2048 + fg * 512
                            load_w(slot, [(wview(Wl["w_up"], c0, 512), 0, 8, 512)])
                            for j in range(4):
                                f = fg * 4 + j
                                p = next_ps()
                                proj_fm(p, slot, j * 128, hT, "hT", 8, 0, 512)
                                q = ctr["sg"] % 2
                                ctr["sg"] += 1
                                S.add("act", lambda e, q=q, p=p: e.activation(out=sgt[q][:], in_=ps[p][:], func=AF.Relu),
                                      reads=[("ps", p)], writes=[("sg", q)])
                                S.add("dve", lambda e, q=q, f=f: e.tensor_tensor(out=big[:, f, :], in0=sgt[q][:], in1=sgt[q][:], op=ALU.mult),
                                      reads=[("sg", q)], writes=[("big", f)])
                        for ng in range(4):
                            slot = next_w()
                            src = Wl["w_down"][hf * 2048:(hf + 1) * 2048, :].rearrange("(kc p) n -> p kc n", p=128)[:, :, ng * 256:(ng + 1) * 256]
                            load_w(slot, [(src, 0, 16, 256)])
                            for j in range(2):
                                n = ng * 2 + j
                                p = next_ps()
                                proj_fm(p, slot, j * 128, big, "big", 16, 0, 256)
                                S.add("dve", lambda e, n=n, p=p: e.tensor_tensor(out=xs[:, n, :], in0=xs[:, n, :], in1=ps[p][:], op=ALU.add),
                                      reads=[("xs", n), ("ps", p)], writes=[("xs", n)])
                    if not last:
                        for fc in range(8):
                            S.add("sp", lambda e, fc=fc, t0=t0: e.dma_start(out=xcur[fc * 128:(fc + 1) * 128, t0:t0 + ST], in_=xs[:, fc, :]),
                                  reads=[("xs", fc)], writes=["xcur"], dma=True)
                        rmsnorm_to_hT("gmix%d" % (l + 1))
                        for fc in range(8):
                            S.add("sp", lambda e, fc=fc, t0=t0: e.dma_start(out=hl_t[fc // 2].ap()[(fc % 2) * 128:(fc % 2 + 1) * 128, t0:t0 + ST], in_=hT[:, fc, :]),
                                  reads=[("hT", fc)] + [("ha", jj) for jj in range(4)], writes=["hloc"], dma=True)
                    else:
                        stats()
                        g = GI["gfin"]
                        for fc in range(8):
                            q = ctr["sg"] % 2
                            ctr["sg"] += 1
                            S.add("dve", lambda e, fc=fc, g=g, q=q: e.scalar_tensor_tensor(
                                out=sgt[q][:], in0=xs[:, fc, :], scalar=gv[:, g, fc:fc + 1], in1=rstd[:],
                                op0=ALU.mult, op1=ALU.mult),
                                reads=[("xs", fc), "rstd", ("gv", g)], writes=[("sg", q)])
                            S.add("sp", lambda e, fc=fc, q=q, t0=t0: e.dma_start(out=outT[fc * 128:(fc + 1) * 128, t0:t0 + ST], in_=sgt[q][:]),
                                  reads=[("sg", q)], dma=True)

        def emit_a2(tag, l):
            wA = W[l]["wA"]
            with contextlib.ExitStack() as es:
                def sb(name, shape, dt):
                    return es.enter_context(nc.sbuf_tensor(tag + name, list(shape), dt))
                wa = sb("wa", [128, 8, A_N], BF16)
                hhs = [sb("hh%d" % i, [128, 8, 4096], BF16) for i in range(2)]
                ost = [sb("ost%d" % i, [128, ST], BF16) for i in range(4)]
                ostf = sb("ostf", [2, ST], F32)
                for kc in range(8):
                    S.add("pool", lambda e, kc=kc: e.dma_start(out=wa[:, kc, :], in_=wA[kc * 128:(kc + 1) * 128, :]),
                          writes=[("wa", kc)], dma=True)
                warr = [("wa", kc) for kc in range(8)]

                def evac(p, m, pairs_fn, f32=False):
                    if f32:
                        S.add("dve", lambda e: e.tensor_copy(out=ostf[0:m, :], in_=ps[p][0:m, :]), reads=[("ps", p)], writes=["ostf"])
                        for (da, sa) in pairs_fn(ostf):
                            S.add("sp", lambda e, da=da, sa=sa: e.dma_start(out=da, in_=sa), reads=["ostf"], writes=["mixin"], dma=True)
                        return
                    o = ctr["ost"] % 4
                    ctr["ost"] += 1
                    if o % 2 == 0:
                        S.add("dve", lambda e: e.tensor_copy(out=ost[o][0:m, :], in_=ps[p][0:m, :]), reads=[("ps", p)], writes=[("ost", o)])
                    else:
                        S.add("act", lambda e: e.activation(out=ost[o][0:m, :], in_=ps[p][0:m, :], func=AF.Copy), reads=[("ps", p)], writes=[("ost", o)])
                    for (da, sa) in pairs_fn(ost[o]):
                        S.add("sp", lambda e, da=da, sa=sa: e.dma_start(out=da, in_=sa), reads=[("ost", o)], writes=["mixin"], dma=True)

                fm = []
                for j in range(2):
                    fm.append((A_Q + j * 128, 128, ctx["m_qT"], j * 128, False))
                    fm.append((A_K + j * 128, 128, ctx["m_kT"], j * 128, False))
                fm.append((A_IF, 2, ctx["m_if"], 0, True))
                for w in range(3):
                    for j in range(2):
                        fm.append((A_SC + w * 256 + j * 128, 128, ctx["scin"][w], j * 128, False))
                for j in range(2):
                    fm.append((A_SBQ + j * 128, 128, ctx["sbqk"][0], j * 128, False))
                    fm.append((A_SBK + j * 128, 128, ctx["sbqk"][1], j * 128, False))
                for half in range(2):
                    hh = hhs[half]
                    for r2 in range(2):
                        r = half * 2 + r2
                        for kc in range(8):
                            S.add("sp", lambda e, r=r, r2=r2, kc=kc, hh=hh: e.dma_start(
                                out=hh[:, kc, r2 * TOK:(r2 + 1) * TOK],
                                in_=ha_t[kc // 2].ap()[r * 256 + (kc % 2) * 128: r * 256 + (kc % 2 + 1) * 128, :]),
                                reads=[("ha", jj) for jj in range(4)], writes=[("hh", half, kc)], dma=True)
                for half in range(2):
                    hh = hhs[half]
                    for (c0, m, dst, row0, f32) in fm:
                        for tt in range(4):
                            tok = half * 4096 + tt * ST
                            p = next_ps()
                            for kc in range(8):
                                for h in range(2):
                                    S.add("pe", lambda e, p=p, h=h, kc=kc, c0=c0, m=m, tt=tt, hh=hh: e.matmul(
                                        ps[p][0:m, h * 512:(h + 1) * 512], lhsT=wa[:, kc, c0:c0 + m],
                                        rhs=hh[:, kc, tt * ST + h * 512: tt * ST + (h + 1) * 512],
                                        start=(kc == 0), stop=(kc == 7)),
                                        reads=[("wa", kc), ("hh", half, kc)], writes=[("ps", p)])
                            evac(p, m, lambda stg, dst=dst, row0=row0, m=m, tok=tok: [(dst[row0:row0 + m, tok:tok + ST], stg[0:m, :])], f32=f32)
                    for tt in range(32):
                        tok = half * 4096 + tt * 128
                        p = next_ps()
                        for kc in range(8):
                            for h in range(2):
                                cc0 = A_K + h * 512
                                S.add("pe", lambda e, p=p, h=h, kc=kc, cc0=cc0, tt=tt, hh=hh: e.matmul(
                                    ps[p][:, h * 512:(h + 1) * 512], lhsT=hh[:, kc, tt * 128:(tt + 1) * 128],
                                    rhs=wa[:, kc, cc0:cc0 + 512], start=(kc == 0), stop=(kc == 7)),
                                    reads=[("wa", kc), ("hh", half, kc)], writes=[("ps", p)])
                        evac(p, 128, lambda stg, tok=tok: [
                            (ctx["m_k"][tok:tok + 128, :], stg[:, 0:256]), (ctx["m_v"][tok:tok + 128, :], stg[:, 256:512]),
                            (ctx["m_o"][tok:tok + 128, :], stg[:, 512:768]), (ctx["sbv"][tok:tok + 128, :], stg[:, 768:1024])])

        if "h0" in phases:
            emit_ts("h0_", 0, "h0")
        barrier()
        if "ag" in phases:
            gather_h()
        for l in range(DEPTH):
            if "a2" in phases:
                emit_a2("a%d_" % l, l)
            barrier()
            if "mix" in phases:
                cur_layer["l"] = l
                build_mix(ctx=ctx, tag="L%d" % l)
            barrier()
            issued.clear()
            if "c" in phases:
                emit_ts("c%d_" % l, l, "c")
            if l + 1 < DEPTH:
                barrier()
                if "ag" in phases:
                    gather_h()
        S.finalize()
    return nc


_CACHE = {}


def _get(name, fn):
    if name not in _CACHE:
        _CACHE[name] = fn()
    return _CACHE[name]


def _run(nc, in_maps):
    res = run_bass_kernel_spmd(nc, in_maps, core_ids=list(range(NCORE)))
    return res.results


def _mix_inputs(E, l, b_if, ml_norm_g, conv_w):
    maps = []
    for c in range(NCORE):
        b, g = divmod(c, 4)
        src = [E[b * 4 + q] for q in range(4)]
        s = slice(g * 256, (g + 1) * 256)
        d = {
            "m_qT": np.ascontiguousarray(np.concatenate([r["e_qT"][g] for r in src], axis=1)),
            "m_kT": np.ascontiguousarray(np.concatenate([r["e_kT"][g] for r in src], axis=1)),
            "m_k": np.ascontiguousarray(np.concatenate([r["e_k"][g] for r in src], axis=0)),
            "m_v": np.ascontiguousarray(np.concatenate([r["e_v"][g] for r in src], axis=0)),
            "m_o": np.ascontiguousarray(np.concatenate([r["e_o"][g] for r in src], axis=0)),
            "m_if": np.ascontiguousarray(np.stack([np.concatenate([r["e_if"][g] for r in src]),
                                                   np.concatenate([r["e_if"][4 + g] for r in src])])),
            "bif": np.ascontiguousarray(np.broadcast_to(
                np.array([b_if[l, g], b_if[l, 4 + g]], np.float32), (128, 2))),
            "mlg": np.ascontiguousarray(np.broadcast_to(ml_norm_g[l, s].astype(np.float32), (64, 256))),
            "scin": np.ascontiguousarray(np.concatenate([r["e_sc"][g] for r in src], axis=2)),
            "convw": np.ascontiguousarray(conv_w[l][:, s].astype(np.float32).reshape(3, 2, 128).transpose(2, 1, 0)),
            "sbqk": np.ascontiguousarray(np.concatenate([r["e_sbqk"][g] for r in src], axis=2)),
            "sbv": np.ascontiguousarray(np.concatenate([r["e_sbv"][g] for r in src], axis=0)),
        }
        maps.append(d)
    return maps


def _y_to_ts(Y):
    out = []
    for c in range(NCORE):
        b, q = divmod(c, 4)
        ts = slice(q * TOK, (q + 1) * TOK)
        rows = []
        for br in range(3):
            for g in range(4):
                rows.append(Y[b * 4 + g]["yT"][br * 256:(br + 1) * 256, ts])
        out.append(np.ascontiguousarray(np.concatenate(rows, axis=0)))
    return out


def kernel_unfused(x, norm_mix_g, w_in, b_if, ml_norm_g, conv_w, w_ml_proj, w_sc_proj, w_sb_proj, w_out,
           norm_mlp_g, w_up, w_down, norm_final_g):
    f32 = np.float32
    x = np.asarray(x, f32)
    B_, S_, D_ = x.shape
    xf = x.reshape(NCORE, TOK, D_)
    xT = [np.ascontiguousarray(xf[c].T) for c in range(NCORE)]
    depth = w_in.shape[0]

    def cw(a):
        return np.ascontiguousarray(np.asarray(a, f32))

    nc_a = _get("ts_a", lambda: build_ts(False, True, False))
    nc_mix = _get("mix", lambda: build_mix())
    nc_ca = _get("ts_ca", lambda: build_ts(True, True, False))
    nc_cf = _get("ts_cf", lambda: build_ts(True, False, True))

    wmix0 = cw(w_in[0][:, :OFF_G])
    g0 = _g8(norm_mix_g[0])
    E = _run(nc_a, [{"xT": xT[c], "gmix_a": g0, "w_mix": wmix0} for c in range(NCORE)])
    out = None
    for l in range(depth):
        Y = _run(nc_mix, _mix_inputs(E, l, np.asarray(b_if, f32), np.asarray(ml_norm_g, f32), np.asarray(conv_w, f32)))
        yts = _y_to_ts(Y)
        base = {
            "gmix_c": _g8(norm_mix_g[l]), "w_g": cw(w_in[l][:, OFF_G:]),
            "w_ml": cw(w_ml_proj[l]), "w_sc": cw(w_sc_proj[l]), "w_sb": cw(w_sb_proj[l]), "w_out": cw(w_out[l]),
            "gmlp": _g8(norm_mlp_g[l]), "w_up": cw(w_up[l]), "w_down": cw(w_down[l]),
        }
        if l + 1 < depth:
            base["gmix_a"] = _g8(norm_mix_g[l + 1])
            base["w_mix"] = cw(w_in[l + 1][:, :OFF_G])
            E = _run(nc_ca, [dict(base, xT=xT[c], yT=yts[c]) for c in range(NCORE)])
            xT = [np.ascontiguousarray(E[c]["xoT"]) for c in range(NCORE)]
        else:
            base["gfin"] = _g8(norm_final_g)
            R = _run(nc_cf, [dict(base, xT=xT[c], yT=yts[c]) for c in range(NCORE)])
            out = np.stack([R[c]["outT"].T for c in range(NCORE)], axis=0)
    return np.ascontiguousarray(out.reshape(B_, S_, D_).astype(f32))


def _wa_cols(w, g):
    s = slice(g * 256, (g + 1) * 256)
    segs = [w[:, OFF_Q:OFF_Q + 1024][:, s], w[:, OFF_K:OFF_K + 1024][:, s], w[:, OFF_V:OFF_V + 1024][:, s],
            w[:, OFF_O:OFF_O + 1024][:, s], w[:, OFF_SBV:OFF_SBV + 1024][:, s],
            w[:, OFF_IF + g:OFF_IF + g + 1], w[:, OFF_IF + 4 + g:OFF_IF + 4 + g + 1],
            w[:, OFF_SC:OFF_SC + 1024][:, s], w[:, OFF_SC + 1024:OFF_SC + 2048][:, s], w[:, OFF_SC + 2048:OFF_SC + 3072][:, s],
            w[:, OFF_SBQ:OFF_SBQ + 1024][:, s], w[:, OFF_SBK:OFF_SBK + 1024][:, s]]
    return np.ascontiguousarray(np.concatenate(segs, axis=1))


def _yidx(q):
    idx = np.zeros((128, 48), np.int32)
    p = np.arange(128)
    for kc in range(24):
        br, rem = divmod(kc, 8)
        g, rr = divmod(rem, 2)
        row = g * 768 + br * 256 + rr * 128 + p
        for st in range(2):
            idx[:, kc * 2 + st] = row * 8 + q * 2 + st
    return idx


def kernel(x, norm_mix_g, w_in, b_if, ml_norm_g, conv_w, w_ml_proj, w_sc_proj, w_sb_proj, w_out,
           norm_mlp_g, w_up, w_down, norm_final_g):
    f32 = np.float32
    x = np.asarray(x, f32)
    B_, S_, D_ = x.shape
    xf = x.reshape(NCORE, TOK, D_)

    def cw(a):
        return np.ascontiguousarray(np.asarray(a, f32))

    nc = _get("fused", build_fused)
    shared = {"gfin": _g8(norm_final_g)}
    for l in range(DEPTH):
        t = "%d" % l
        shared.update({
            "gmix" + t: _g8(norm_mix_g[l]), "w_g" + t: cw(w_in[l][:, OFF_G:]),
            "w_ml" + t: cw(w_ml_proj[l]), "w_sc" + t: cw(w_sc_proj[l]), "w_sb" + t: cw(w_sb_proj[l]),
            "w_out" + t: cw(w_out[l]), "gmlp" + t: _g8(norm_mlp_g[l]), "w_up" + t: cw(w_up[l]), "w_down" + t: cw(w_down[l])})
    pergroup = []
    for g in range(4):
        d = {"yidx": _yidx(g)}
        s = slice(g * 256, (g + 1) * 256)
        for l in range(DEPTH):
            t = "%d" % l
            d["wA" + t] = _wa_cols(np.asarray(w_in[l], f32), g)
            d["bif" + t] = np.ascontiguousarray(np.broadcast_to(
                np.array([b_if[l, g], b_if[l, 4 + g]], f32), (128, 2)))
            d["mlg" + t] = np.ascontiguousarray(np.broadcast_to(np.asarray(ml_norm_g[l, s], f32), (64, 256)))
            d["convw" + t] = np.ascontiguousarray(np.asarray(conv_w[l], f32)[:, s].reshape(3, 2, 128).transpose(2, 1, 0))
        pergroup.append(d)
    in_maps = []
    for c in range(NCORE):
        d = dict(shared)
        d.update(pergroup[c % 4])
        d["xT"] = np.ascontiguousarray(xf[c].T)
        in_maps.append(d)
    R = _run(nc, in_maps)
    out = np.stack([R[c]["outT"].T for c in range(NCORE)], axis=0)
    return np.ascontiguousarray(out.reshape(B_, S_, D_).astype(f32))
```

```python
import contextlib
import numpy as np
import ml_dtypes
import concourse.bass as bass
import concourse.mybir as mybir
from concourse.bass_utils import run_bass_kernel_spmd

F32 = mybir.dt.float32
BF16 = mybir.dt.bfloat16
AF = mybir.ActivationFunctionType
ALU = mybir.AluOpType
AX = mybir.AxisListType
NPBF = ml_dtypes.bfloat16

D = 1024
NCORE = 8
SEQ = 8192
TOK = 2048
ST = 1024
EPS = 1e-6
NEG = -30000.0


class Op:
    __slots__ = ("eng", "fn", "dma", "deps", "signal", "sem", "target", "prev", "idx", "cc", "nobar")


class Sched:
    NDMA = 8

    def __init__(self, nc):
        self.nc = nc
        self.ops = []
        self.last_write = {}
        self.readers = {}
        self.bar = None
        self.bar_seen = set()
        self.bar_start = 0

    def barrier(self, fn):
        prev = [o for o in self.ops[self.bar_start:] if not o.nobar]
        b = self.add("dve", fn)
        seen = {d.idx for d in b.deps}
        b.deps += [o for o in prev if o.idx not in seen]
        self.bar = b
        self.bar_seen = {"dve"}
        self.bar_start = b.idx
        return b

    def add(self, eng, fn, reads=(), writes=(), dma=False, cc=False, nobar=False):
        op = Op()
        op.nobar = nobar
        op.eng, op.fn, op.dma = eng, fn, dma
        op.cc = cc
        op.signal = dma or cc
        op.sem = None
        op.target = 0
        op.prev = None
        op.idx = len(self.ops)
        deps = {}
        for k in reads:
            w = self.last_write.get(k)
            if w is not None:
                deps[w.idx] = w
        for k in writes:
            w = self.last_write.get(k)
            if w is not None:
                deps[w.idx] = w
            for r in self.readers.get(k, ()):
                deps[r.idx] = r
        for k in reads:
            self.readers.setdefault(k, []).append(op)
        for k in writes:
            self.last_write[k] = op
            self.readers[k] = []
        deps.pop(op.idx, None)
        if self.bar is not None and eng not in self.bar_seen:
            deps[self.bar.idx] = self.bar
            self.bar_seen.add(eng)
        op.deps = list(deps.values())
        self.ops.append(op)
        return op

    @staticmethod
    def _skip(d, op):
        return d.eng == "pe" and op.eng == "pe" and not d.dma and not op.dma and not d.cc and not op.cc

    def finalize(self):
        nc = self.nc
        engs = {"pe": nc.tensor, "act": nc.scalar, "dve": nc.vector, "pool": nc.gpsimd, "sp": nc.sync}
        for op in self.ops:
            for d in op.deps:
                if not self._skip(d, op):
                    d.signal = True
        with contextlib.ExitStack() as es:
            esem = {e: es.enter_context(nc.semaphore("s_" + e)) for e in ("pe", "act", "dve", "pool")}
            dsem = {e: [es.enter_context(nc.semaphore("d_%s%d" % (e, i))) for i in range(self.NDMA)]
                    for e in ("sp", "pool")}
            semid = {}
            for s in list(esem.values()) + [x for v in dsem.values() for x in v]:
                semid[id(s)] = len(semid)
            ccsem = es.enter_context(nc.semaphore("s_cc"))
            semid[id(ccsem)] = len(semid)
            cccnt = 0
            cnt = {e: 0 for e in esem}
            dcnt = {e: 0 for e in dsem}
            dlist = {e: [] for e in dsem}
            per = {e: [] for e in engs}
            for op in self.ops:
                per[op.eng].append(op)
                if op.cc:
                    cccnt += 1
                    op.sem = ccsem
                    op.target = cccnt
                elif op.dma:
                    j = dcnt[op.eng]
                    dcnt[op.eng] += 1
                    op.sem = dsem[op.eng][j % self.NDMA]
                    op.target = 16 * (j // self.NDMA + 1)
                    op.prev = dlist[op.eng][j - self.NDMA] if j >= self.NDMA else None
                    dlist[op.eng].append(op)
                elif op.signal:
                    cnt[op.eng] += 1
                    op.sem = esem[op.eng]
                    op.target = cnt[op.eng]
            self.counts = dict(cnt)
            self.dcounts = dict(dcnt)

            def emit(name, e):
                waited = {}

                def need(d):
                    key = semid[id(d.sem)]
                    if waited.get(key, 0) < d.target:
                        e.wait_ge(d.sem, d.target)
                        waited[key] = d.target

                for op in per[name]:
                    for d in op.deps:
                        if not self._skip(d, op):
                            need(d)
                    if op.prev is not None:
                        need(op.prev)
                    ins = op.fn(e)
                    if op.signal:
                        ins.then_inc(op.sem, 16 if op.dma else 1)
                if name in dlist:
                    for d in dlist[name][-self.NDMA:]:
                        need(d)

            with nc.Block() as block:
                @block.tensor
                def _(e):
                    emit("pe", e)

                @block.scalar
                def _(e):
                    emit("act", e)

                @block.vector
                def _(e):
                    emit("dve", e)

                @block.gpsimd
                def _(e):
                    emit("pool", e)

                @block.sync
                def _(e):
                    emit("sp", e)


OFF_Q, OFF_K, OFF_V, OFF_O, OFF_IF = 0, 1024, 2048, 3072, 4096
OFF_SC = 4104
OFF_SBQ, OFF_SBK, OFF_SBV = 7176, 8200, 9224
OFF_G = 10248
N_IN = 13320


def build_ts(do_c, do_a, do_final):
    nc = bass.Bass("TRN2", target_bir_lowering=False)
    S = Sched(nc)

    def din(name, shape, dt=F32):
        return nc.dram_tensor(name, list(shape), dt, kind="ExternalInput").ap()

    def dout(name, shape, dt=F32):
        return nc.dram_tensor(name, list(shape), dt, kind="ExternalOutput").ap()

    xT = din("xT", [D, TOK])
    if do_c:
        yT = din("yT", [3 * D, TOK], BF16)
        gmix_c = din("gmix_c", [128, 8])
        w_g = din("w_g", [D, 3 * D])
        w_br = [din("w_ml", [D, D]), din("w_sc", [D, D]), din("w_sb", [D, D])]
        w_out = din("w_out", [D, D])
        gmlp = din("gmlp", [128, 8])
        w_up = din("w_up", [D, 4 * D])
        w_down = din("w_down", [4 * D, D])
    if do_final:
        gfin = din("gfin", [128, 8])
        outT = dout("outT", [D, TOK])
    if do_a:
        gmix_a = din("gmix_a", [128, 8])
        w_mix = din("w_mix", [D, OFF_G])
        if do_c:
            xoT = dout("xoT", [D, TOK])
        e_qT = dout("e_qT", [4, 256, TOK], BF16)
        e_kT = dout("e_kT", [4, 256, TOK], BF16)
        e_k = dout("e_k", [4, TOK, 256], BF16)
        e_v = dout("e_v", [4, TOK, 256], BF16)
        e_o = dout("e_o", [4, TOK, 256], BF16)
        e_if = dout("e_if", [8, TOK], F32)
        e_sc = dout("e_sc", [4, 3, 256, TOK], BF16)
        e_sbqk = dout("e_sbqk", [4, 2, 256, TOK], BF16)
        e_sbv = dout("e_sbv", [4, TOK, 256], BF16)

    with contextlib.ExitStack() as es:
        def sb(name, shape, dt):
            return es.enter_context(nc.sbuf_tensor(name, list(shape), dt))

        xs = sb("xs", [128, 8, ST], F32)
        hT = sb("hT", [128, 8, ST], BF16)
        sq = [sb("sq%d" % i, [128, ST], F32) for i in range(2)]
        rstd = sb("rstd", [128, ST], F32)
        big = sb("big", [128, 24, ST], BF16)
        mT = sb("mT", [128, 8, ST], BF16)
        NW = 3
        wsl = [sb("wsl%d" % i, [128, 8192], BF16) for i in range(NW)]
        ost = [sb("ost%d" % i, [128, ST], BF16) for i in range(4)]
        ostf = sb("ostf", [128, ST], F32)
        sgt = [sb("sgt%d" % i, [128, ST], F32) for i in range(2)]
        mac = sb("mac", [128, ST], F32)
        gv = sb("gv", [128, 4, 8], F32)
        ones = sb("ones", [128, 128], F32)
        ps = [es.enter_context(nc.psum_tensor("ps%d" % i, [128, ST], F32)) for i in range(4)]

        S.add("pool", lambda e: e.memset(ones[:], 1.0), writes=["ones"])
        gi = 0
        gidx = {}
        for nm, ap in (("gmix_c", gmix_c if do_c else None), ("gmlp", gmlp if do_c else None),
                       ("gfin", gfin if do_final else None), ("gmix_a", gmix_a if do_a else None)):
            if ap is not None:
                S.add("sp", lambda e, ap=ap, gi=gi: e.dma_start(out=gv[:, gi, :], in_=ap), writes=[("gv", gi)], dma=True)
                gidx[nm] = gi
                gi += 1

        st_ctr = {"w": 0, "ps": 0, "ost": 0, "sq": 0, "sg": 0}

        def next_w():
            i = st_ctr["w"] % NW
            st_ctr["w"] += 1
            return i

        def next_ps():
            i = st_ctr["ps"] % 4
            st_ctr["ps"] += 1
            return i

        def load_w(slot, pieces):
            for (src, off, a, b) in pieces:
                dst = wsl[slot][:, off:off + a * b].rearrange("p (a b) -> p a b", a=a)
                S.add("pool", lambda e, dst=dst, src=src: e.dma_start(out=dst, in_=src),
                      writes=[("w", slot)], dma=True)

        def wview(w, c0, ncols):
            return w.rearrange("(kc p) n -> p kc n", p=128)[:, :, c0:c0 + ncols]

        def rmsnorm_to_hT(gkey):
            p = next_ps()
            for fc in range(8):
                q = st_ctr["sq"] % 2
                st_ctr["sq"] += 1
                S.add("act", lambda e, q=q, fc=fc: e.activation(out=sq[q][:], in_=xs[:, fc, :], func=AF.Square),
                      reads=[("xs", fc)], writes=[("sq", q)])
                for h in range(2):
                    S.add("pe", lambda e, q=q, fc=fc, h=h, p=p: e.matmul(
                        ps[p][:, h * 512:(h + 1) * 512], lhsT=ones[:], rhs=sq[q][:, h * 512:(h + 1) * 512],
                        start=(fc == 0), stop=(fc == 7)),
                        reads=[("sq", q), "ones"], writes=[("ps", p)])
            S.add("act", lambda e, p=p: e.activation(out=rstd[:], in_=ps[p][:], func=AF.Sqrt, bias=EPS, scale=1.0 / D),
                  reads=[("ps", p)], writes=["rstd"])
            S.add("dve", lambda e: e.reciprocal(out=rstd[:], in_=rstd[:]), reads=["rstd"], writes=["rstd"])
            g = gidx[gkey]
            for fc in range(8):
                S.add("dve", lambda e, fc=fc, g=g: e.scalar_tensor_tensor(
                    out=hT[:, fc, :], in0=xs[:, fc, :], scalar=gv[:, g, fc:fc + 1], in1=rstd[:],
                    op0=ALU.mult, op1=ALU.mult),
                    reads=[("xs", fc), "rstd", ("gv", g)], writes=[("hT", fc)])

        def proj_fm(p, wslot, wcol, ncols_m, act_tile, act_key, nk, kbase=0, wstride=None):
            for h in range(2):
                for kc in range(nk):
                    S.add("pe", lambda e, h=h, kc=kc: e.matmul(
                        ps[p][0:ncols_m, h * 512:(h + 1) * 512],
                        lhsT=wsl[wslot][:, kc * wstride + wcol: kc * wstride + wcol + ncols_m],
                        rhs=act_tile[:, kbase + kc, h * 512:(h + 1) * 512],
                        start=(kc == 0), stop=(kc == nk - 1)),
                        reads=[("w", wslot), (act_key, kbase + kc)], writes=[("ps", p)])

        for st in range(TOK // ST):
            t0 = st * ST
            for fc in range(8):
                S.add("sp", lambda e, fc=fc, t0=t0: e.dma_start(out=xs[:, fc, :], in_=xT[fc * 128:(fc + 1) * 128, t0:t0 + ST]),
                      writes=[("xs", fc)], dma=True)
            if do_c:
                for kc in range(24):
                    S.add("sp", lambda e, kc=kc, t0=t0: e.dma_start(out=big[:, kc, :], in_=yT[kc * 128:(kc + 1) * 128, t0:t0 + ST]),
                          writes=[("big", kc)], dma=True)
                rmsnorm_to_hT("gmix_c")
                for n in range(8):
                    slot = next_w()
                    pieces = []
                    for b in range(3):
                        pieces.append((wview(w_br[b], n * 128, 128), b * 1024, 8, 128))
                        pieces.append((wview(w_g, b * D + n * 128, 128), (3 + b) * 1024, 8, 128))
                    load_w(slot, pieces)
                    for b in range(3):
                        pg = next_ps()
                        proj_fm(pg, slot, (3 + b) * 1024, 128, hT, "hT", 8, wstride=128)
                        q = st_ctr["sg"] % 2
                        st_ctr["sg"] += 1
                        S.add("act", lambda e, q=q, pg=pg: e.activation(out=sgt[q][:], in_=ps[pg][:], func=AF.Sigmoid),
                              reads=[("ps", pg)], writes=[("sg", q)])
                        pb = next_ps()
                        proj_fm(pb, slot, b * 1024, 128, big, "big", 8, kbase=b * 8, wstride=128)
                        if b == 0:
                            S.add("dve", lambda e, q=q, pb=pb: e.tensor_tensor(out=mac[:], in0=ps[pb][:], in1=sgt[q][:], op=ALU.mult),
                                  reads=[("ps", pb), ("sg", q)], writes=["mac"])
                        else:
                            S.add("dve", lambda e, q=q, pb=pb: e.tensor_tensor(out=sgt[q][:], in0=ps[pb][:], in1=sgt[q][:], op=ALU.mult),
                                  reads=[("ps", pb), ("sg", q)], writes=[("sg", q)])
                            if b == 1:
                                S.add("pool", lambda e, q=q: e.tensor_tensor(out=mac[:], in0=mac[:], in1=sgt[q][:], op=ALU.add),
                                      reads=["mac", ("sg", q)], writes=["mac"])
                            else:
                                S.add("pool", lambda e, q=q, n=n: e.tensor_tensor(out=mT[:, n, :], in0=mac[:], in1=sgt[q][:], op=ALU.add),
                                      reads=["mac", ("sg", q)], writes=[("mT", n), "mac"])
                slot = next_w()
                load_w(slot, [(wview(w_out, 0, 1024), 0, 8, 1024)])
                for n in range(8):
                    p = next_ps()
                    proj_fm(p, slot, n * 128, 128, mT, "mT", 8, wstride=1024)
                    S.add("dve", lambda e, n=n, p=p: e.tensor_tensor(out=xs[:, n, :], in0=xs[:, n, :], in1=ps[p][:], op=ALU.add),
                          reads=[("xs", n), ("ps", p)], writes=[("xs", n)])
                rmsnorm_to_hT("gmlp")
                for hf in range(2):
                    for fg in range(4):
                        slot = next_w()
                        c0 = hf * 2048 + fg * 512
                        load_w(slot, [(wview(w_up, c0, 512), 0, 8, 512)])
                        for j in range(4):
                            f = fg * 4 + j
                            p = next_ps()
                            proj_fm(p, slot, j * 128, 128, hT, "hT", 8, wstride=512)
                            q = st_ctr["sg"] % 2
                            st_ctr["sg"] += 1
                            S.add("act", lambda e, q=q, p=p: e.activation(out=sgt[q][:], in_=ps[p][:], func=AF.Relu),
                                  reads=[("ps", p)], writes=[("sg", q)])
                            eng = "dve" if f % 2 == 0 else "pool"
                            S.add(eng, lambda e, q=q, f=f: e.tensor_tensor(out=big[:, f, :], in0=sgt[q][:], in1=sgt[q][:], op=ALU.mult),
                                  reads=[("sg", q)], writes=[("big", f)])
                    for ng in range(4):
                        slot = next_w()
                        src = w_down[hf * 2048:(hf + 1) * 2048, :].rearrange("(kc p) n -> p kc n", p=128)[:, :, ng * 256:(ng + 1) * 256]
                        load_w(slot, [(src, 0, 16, 256)])
                        for j in range(2):
                            n = ng * 2 + j
                            p = next_ps()
                            proj_fm(p, slot, j * 128, 128, big, "big", 16, wstride=256)
                            S.add("dve", lambda e, n=n, p=p: e.tensor_tensor(out=xs[:, n, :], in0=xs[:, n, :], in1=ps[p][:], op=ALU.add),
                                  reads=[("xs", n), ("ps", p)], writes=[("xs", n)])
                if do_a:
                    for fc in range(8):
                        S.add("sp", lambda e, fc=fc, t0=t0: e.dma_start(out=xoT[fc * 128:(fc + 1) * 128, t0:t0 + ST], in_=xs[:, fc, :]),
                              reads=[("xs", fc)], dma=True)
            if do_final:
                p = next_ps()
                for fc in range(8):
                    q = st_ctr["sq"] % 2
                    st_ctr["sq"] += 1
                    S.add("act", lambda e, q=q, fc=fc: e.activation(out=sq[q][:], in_=xs[:, fc, :], func=AF.Square),
                          reads=[("xs", fc)], writes=[("sq", q)])
                    for h in range(2):
                        S.add("pe", lambda e, q=q, fc=fc, h=h, p=p: e.matmul(
                            ps[p][:, h * 512:(h + 1) * 512], lhsT=ones[:], rhs=sq[q][:, h * 512:(h + 1) * 512],
                            start=(fc == 0), stop=(fc == 7)),
                            reads=[("sq", q), "ones"], writes=[("ps", p)])
                S.add("act", lambda e, p=p: e.activation(out=rstd[:], in_=ps[p][:], func=AF.Sqrt, bias=EPS, scale=1.0 / D),
                      reads=[("ps", p)], writes=["rstd"])
                S.add("dve", lambda e: e.reciprocal(out=rstd[:], in_=rstd[:]), reads=["rstd"], writes=["rstd"])
                g = gidx["gfin"]
                for fc in range(8):
                    q = st_ctr["sg"] % 2
                    st_ctr["sg"] += 1
                    S.add("dve", lambda e, fc=fc, g=g, q=q: e.scalar_tensor_tensor(
                        out=sgt[q][:], in0=xs[:, fc, :], scalar=gv[:, g, fc:fc + 1], in1=rstd[:],
                        op0=ALU.mult, op1=ALU.mult),
                        reads=[("xs", fc), "rstd", ("gv", g)], writes=[("sg", q)])
                    S.add("sp", lambda e, fc=fc, q=q, t0=t0: e.dma_start(out=outT[fc * 128:(fc + 1) * 128, t0:t0 + ST], in_=sgt[q][:]),
                          reads=[("sg", q)], dma=True)
            if do_a:
                rmsnorm_to_hT("gmix_a")

                def evac(p, m, dst_fn, f32=False):
                    if f32:
                        S.add("dve", lambda e: e.tensor_copy(out=ostf[0:m, :], in_=ps[p][0:m, :]),
                              reads=[("ps", p)], writes=["ostf"])
                        for (da, sa) in dst_fn(ostf):
                            S.add("sp", lambda e, da=da, sa=sa: e.dma_start(out=da, in_=sa), reads=["ostf"], dma=True)
                        return
                    o = st_ctr["ost"] % 4
                    st_ctr["ost"] += 1
                    if o % 2 == 0:
                        S.add("dve", lambda e: e.tensor_copy(out=ost[o][0:m, :], in_=ps[p][0:m, :]),
                              reads=[("ps", p)], writes=[("ost", o)])
                    else:
                        S.add("act", lambda e: e.activation(out=ost[o][0:m, :], in_=ps[p][0:m, :], func=AF.Copy),
                              reads=[("ps", p)], writes=[("ost", o)])
                    for (da, sa) in dst_fn(ost[o]):
                        S.add("sp", lambda e, da=da, sa=sa: e.dma_start(out=da, in_=sa), reads=[("ost", o)], dma=True)

                def fm_group(c0, ncols, dst_of_block):
                    slot = next_w()
                    load_w(slot, [(wview(w_mix, c0, ncols), 0, 8, ncols)])
                    nb = (ncols + 127) // 128
                    for j in range(nb):
                        m = min(128, ncols - j * 128)
                        p = next_ps()
                        proj_fm(p, slot, j * 128, m, hT, "hT", 8, wstride=ncols)
                        dst_of_block(p, m, c0 + j * 128)

                def dst_q(p, m, col):
                    g, r = divmod(col - OFF_Q, 256)
                    evac(p, m, lambda stg: [(e_qT[g, r:r + 128, t0:t0 + ST], stg[:, :])])

                def dst_kT(p, m, col):
                    g, r = divmod(col - OFF_K, 256)
                    evac(p, m, lambda stg: [(e_kT[g, r:r + 128, t0:t0 + ST], stg[:, :])])

                def dst_if(p, m, col):
                    evac(p, m, lambda stg: [(e_if[:, t0:t0 + ST], stg[0:8, :])], f32=True)

                def dst_sc(p, m, col):
                    w, r = divmod(col - OFF_SC, 1024)
                    g, r2 = divmod(r, 256)
                    evac(p, m, lambda stg: [(e_sc[g, w, r2:r2 + 128, t0:t0 + ST], stg[:, :])])

                def dst_sbqk(p, m, col):
                    w, r = divmod(col - OFF_SBQ, 1024)
                    g, r2 = divmod(r, 256)
                    evac(p, m, lambda stg: [(e_sbqk[g, w, r2:r2 + 128, t0:t0 + ST], stg[:, :])])

                for c0 in (0, 512):
                    fm_group(OFF_Q + c0, 512, dst_q)
                for c0 in (0, 512):
                    fm_group(OFF_K + c0, 512, dst_kT)
                fm_group(OFF_IF, 8, dst_if)
                for c0 in range(0, 3072, 512):
                    fm_group(OFF_SC + c0, 512, dst_sc)
                for c0 in range(0, 2048, 512):
                    fm_group(OFF_SBQ + c0, 512, dst_sbqk)

                def tm_group(base, coff, dst):
                    c0 = base + coff
                    slot = next_w()
                    load_w(slot, [(wview(w_mix, c0, 512), 0, 8, 512)])
                    for tp in range(ST // 256):
                        p = next_ps()
                        for h in range(2):
                            tt = tp * 2 + h
                            for kc in range(8):
                                S.add("pe", lambda e, h=h, kc=kc, tt=tt, p=p: e.matmul(
                                    ps[p][:, h * 512:(h + 1) * 512],
                                    lhsT=hT[:, kc, tt * 128:(tt + 1) * 128],
                                    rhs=wsl[slot][:, kc * 512:(kc + 1) * 512],
                                    start=(kc == 0), stop=(kc == 7)),
                                    reads=[("w", slot), ("hT", kc)], writes=[("ps", p)])
                        g0 = coff // 256

                        def dfn(stg, tp=tp, g0=g0):
                            res = []
                            for h in range(2):
                                tok = t0 + (tp * 2 + h) * 128
                                for gg in range(2):
                                    res.append((dst[g0 + gg, tok:tok + 128, :],
                                                stg[:, h * 512 + gg * 256: h * 512 + (gg + 1) * 256]))
                            return res
                        evac(p, 128, dfn)

                for c0 in (0, 512):
                    tm_group(OFF_K, c0, e_k)
                for c0 in (0, 512):
                    tm_group(OFF_V, c0, e_v)
                for c0 in (0, 512):
                    tm_group(OFF_O, c0, e_o)
                for c0 in (0, 512):
                    tm_group(OFF_SBV, c0, e_sbv)

        S.finalize()
    return nc


def _g8(v):
    return np.ascontiguousarray(np.asarray(v, np.float32).reshape(8, 128).T)


def _mix_consts(nc, S, sb0):
    K = {}
    cf = K["cf"] = sb0("cf", [128, 128], F32)
    identf = K["identf"] = sb0("identf", [128, 128], F32)
    identb = K["identb"] = sb0("identb", [128, 128], BF16)
    negtri = K["negtri"] = sb0("negtri", [128, 128], BF16)
    mstrict = K["mstrict"] = sb0("mstrict", [128, 128], BF16)
    mincl8 = K["mincl8"] = sb0("mincl8", [64, 8, 64], BF16)
    triu64 = K["triu64"] = sb0("triu64", [64, 64], F32)
    striu = K["striu"] = sb0("striu", [128, 128], F32)
    onesf = K["onesf"] = sb0("onesf", [128, 128], F32)
    onesb = K["onesb"] = sb0("onesb", [128, 128], BF16)
    negonesb = K["negonesb"] = sb0("negonesb", [128, 1], BF16)
    K["bart"] = sb0("bart", [128, 1], F32)

    def P(fn, reads=(), writes=()):
        return S.add("pool", fn, reads=reads, writes=writes)

    def V(fn, reads=(), writes=()):
        return S.add("dve", fn, reads=reads, writes=writes)

    def asel(out, in_, cmp, fill, cm, pat):
        return lambda e: e.affine_select(out=out, in_=in_, compare_op=cmp, fill=fill, base=0,
                                         pattern=pat, channel_multiplier=cm)
    P(lambda e: e.memset(onesf[:], 1.0), writes=["onesf"])
    P(lambda e: e.memset(onesb[:], 1.0), writes=["onesb"])
    P(lambda e: e.memset(negonesb[:], -1.0), writes=["negonesb"])
    P(asel(identf[:], onesf[:], ALU.is_equal, 0.0, 1, [[-1, 128]]), reads=["onesf"], writes=["identf"])
    V(lambda e: e.tensor_copy(out=identb[:], in_=identf[:]), reads=["identf"], writes=["identb"])
    P(lambda e: e.memset(cf[:], -1.0), writes=["cf"])
    P(asel(cf[:], cf[:], ALU.is_ge, 0.0, 1, [[-1, 128]]), reads=["cf"], writes=["cf"])
    V(lambda e: e.tensor_copy(out=negtri[:], in_=cf[:]), reads=["cf"], writes=["negtri"])
    P(lambda e: e.memset(cf[:], 0.0), reads=["cf"], writes=["cf"])
    P(asel(cf[:], cf[:], ALU.is_gt, NEG, -1, [[1, 128]]), reads=["cf"], writes=["cf"])
    V(lambda e: e.tensor_copy(out=mstrict[:], in_=cf[:]), reads=["cf"], writes=["mstrict"])
    P(lambda e: e.memset(cf[:], 0.0), reads=["cf"], writes=["cf"])
    P(asel(cf[:], cf[:], ALU.is_ge, NEG, -1, [[1, 128]]), reads=["cf"], writes=["cf"])
    for j in range(8):
        V(lambda e, j=j: e.tensor_copy(out=mincl8[:, j, :], in_=cf[0:64, 0:64]), reads=["cf"], writes=["mincl8"])
    P(asel(striu[:], onesf[:], ALU.is_ge, 0.0, -1, [[1, 128]]), reads=["onesf"], writes=["striu"])
    V(lambda e: e.tensor_copy(out=triu64[:], in_=striu[0:64, 0:64]), reads=["striu"], writes=["triu64"])
    P(asel(striu[:], onesf[:], ALU.is_gt, 0.0, -1, [[1, 128]]), reads=["onesf", "triu64"], writes=["striu"])

    return K


def build_mix(parts=("conv", "ml", "sb"), ctx=None, tag=""):
    if ctx is None:
        nc = bass.Bass("TRN2", target_bir_lowering=False)
        S = Sched(nc)

        def din(name, shape, dt=F32):
            return nc.dram_tensor(name, list(shape), dt, kind="ExternalInput").ap()

        m_qT = din("m_qT", [256, SEQ], BF16)
        m_kT = din("m_kT", [256, SEQ], BF16)
        m_k = din("m_k", [SEQ, 256], BF16)
        m_v = din("m_v", [SEQ, 256], BF16)
        m_o = din("m_o", [SEQ, 256], BF16)
        m_if = din("m_if", [2, SEQ], F32)
        bif = din("bif", [128, 2], F32)
        mlg = din("mlg", [64, 256], F32)
        scin = din("scin", [3, 256, SEQ], BF16)
        convw = din("convw", [128, 2, 3], F32)
        sbqk = din("sbqk", [2, 256, SEQ], BF16)
        sbv = din("sbv", [SEQ, 256], BF16)
        yT = nc.dram_tensor("yT", [768, SEQ], BF16, kind="ExternalOutput").ap()
        scr = nc.dram_tensor("scr", [2, SEQ], F32).ap()
    else:
        nc, S = ctx["nc"], ctx["S"]
        m_qT, m_kT, m_k, m_v, m_o, m_if = (ctx[k] for k in ("m_qT", "m_kT", "m_k", "m_v", "m_o", "m_if"))
        scin, sbqk, sbv, yT, scr = (ctx[k] for k in ("scin", "sbqk", "sbv", "yT", "scr"))
        bif, mlg, convw = ctx["bif" + tag], ctx["mlg" + tag], ctx["convw" + tag]

    def ysplit(r0, nrows, c0, c1):
        if ctx is None:
            return [(yT[r0:r0 + nrows, c0:c1], 0, nrows)]
        res = []
        for a in range(0, nrows, 64):
            j, i = divmod(r0 + a, 64)
            res.append((yT[j][i:i + 64, c0:c1], a, 64))
        return res

    def ykeys(r0, a):
        return [("yl", (r0 + a) // 64)]

    def hook(which):
        if ctx is not None and "hook" in ctx:
            ctx["hook"](which)

    with contextlib.ExitStack() as es0:
        def sb0(name, shape, dt):
            return es0.enter_context(nc.sbuf_tensor(tag + name, list(shape), dt))

        if ctx is None:
            pall = es0.enter_context(nc.psum_tensor("pall", [128, 4096], F32))
            pb = [pall[:, i * 512:(i + 1) * 512] for i in range(8)]
            pb7b = pall.bitcast(BF16)[:, 7 * 1024:8 * 1024]
            K = _mix_consts(nc, S, sb0)
        else:
            pb, pb7b, K = ctx["pb"], ctx["pb7b"], ctx["consts"]
        identf, identb, negtri, mstrict, mincl8 = K["identf"], K["identb"], K["negtri"], K["mstrict"], K["mincl8"]
        triu64, striu, onesf, onesb, negonesb, bart = K["triu64"], K["striu"], K["onesf"], K["onesb"], K["negonesb"], K["bart"]

        def P(fn, reads=(), writes=()):
            return S.add("pool", fn, reads=reads, writes=writes)

        def V(fn, reads=(), writes=()):
            return S.add("dve", fn, reads=reads, writes=writes)

        def A(fn, reads=(), writes=()):
            return S.add("act", fn, reads=reads, writes=writes)

        def T(fn, reads=(), writes=()):
            return S.add("pe", fn, reads=reads, writes=writes)

        def DMA(fn, reads=(), writes=()):
            return S.add("sp", fn, reads=reads, writes=writes, dma=True)

        def barrier():
            S.barrier(lambda e: e.memset(bart[:], 0.0))

        if "conv" in parts:
            with contextlib.ExitStack() as es:
                def sb(name, shape, dt):
                    return es.enter_context(nc.sbuf_tensor(tag + name, list(shape), dt))
                cw = sb("cw", [128, 2, 3], F32)
                tb = sb("c_b", [128, SEQ], BF16)
                tc_ = sb("c_c", [128, SEQ], BF16)
                tu = sb("c_u", [128, SEQ], BF16)
                z = sb("c_z", [128, SEQ + 2], F32)
                yv = sb("c_y", [128, SEQ], F32)
                ob = sb("c_o", [128, SEQ], BF16)
                DMA(lambda e: e.dma_start(out=cw[:], in_=convw), writes=["cw"])
                V(lambda e: e.memset(z[:, 0:2], 0.0), writes=["zpad"])
                for cc in range(2):
                    r0 = cc * 128
                    DMA(lambda e, r0=r0: e.dma_start(out=tb[:], in_=scin[0, r0:r0 + 128, :]), writes=["c_b"])
                    DMA(lambda e, r0=r0: e.dma_start(out=tc_[:], in_=scin[1, r0:r0 + 128, :]), writes=["c_c"])
                    DMA(lambda e, r0=r0: e.dma_start(out=tu[:], in_=scin[2, r0:r0 + 128, :]), writes=["c_u"])
                    V(lambda e: e.tensor_tensor(out=z[:, 2:SEQ + 2], in0=tc_[:], in1=tu[:], op=ALU.mult),
                      reads=["c_c", "c_u"], writes=["c_z"])
                    V(lambda e, cc=cc: e.tensor_scalar(out=yv[:], in0=z[:, 2:SEQ + 2], scalar1=cw[:, cc, 2:3], scalar2=None, op0=ALU.mult),
                      reads=["c_z", "cw"], writes=["c_y"])
                    V(lambda e, cc=cc: e.scalar_tensor_tensor(out=yv[:], in0=z[:, 1:SEQ + 1], scalar=cw[:, cc, 1:2], in1=yv[:],
                                                              op0=ALU.mult, op1=ALU.add),
                      reads=["c_z", "zpad", "cw", "c_y"], writes=["c_y"])
                    V(lambda e, cc=cc: e.scalar_tensor_tensor(out=yv[:], in0=z[:, 0:SEQ], scalar=cw[:, cc, 0:1], in1=yv[:],
                                                              op0=ALU.mult, op1=ALU.add),
                      reads=["c_z", "zpad", "cw", "c_y"], writes=["c_y"])
                    V(lambda e: e.tensor_tensor(out=ob[:], in0=yv[:], in1=tb[:], op=ALU.mult),
                      reads=["c_y", "c_b"], writes=["c_o"])
                    for (dst, a, n_) in ysplit(256 + r0, 128, 0, SEQ):
                        DMA(lambda e, dst=dst, a=a, n_=n_: e.dma_start(out=dst, in_=ob[a:a + n_, :]), reads=["c_o"], writes=ykeys(256 + r0, a))
                barrier()
                hook("conv")

        if "ml" in parts:
            with contextlib.ExitStack() as es:
                def sb(name, shape, dt):
                    return es.enter_context(nc.sbuf_tensor(tag + name, list(shape), dt))
                qT = sb("qT", [128, 2, SEQ], BF16)
                kT = sb("kT", [128, 2, SEQ], BF16)
                ifr = sb("ifr", [128, 2, 64], F32)
                bif_s = sb("bif_s", [128, 2], F32)
                mlg_s = sb("mlg_s", [64, 256], F32)
                G = {n: sb("g_" + n, [128, 64], F32) for n in
                     ("ipre", "l1", "bneg", "u", "pmA", "pmB", "Pm", "ai", "em", "ws", "negP", "tmp")}
                col = {n: sb("c_" + n, [128, 1], F32) for n in ("tot", "off", "cmax", "mprev", "M", "negM", "dec")}
                l1T = sb("l1T", [64, 128], F32)
                uT = sb("uT", [64, 128], F32)
                emT = sb("emT", [64, 128], F32)
                wsT = sb("wsT", [64, 128], F32)
                rowA = sb("rowA", [1, 130], F32)
                rowB = sb("rowB", [1, 130], F32)
                decrow = sb("decrow", [1, 128], F32)
                decbc = sb("decbc", [128, 128], F32)
                DMA(lambda e: e.dma_start(out=ifr[:], in_=m_if.rearrange("r (c t) -> c r t", c=128)), writes=["ifr"])
                DMA(lambda e: e.dma_start(out=bif_s[:], in_=bif), writes=["bif"])
                DMA(lambda e: e.dma_start(out=mlg_s[:], in_=mlg), writes=["mlg"])
                for kc in range(2):
                    DMA(lambda e, kc=kc: e.dma_start(out=qT[:, kc, :], in_=m_qT[kc * 128:(kc + 1) * 128, :]), writes=[("qT", kc)])
                    DMA(lambda e, kc=kc: e.dma_start(out=kT[:, kc, :], in_=m_kT[kc * 128:(kc + 1) * 128, :]), writes=[("kT", kc)])
                V(lambda e: e.tensor_scalar(out=G["ipre"][:], in0=ifr[:, 0, :], scalar1=bif_s[:, 0:1], scalar2=None, op0=ALU.add),
                  reads=["ifr", "bif"], writes=["ipre"])
                V(lambda e: e.tensor_scalar(out=G["tmp"][:], in0=ifr[:, 1, :], scalar1=bif_s[:, 1:2], scalar2=None, op0=ALU.add),
                  reads=["ifr", "bif"], writes=["tmp"])
                A(lambda e: e.activation(out=G["l1"][:], in_=G["tmp"][:], func=AF.Exp, scale=-1.0), reads=["tmp"], writes=["l1"])
                A(lambda e: e.activation(out=G["l1"][:], in_=G["l1"][:], func=AF.Ln, bias=1.0), reads=["l1"], writes=["l1"])
                T(lambda e: e.transpose(out=pb[0][0:64, 0:128], in_=G["l1"][:], identity=identf[:]), reads=["l1", "identf"], writes=[("pb", 0)])
                V(lambda e: e.tensor_copy(out=l1T[:], in_=pb[0][0:64, 0:128]), reads=[("pb", 0)], writes=["l1T"])
                T(lambda e: e.matmul(pb[1][:, 0:64], lhsT=l1T[:], rhs=triu64[:], start=True, stop=True), reads=["l1T", "triu64"], writes=[("pb", 1)])
                V(lambda e: e.tensor_copy(out=col["tot"][:], in_=pb[1][:, 63:64]), reads=[("pb", 1)], writes=["tot"])
                T(lambda e: e.matmul(pb[2][:, 0:1], lhsT=striu[:], rhs=col["tot"][:], start=True, stop=True), reads=["tot", "striu"], writes=[("pb", 2)])
                V(lambda e: e.tensor_copy(out=col["off"][:], in_=pb[2][:, 0:1]), reads=[("pb", 2)], writes=["off"])
                V(lambda e: e.tensor_scalar(out=G["bneg"][:], in0=pb[1][:, 0:64], scalar1=col["off"][:, 0:1], scalar2=None, op0=ALU.add),
                  reads=[("pb", 1), "off"], writes=["bneg"])
                V(lambda e: e.tensor_tensor(out=G["u"][:], in0=G["ipre"][:], in1=G["bneg"][:], op=ALU.add), reads=["ipre", "bneg"], writes=["u"])
                src, dst = "u", "pmA"
                for sh in (1, 2, 4, 8, 16, 32):
                    V(lambda e, s=src, d=dst, sh=sh: e.tensor_copy(out=G[d][:, 0:sh], in_=G[s][:, 0:sh]), reads=[src], writes=[dst])
                    V(lambda e, s=src, d=dst, sh=sh: e.tensor_tensor(out=G[d][:, sh:64], in0=G[s][:, sh:64], in1=G[s][:, 0:64 - sh], op=ALU.max),
                      reads=[src], writes=[dst])
                    src, dst = dst, ("pmB" if dst == "pmA" else "pmA")
                pm = src
                V(lambda e: e.tensor_copy(out=col["cmax"][:], in_=G[pm][:, 63:64]), reads=[pm], writes=["cmax"])
                T(lambda e: e.transpose(out=pb[3][0:1, 0:128], in_=col["cmax"][:], identity=identf[:]), reads=["cmax", "identf"], writes=[("pb", 3)])
                V(lambda e: e.memset(rowA[:], 0.0), writes=["rowA"])
                V(lambda e: e.tensor_copy(out=rowA[0:1, 1:128], in_=pb[3][0:1, 0:127]), reads=[("pb", 3), "rowA"], writes=["rowA"])
                rs, rd = rowA, rowB
                rsn, rdn = "rowA", "rowB"
                for sh in (1, 2, 4, 8, 16, 32, 64):
                    V(lambda e, s=rs, d=rd, sh=sh: e.tensor_copy(out=d[0:1, 0:sh], in_=s[0:1, 0:sh]), reads=[rsn], writes=[rdn])
                    V(lambda e, s=rs, d=rd, sh=sh: e.tensor_tensor(out=d[0:1, sh:128], in0=s[0:1, sh:128], in1=s[0:1, 0:128 - sh], op=ALU.max),
                      reads=[rsn], writes=[rdn])
                    rs, rd, rsn, rdn = rd, rs, rdn, rsn
                T(lambda e, r=rs: e.matmul(pb[4][:, 0:1], lhsT=r[0:1, 0:128], rhs=onesf[0:1, 0:1], start=True, stop=True),
                  reads=[rsn, "onesf"], writes=[("pb", 4)])
                V(lambda e: e.tensor_copy(out=col["mprev"][:], in_=pb[4][:, 0:1]), reads=[("pb", 4)], writes=["mprev"])
                V(lambda e: e.tensor_scalar(out=G["Pm"][:], in0=G[pm][:], scalar1=col["mprev"][:, 0:1], scalar2=None, op0=ALU.max),
                  reads=[pm, "mprev"], writes=["Pm"])
                V(lambda e: e.tensor_copy(out=col["M"][:], in_=G["Pm"][:, 63:64]), reads=["Pm"], writes=["M"])
                V(lambda e: e.tensor_scalar(out=col["negM"][:], in0=col["M"][:], scalar1=-1.0, scalar2=None, op0=ALU.mult), reads=["M"], writes=["negM"])
                V(lambda e: e.tensor_scalar(out=G["negP"][:], in0=G["Pm"][:], scalar1=-1.0, scalar2=None, op0=ALU.mult), reads=["Pm"], writes=["negP"])
                A(lambda e: e.activation(out=G["ai"][:], in_=G["Pm"][:], func=AF.Exp, scale=-1.0, bias=col["mprev"][:, 0:1]),
                  reads=["Pm", "mprev"], writes=["ai"])
                V(lambda e: e.tensor_tensor(out=G["tmp"][:], in0=G["bneg"][:], in1=G["Pm"][:], op=ALU.subtract), reads=["bneg", "Pm", "tmp"], writes=["tmp"])
                A(lambda e: e.activation(out=G["em"][:], in_=G["tmp"][:], func=AF.Exp), reads=["tmp"], writes=["em"])
                A(lambda e: e.activation(out=G["ws"][:], in_=G["u"][:], func=AF.Exp, bias=col["negM"][:, 0:1]), reads=["u", "negM"], writes=["ws"])
                V(lambda e: e.tensor_scalar(out=G["ws"][:], in0=G["ws"][:], scalar1=1.0 / 16, scalar2=None, op0=ALU.mult), reads=["ws"], writes=["ws"])
                A(lambda e: e.activation(out=col["dec"][:], in_=col["mprev"][:], func=AF.Exp, bias=col["negM"][:, 0:1]),
                  reads=["mprev", "negM"], writes=["dec"])
                for (srcn, dstt, dstn, bank) in (("u", uT, "uT", 0), ("em", emT, "emT", 1), ("ws", wsT, "wsT", 2)):
                    T(lambda e, s=srcn, b=bank: e.transpose(out=pb[b][0:64, 0:128], in_=G[s][:], identity=identf[:]),
                      reads=[srcn, "identf"], writes=[("pb", bank)])
                    if dstn == "uT":
                        V(lambda e, d=dstt, b=bank: e.tensor_scalar(out=d[:], in0=pb[b][0:64, 0:128], scalar1=-float(np.log(16.0)), scalar2=None, op0=ALU.add),
                          reads=[("pb", bank)], writes=[dstn])
                    else:
                        V(lambda e, d=dstt, b=bank: e.tensor_copy(out=d[:], in_=pb[b][0:64, 0:128]), reads=[("pb", bank)], writes=[dstn])
                T(lambda e: e.transpose(out=pb[3][0:1, 0:128], in_=col["dec"][:], identity=identf[:]), reads=["dec", "identf"], writes=[("pb", 3)])
                V(lambda e: e.tensor_copy(out=decrow[:], in_=pb[3][0:1, 0:128]), reads=[("pb", 3)], writes=["decrow"])
                T(lambda e: e.matmul(pb[4][:, 0:128], lhsT=onesf[0:1, 0:128], rhs=decrow[0:1, 0:128], start=True, stop=True),
                  reads=["decrow", "onesf"], writes=[("pb", 4)])
                V(lambda e: e.tensor_copy(out=decbc[:], in_=pb[4][:, 0:128]), reads=[("pb", 4)], writes=["decbc"])
                DMA(lambda e: e.dma_start(out=scr[0, :].rearrange("(c t) -> c t", c=128), in_=G["negP"][:]), reads=["negP"], writes=["scr0"])
                DMA(lambda e: e.dma_start(out=scr[1, :].rearrange("(c t) -> c t", c=128), in_=G["ai"][:]), reads=["ai"], writes=["scr1"])

                NB = 2
                kg = [sb("kg%d" % i, [64, 8, 256], BF16) for i in range(NB)]
                vg = [sb("vg%d" % i, [64, 8, 257], BF16) for i in range(NB)]
                og = [sb("og%d" % i, [64, 8, 256], BF16) for i in range(NB)]
                prow = [sb("prow%d" % i, [1, 2, 512], F32) for i in range(NB)]
                sgo = sb("sgo", [64, 8, 256], F32)
                yg = sb("yg", [64, 8, 256], F32)
                yng = sb("yng", [64, 8, 256], BF16)
                nraw = sb("nraw", [64, 8, 257], F32)
                sqs = sb("sqs", [64, 256], F32)
                qsT = [sb("qsT%d" % i, [128, 2, 512], BF16) for i in range(NB)]
                ymT = [sb("ymT%d" % i, [128, 2, 512], BF16) for i in range(NB)]
                wtg = [sb("wtg%d" % i, [64, 512], F32) for i in range(NB)]
                swT = [sb("swT%d" % i, [64, 64], BF16) for i in range(16)]
                kw = [sb("kw%d" % i, [64, 256], BF16) for i in range(16)]
                Cf = sb("Cf", [128, 2, 257], F32)
                Cb = [sb("Cb%d" % i, [128, 2, 257], BF16) for i in range(2)]
                ss = sb("ss", [64, 8], F32)
                rsd = sb("rsd", [64, 8], F32)
                dm1 = sb("dm1", [64, 8], F32)
                dm2 = sb("dm2", [64, 8], F32)
                pb0b = ctx["pb0b"] if ctx is not None else pall.bitcast(BF16)[:, 0:1024]
                for i in range(NB):
                    V(lambda e, i=i: e.memset(vg[i][:, :, 256:257], 1.0), writes=[("vg1", i)])
                V(lambda e: e.memset(Cf[:], 0.0), writes=[("Cf", 0), ("Cf", 1)])
                V(lambda e: e.memset(Cb[0][:], 0.0), writes=[("Cb", 0, 0), ("Cb", 0, 1)])
                for gi in range(16):
                    b = gi % NB
                    tok0 = gi * 512
                    DMA(lambda e, b=b, tok0=tok0: e.dma_start(out=kg[b][:], in_=m_k[tok0:tok0 + 512, :].rearrange("(j s) d -> s j d", j=8)), writes=[("kg", b)])
                    DMA(lambda e, b=b, tok0=tok0: e.dma_start(out=vg[b][:, :, 0:256], in_=m_v[tok0:tok0 + 512, :].rearrange("(j s) d -> s j d", j=8)), writes=[("vg", b)])
                    DMA(lambda e, b=b, tok0=tok0: e.dma_start(out=og[b][:], in_=m_o[tok0:tok0 + 512, :].rearrange("(j s) d -> s j d", j=8)), writes=[("og", b)])
                    DMA(lambda e, b=b, tok0=tok0: e.dma_start(out=prow[b][0:1, :, :], in_=scr[:, tok0:tok0 + 512]),
                        reads=["scr0", "scr1"], writes=[("prow", b)])
                    T(lambda e, b=b: e.matmul(pb[0][0:64, :], lhsT=onesf[0:1, 0:64], rhs=prow[b][0:1, 0, :], start=True, stop=False),
                      reads=[("prow", b), "onesf"], writes=[("pb", 0)])
                    T(lambda e: e.matmul(pb[0][0:64, :], lhsT=identb[0:64, 0:64], rhs=mincl8[:].rearrange("s j t -> s (j t)"), start=False, stop=True),
                      reads=["identb", "mincl8"], writes=[("pb", 0)])
                    for j in range(8):
                        c = gi * 8 + j
                        A(lambda e, j=j, c=c, b=b: e.activation(out=wtg[b][:, j * 64:(j + 1) * 64], in_=pb[0][0:64, j * 64:(j + 1) * 64], func=AF.Exp, bias=uT[:, c:c + 1]),
                          reads=[("pb", 0), "uT"], writes=[("wtg", b, j)])
                    T(lambda e, b=b: e.matmul(pb[1][:, :], lhsT=onesf[0:1, 0:128], rhs=prow[b][0:1, 1, :], start=True, stop=True),
                      reads=[("prow", b), "onesf"], writes=[("pb", 1)])
                    for kc in range(2):
                        V(lambda e, b=b, kc=kc, tok0=tok0: e.tensor_tensor(out=qsT[b][:, kc, :], in0=qT[:, kc, tok0:tok0 + 512], in1=pb[1][:, :], op=ALU.mult),
                          reads=[("qT", kc), ("pb", 1)], writes=[("qsT", b, kc)])
                    A(lambda e, b=b: e.activation(out=sgo[:], in_=og[b][:], func=AF.Sigmoid), reads=[("og", b)], writes=["sgo"])
                    V(lambda e: e.memset(ss[:], 0.0), reads=["ss"], writes=["ss"])
                    for j in range(8):
                        c = gi * 8 + j
                        w16 = c % 16
                        cs = slice(c * 64, (c + 1) * 64)
                        sbk = 1 + (j % 2)
                        for kc in range(2):
                            T(lambda e, kc=kc, cs=cs, sbk=sbk: e.matmul(pb[sbk][0:64, 0:64], lhsT=kT[:, kc, cs], rhs=qT[:, kc, cs], start=(kc == 0), stop=(kc == 1)),
                              reads=[("kT", kc), ("qT", kc)], writes=[("pb", sbk)])
                        V(lambda e, w16=w16, sbk=sbk, b=b, j=j: e.tensor_tensor(out=swT[w16][:], in0=pb[sbk][0:64, 0:64], in1=wtg[b][:, j * 64:(j + 1) * 64], op=ALU.mult),
                          reads=[("pb", sbk), ("wtg", b, j)], writes=[("swT", w16)])
                        A(lambda e, w16=w16, b=b, j=j, c=c: e.activation(out=kw[w16][:], in_=kg[b][:, j, :], func=AF.Copy, scale=wsT[:, c:c + 1]),
                          reads=[("kg", b), "wsT"], writes=[("kw", w16)])
                    for j in range(8):
                        c = gi * 8 + j
                        w = c % 2
                        w16 = c % 16
                        js = slice(j * 64, (j + 1) * 64)
                        ub = 4 + 2 * w
                        nb = 3 if j % 2 == 0 else 0
                        for kc in range(2):
                            T(lambda e, w16=w16, b=b, j=j, kc=kc, ub=ub: e.matmul(pb[ub + kc][:, 0:257], lhsT=kw[w16][:, kc * 128:(kc + 1) * 128], rhs=vg[b][:, j, :],
                                                                               start=True, stop=True),
                              reads=[("kw", w16), ("vg", b), ("vg1", b)], writes=[("pb", ub + kc)])
                        T(lambda e, w16=w16, b=b, j=j, nb=nb: e.matmul(pb[nb][0:64, 0:257], lhsT=swT[w16][:], rhs=vg[b][:, j, :], start=True, stop=False),
                          reads=[("swT", w16), ("vg", b), ("vg1", b)], writes=[("pb", nb)])
                        for kc in range(2):
                            T(lambda e, w=w, b=b, kc=kc, js=js, nb=nb: e.matmul(pb[nb][0:64, 0:257], lhsT=qsT[b][:, kc, js], rhs=Cb[w][:, kc, :],
                                                                             start=False, stop=(kc == 1)),
                              reads=[("qsT", b, kc), ("Cb", w, kc)], writes=[("pb", nb)])
                        for kc in range(2):
                            V(lambda e, kc=kc, c=c, w=w, ub=ub: e.scalar_tensor_tensor(out=Cb[1 - w][:, kc, :], in0=Cf[:, kc, :], scalar=decbc[:, c:c + 1], in1=pb[ub + kc][:, 0:257],
                                                                                   op0=ALU.mult, op1=ALU.add),
                              reads=[("Cf", kc), "decbc", ("pb", ub + kc)], writes=[("Cb", 1 - w, kc)])
                        A(lambda e, j=j, nb=nb: e.activation(out=nraw[:, j, :], in_=pb[nb][0:64, 0:257], func=AF.Copy), reads=[("pb", nb)], writes=[("nraw", j)])
                        for kc in range(2):
                            V(lambda e, kc=kc, c=c, ub=ub: e.scalar_tensor_tensor(out=Cf[:, kc, :], in0=Cf[:, kc, :], scalar=decbc[:, c:c + 1], in1=pb[ub + kc][:, 0:257],
                                                                              op0=ALU.mult, op1=ALU.add),
                              reads=[("Cf", kc), "decbc", ("pb", ub + kc)], writes=[("Cf", kc)])
                    nr = [("nraw", j) for j in range(8)]
                    c0 = gi * 8
                    V(lambda e: e.tensor_scalar(out=dm1[:], in0=nraw[:, :, 256], scalar1=-1.0, scalar2=None, op0=ALU.mult), reads=nr, writes=["dm1"])
                    V(lambda e: e.tensor_tensor(out=dm1[:], in0=dm1[:], in1=nraw[:, :, 256], op=ALU.max), reads=nr + ["dm1"], writes=["dm1"])
                    V(lambda e, c0=c0: e.tensor_tensor(out=dm1[:], in0=dm1[:], in1=emT[:, c0:c0 + 8], op=ALU.max), reads=["dm1", "emT"], writes=["dm1"])
                    V(lambda e: e.reciprocal(out=dm2[:], in_=dm1[:]), reads=["dm1"], writes=["dm2"])
                    for j in range(8):
                        V(lambda e, j=j: e.scalar_tensor_tensor(out=yg[:, j, :], in0=nraw[:, j, 0:256], scalar=dm2[:, j:j + 1], in1=sgo[:, j, :],
                                                                op0=ALU.mult, op1=ALU.mult),
                          reads=[("nraw", j), "dm2", "sgo"], writes=[("yg", j)])
                        A(lambda e, j=j: e.activation(out=sqs[:], in_=yg[:, j, :], func=AF.Square, accum_out=ss[:, j:j + 1]),
                          reads=[("yg", j), "ss"], writes=["sqs", "ss"])
                    A(lambda e: e.activation(out=rsd[:], in_=ss[:], func=AF.Ln, scale=1.0 / 256, bias=EPS), reads=["ss"], writes=["rsd"])
                    A(lambda e: e.activation(out=rsd[:], in_=rsd[:], func=AF.Exp, scale=-0.5), reads=["rsd"], writes=["rsd"])
                    for j in range(8):
                        V(lambda e, j=j: e.scalar_tensor_tensor(out=yng[:, j, :], in0=yg[:, j, :], scalar=rsd[:, j:j + 1], in1=mlg_s[:],
                                                                op0=ALU.mult, op1=ALU.mult),
                          reads=[("yg", j), "rsd", "mlg"], writes=[("yng", j)])
                        for vc in range(2):
                            T(lambda e, j=j, vc=vc: e.transpose(out=pb0b[:, vc * 512 + j * 64: vc * 512 + (j + 1) * 64], in_=yng[:, j, vc * 128:(vc + 1) * 128],
                                                                identity=identb[0:64, 0:64]),
                              reads=[("yng", j), "identb"], writes=[("pb", 0)])
                    A(lambda e, b=b: e.activation(out=ymT[b][:].rearrange("p a t -> p (a t)"), in_=pb0b[:, :], func=AF.Copy), reads=[("pb", 0)], writes=[("ymT", b)])
                    for vc in range(2):
                        for (dst, a, n_) in ysplit(vc * 128, 128, tok0, tok0 + 512):
                            DMA(lambda e, b=b, vc=vc, dst=dst, a=a, n_=n_: e.dma_start(out=dst, in_=ymT[b][a:a + n_, vc, :]),
                                reads=[("ymT", b)], writes=ykeys(vc * 128, a))
                barrier()
                hook("ml")

        if "sb" in parts:
            with contextlib.ExitStack() as es:
                def sb(name, shape, dt):
                    return es.enter_context(nc.sbuf_tensor(tag + name, list(shape), dt))
                sq_ = sb("sq_", [128, 2, SEQ], BF16)
                sk_ = sb("sk_", [128, 2, SEQ], BF16)
                sv_ = sb("sv_", [128, 64, 256], BF16)
                for hp in range(2):
                    DMA(lambda e, hp=hp: e.dma_start(out=sq_[:, hp, :], in_=sbqk[0, hp * 128:(hp + 1) * 128, :]), writes=[("sq", hp)])
                    DMA(lambda e, hp=hp: e.dma_start(out=sk_[:, hp, :], in_=sbqk[1, hp * 128:(hp + 1) * 128, :]), writes=[("sk", hp)])
                    V(lambda e, hp=hp: e.tensor_scalar(out=sq_[:, hp, :], in0=sq_[:, hp, :], scalar1=0.125, scalar2=None, op0=ALU.mult),
                      reads=[("sq", hp)], writes=[("sq", hp)])
                for q4 in range(4):
                    DMA(lambda e, q4=q4: e.dma_start(out=sv_[:, q4 * 16:(q4 + 1) * 16, :],
                                                     in_=sbv[q4 * 2048:(q4 + 1) * 2048, :].rearrange("(kb s) d -> s kb d", s=128)),
                        writes=[("sv", q4)])
                if ctx is not None and "pre_sb" in ctx:
                    ctx["pre_sb"](sb, tag, [("sq", 0), ("sq", 1), ("sk", 0), ("sk", 1)] + [("sv", q4) for q4 in range(4)])
                e_sb = [sb("e_sb%d" % i, [128, 1024], F32) for i in range(2)]
                l_sb = [sb("l_sb%d" % i, [128, 1024], BF16) for i in range(2)]
                a_sb = [sb("a_sb%d" % i, [128, 1024], BF16) for i in range(2)]
                lacc = [sb("lacc%d" % i, [128, 512], F32) for i in range(2)]
                laccb = [[sb("laccb%d_%d" % (i, k), [128, 512], BF16) for k in range(2)] for i in range(2)]
                negones = sb("negones", [128, 128], BF16)
                osb = [[sb("osb%d_%d" % (i, k), [64, 512], BF16) for k in range(2)] for i in range(2)]
                V(lambda e: e.memset(negones[:], -1.0), writes=["negones"])
                pallv = ctx["pall"] if ctx is not None else pall
                steps = []
                for hp in range(2):
                    for qt in range(16):
                        for kb in range(4 * qt + 3, -1, -1):
                            for hh in range(2):
                                steps.append((2 * hp + hh, qt, kb))
                n = len(steps)
                ng = n // 2

                def info(i):
                    h, qt, kb = steps[i]
                    j = kb - 4 * qt
                    c0 = 128 * j if j > 0 else 0
                    first = (kb == 4 * qt + 3)
                    last = (kb == 0)
                    g, k = divmod(i, 2)
                    zb = (g % 3) * 2 + k
                    off = k * 512
                    seq = 4 * qt + 3 - kb
                    return h, qt, kb, j, c0, first, last, g, zb, off, seq

                def gcols(g):
                    return info(2 * g)[4], 1024

                def zgrp(g):
                    base = (g % 3) * 1024
                    return pallv[:, base:base + 1024]

                def s1(i):
                    h, qt, kb, j, c0, first, last, g, zb, off, seq = info(i)
                    hp, hh = divmod(h, 2)
                    pr = slice(hh * 64, hh * 64 + 64)
                    T(lambda e: e.matmul(pb[zb][:, c0:512], lhsT=sk_[pr, hp, kb * 128:(kb + 1) * 128], rhs=sq_[pr, hp, qt * 512 + c0:(qt + 1) * 512],
                                         start=True, stop=True),
                      reads=[("sk", hp), ("sq", hp)], writes=[("zg", g % 3)])
                    if j >= 0:
                        T(lambda e: e.matmul(pb[zb][:, c0:c0 + 128], lhsT=identb[:], rhs=mstrict[:], start=False, stop=True, skip_group_check=True),
                          reads=["identb", "mstrict"], writes=[("zg", g % 3)])

                def v2(ap, lo):
                    return ap.rearrange("p (k c) -> p k c", k=2)[:, :, lo:512]

                def ga_e(g):
                    lo, hi = gcols(g)
                    A(lambda e: e.activation(out=v2(e_sb[g % 2][:], lo), in_=v2(zgrp(g), lo), func=AF.Exp),
                      reads=[("zg", g % 3)], writes=[("e_sb", g % 2)])

                def ga_l(g):
                    lo, hi = gcols(g)
                    A(lambda e: e.activation(out=v2(l_sb[g % 2][:], lo), in_=v2(e_sb[g % 2][:], lo), func=AF.Ln, bias=1.0),
                      reads=[("e_sb", g % 2)], writes=[("l_sb", g % 2)])

                def ga_a(g):
                    lo, hi = gcols(g)
                    A(lambda e: e.activation(out=v2(a_sb[g % 2][:], lo), in_=v2(zgrp(g), lo), func=AF.Exp),
                      reads=[("zg", g % 3)], writes=[("a_sb", g % 2)])

                def s3_tri(i):
                    h, qt, kb, j, c0, first, last, g, zb, off, seq = info(i)
                    lt = l_sb[g % 2]
                    T(lambda e: e.matmul(pb[zb][:, c0:512], lhsT=negtri[:], rhs=lt[:, off + c0:off + 512], start=False, stop=first, skip_group_check=True),
                      reads=["negtri", ("l_sb", g % 2)], writes=[("zg", g % 3)])

                def s3(i):
                    h, qt, kb, j, c0, first, last, g, zb, off, seq = info(i)
                    st = h % 2
                    lt = l_sb[g % 2]
                    pc0 = 128 * (j + 1) if j >= 0 else 0
                    if not first:
                        prevb = laccb[st][(seq - 1) % 2]
                        T(lambda e: e.matmul(pb[zb][:, pc0:512], lhsT=negones[:], rhs=prevb[:, pc0:512], start=False, stop=True, skip_group_check=True),
                          reads=["negones", ("laccb", st, (seq - 1) % 2)], writes=[("zg", g % 3)])
                    if not last:
                        if first:
                            V(lambda e: e.tensor_copy(out=lacc[st][:, c0:512], in_=lt[:, off + c0:off + 512]),
                              reads=[("l_sb", g % 2)], writes=[("lacc", st)])
                        else:
                            if pc0 > c0:
                                V(lambda e: e.tensor_copy(out=lacc[st][:, c0:pc0], in_=lt[:, off + c0:off + pc0]),
                                  reads=[("l_sb", g % 2)], writes=[("lacc", st)])
                            V(lambda e: e.tensor_tensor(out=lacc[st][:, pc0:512], in0=lacc[st][:, pc0:512], in1=lt[:, off + pc0:off + 512], op=ALU.add),
                              reads=[("l_sb", g % 2), ("lacc", st)], writes=[("lacc", st)])
                        curb = laccb[st][seq % 2]
                        V(lambda e: e.tensor_copy(out=curb[:, c0:512], in_=lacc[st][:, c0:512]),
                          reads=[("lacc", st)], writes=[("laccb", st, seq % 2)])

                def s5(i):
                    h, qt, kb, j, c0, first, last, g, zb, off, seq = info(i)
                    st = h % 2
                    T(lambda e: e.matmul(pb[6 + st][0:64, c0:512], lhsT=sv_[:, kb, h * 64:(h + 1) * 64], rhs=a_sb[g % 2][:, off + c0:off + 512],
                                         start=first, stop=last, skip_group_check=True),
                      reads=[("sv", kb // 16), ("a_sb", g % 2)], writes=[("pb", 6 + st)])
                    if last:
                        o = qt % 2
                        V(lambda e: e.tensor_copy(out=osb[st][o][:], in_=pb[6 + st][0:64, :]), reads=[("pb", 6 + st)], writes=[("osb", st, o)])
                        for (dst, a, n_) in ysplit(512 + h * 64, 64, qt * 512, (qt + 1) * 512):
                            DMA(lambda e, dst=dst, a=a, n_=n_: e.dma_start(out=dst, in_=osb[st][o][a:a + n_, :]),
                                reads=[("osb", st, o)], writes=ykeys(512 + h * 64, a))
                        if qt == 15:
                            hook(("sb", h))

                def steps_of(g):
                    return [2 * g, 2 * g + 1]

                for i in steps_of(0):
                    s1(i)
                for g in range(ng + 1):
                    if g + 1 < ng:
                        for i in steps_of(g + 1):
                            s1(i)
                    if g < ng:
                        ga_e(g)
                        ga_l(g)
                    if 0 <= g - 1 < ng:
                        ga_a(g - 1)
                    if g < ng:
                        for i in steps_of(g):
                            s3_tri(i)
                        for i in steps_of(g):
                            s3(i)
                    if 0 <= g - 1 < ng:
                        for i in steps_of(g - 1):
                            s5(i)
        if ctx is None:
            S.finalize()
    return nc


A_Q, A_K, A_V, A_O, A_SBV, A_IF, A_SC, A_SBQ, A_SBK, A_N = 0, 256, 512, 768, 1024, 1280, 1282, 2050, 2306, 2562
GROUPS4 = [[0, 1, 2, 3], [4, 5, 6, 7]]
DEPTH = 2


def build_fused(phases=("h0", "ag", "a2", "mix", "c")):
    nc = bass.Bass("TRN2", target_bir_lowering=False)
    S = Sched(nc)

    def din(name, shape, dt=F32):
        return nc.dram_tensor(name, list(shape), dt, kind="ExternalInput").ap()

    def dint(name, shape, dt=BF16):
        return nc.dram_tensor(name, list(shape), dt)

    xT = din("xT", [D, TOK])
    yidx = din("yidx", [128, 48], mybir.dt.int32)
    gfin = din("gfin", [128, 8])
    outT = nc.dram_tensor("outT", [D, TOK], F32, kind="ExternalOutput").ap()
    W = []
    for l in range(DEPTH):
        t = "%d" % l
        W.append(dict(
            gmix=din("gmix" + t, [128, 8]), wA=din("wA" + t, [D, A_N]), w_g=din("w_g" + t, [D, 3 * D]),
            w_br=[din("w_ml" + t, [D, D]), din("w_sc" + t, [D, D]), din("w_sb" + t, [D, D])],
            w_out=din("w_out" + t, [D, D]), gmlp=din("gmlp" + t, [128, 8]),
            w_up=din("w_up" + t, [D, 4 * D]), w_down=din("w_down" + t, [4 * D, D]),
            bif=din("bif" + t, [128, 2]), mlg=din("mlg" + t, [64, 256]), convw=din("convw" + t, [128, 2, 3])))
    hl_t = [dint("hl%d" % j, [256, TOK]) for j in range(4)]
    ha_t = [dint("ha%d" % j, [4 * 256, TOK]) for j in range(4)]
    yl_t = [dint("yl%d" % j, [64, SEQ]) for j in range(12)]
    ya_t = [dint("ya%d" % j, [4 * 64, SEQ]) for j in range(12)]
    yall_t = dint("yall", [4 * 768 * 8, ST])
    yall = yall_t.ap()
    yloc = [t_.ap() for t_ in yl_t]
    xcur = dint("xcur", [D, TOK], F32).ap()
    ctx = {"nc": nc, "S": S}
    ctx["m_qT"] = dint("m_qT", [256, SEQ]).ap()
    ctx["m_kT"] = dint("m_kT", [256, SEQ]).ap()
    ctx["m_k"] = dint("m_k", [SEQ, 256]).ap()
    ctx["m_v"] = dint("m_v", [SEQ, 256]).ap()
    ctx["m_o"] = dint("m_o", [SEQ, 256]).ap()
    ctx["m_if"] = dint("m_if", [2, SEQ], F32).ap()
    ctx["scin"] = dint("scin", [3, 256, SEQ]).ap()
    ctx["sbqk"] = dint("sbqk", [2, 256, SEQ]).ap()
    ctx["sbv"] = dint("sbv", [SEQ, 256]).ap()
    ctx["scr"] = dint("scr", [2, SEQ], F32).ap()
    ctx["yT"] = yloc
    for l in range(DEPTH):
        for k in ("bif", "mlg", "convw"):
            ctx[k + "L%d" % l] = W[l][k]

    with contextlib.ExitStack() as es0:
        def sb0(name, shape, dt):
            return es0.enter_context(nc.sbuf_tensor(name, list(shape), dt))

        pall = es0.enter_context(nc.psum_tensor("pall", [128, 4096], F32))
        pb = [pall[:, i * 512:(i + 1) * 512] for i in range(8)]
        ps = [pall[:, i * 1024:(i + 1) * 1024] for i in range(4)]
        ctx["pb"] = pb
        ctx["pb7b"] = pall.bitcast(BF16)[:, 7 * 1024:8 * 1024]
        ctx["pb0b"] = pall.bitcast(BF16)[:, 0:1024]
        ctx["pall"] = pall
        K = ctx["consts"] = _mix_consts(nc, S, sb0)
        onesf = K["onesf"]
        bart = K["bart"]
        gv = sb0("gv", [128, 2 * DEPTH + 1, 8], F32)
        yix = sb0("yix", [128, 48], mybir.dt.int32)
        GI = {}
        for i, (nm, ap) in enumerate([("gmix0", W[0]["gmix"]), ("gmlp0", W[0]["gmlp"]), ("gmix1", W[1]["gmix"]),
                                      ("gmlp1", W[1]["gmlp"]), ("gfin", gfin)]):
            S.add("sp", lambda e, ap=ap, i=i: e.dma_start(out=gv[:, i, :], in_=ap), writes=[("gv", i)], dma=True)
            GI[nm] = i
        S.add("sp", lambda e: e.dma_start(out=yix[:], in_=yidx), writes=["yix"], dma=True)

        def barrier():
            S.barrier(lambda e: e.memset(bart[:], 0.0))

        ctr = {"w": 0, "ps": 0, "ost": 0, "sq": 0, "sg": 0}

        def next_ps():
            i = ctr["ps"] % 4
            ctr["ps"] += 1
            return i

        def wview(w, c0, ncols):
            return w.rearrange("(kc p) n -> p kc n", p=128)[:, :, c0:c0 + ncols]

        def allgather(src_t, dst_t, rkeys, wkey, nobar=False):
            S.add("pool", lambda e: e.collective_compute("AllGather", ALU.bypass, replica_groups=GROUPS4,
                                                         ins=[src_t.ap().opt()], outs=[dst_t.ap().opt()]),
                  reads=rkeys, writes=[wkey], cc=True, nobar=nobar)

        def gather_h():
            for j in range(4):
                allgather(hl_t[j], ha_t[j], ["hloc"], ("ha", j))

        yv = yall.rearrange("(g r) t -> g r t", g=4)
        issued = []

        def y_hook(which):
            pcs = {"conv": [4, 5, 6, 7], "ml": [0, 1, 2, 3]}.get(which) or [8 + which[1]]
            for j in pcs:
                allgather(yl_t[j], ya_t[j], [("yl", j)], ("ya", j), nobar=True)
                issued.append(j)
            for j in pcs:
                S.add("pool", lambda e, j=j: e.dma_start(
                    out=yv[:, j * 512:(j + 1) * 512, :],
                    in_=ya_t[j].ap().rearrange("(g f) (e t) -> g (f e) t", g=4, t=ST)),
                    reads=[("ya", jj) for jj in issued], writes=[("yall", j)], dma=True, nobar=True)

        ctx["hook"] = y_hook

        def wgroups(l):
            Wl = W[l]
            gl = []
            for n in range(8):
                pieces = []
                for b in range(3):
                    pieces.append((wview(Wl["w_br"][b], n * 128, 128), b * 1024, 8, 128))
                    pieces.append((wview(Wl["w_g"], b * D + n * 128, 128), (3 + b) * 1024, 8, 128))
                gl.append((pieces, 6144))
            gl.append(([(wview(Wl["w_out"], 0, 1024), 0, 8, 1024)], 8192))
            for hf in range(2):
                for fg in range(4):
                    gl.append(([(wview(Wl["w_up"], hf * 2048 + fg * 512, 512), 0, 8, 512)], 4096))
                for ng in range(4):
                    src = Wl["w_down"][hf * 2048:(hf + 1) * 2048, :].rearrange("(kc p) n -> p kc n", p=128)[:, :, ng * 256:(ng + 1) * 256]
                    gl.append(([(src, 0, 16, 256)], 4096))
            return gl

        NG = 25
        wsc = [dint("wsc%d" % l, [NG, 128, 8192]).ap() for l in range(DEPTH)]
        cur_layer = {"l": 0}

        def pre_sb(sb, tag, after):
            l = cur_layer["l"]
            stg = [sb("wstg%d" % i, [128, 8192], BF16) for i in range(2)]
            for k, (pieces, used) in enumerate(wgroups(l)):
                i = k % 2
                for (src, off, a, b) in pieces:
                    dst = stg[i][:, off:off + a * b].rearrange("p (a b) -> p a b", a=a)
                    S.add("pool", lambda e, dst=dst, src=src: e.dma_start(out=dst, in_=src), reads=after, writes=[("wstg", i)], dma=True)
                S.add("pool", lambda e, i=i, k=k, used=used, l=l: e.dma_start(out=wsc[l][k, :, 0:used], in_=stg[i][:, 0:used]),
                      reads=[("wstg", i)], writes=[("wsc", l, k)], dma=True)

        ctx["pre_sb"] = pre_sb

        def emit_ts(tag, l, mode):
            last = (l == DEPTH - 1)
            with contextlib.ExitStack() as es:
                def sb(name, shape, dt):
                    return es.enter_context(nc.sbuf_tensor(tag + name, list(shape), dt))
                xs = sb("xs", [128, 8, ST], F32)
                hT = sb("hT", [128, 8, ST], BF16)
                sq = [sb("sq%d" % i, [128, ST], F32) for i in range(2)]
                rstd = sb("rstd", [128, ST], F32)
                sgt = [sb("sgt%d" % i, [128, ST], F32) for i in range(2)]
                if mode == "c":
                    big = sb("big", [128, 24, ST], BF16)
                    mT = sb("mT", [128, 8, ST], BF16)
                    NW = 3
                    wsl = [sb("wsl%d" % i, [128, 8192], BF16) for i in range(NW)]
                    mac = sb("mac", [128, ST], F32)
                    Wl = W[l]

                def next_w():
                    i = ctr["w"] % NW
                    ctr["w"] += 1
                    return i

                wk = {"k": 0}

                def load_w(slot, pieces):
                    k = wk["k"] % NG
                    wk["k"] += 1
                    used = sum(a * b for (_, _, a, b) in pieces)
                    S.add("sp", lambda e, k=k, used=used: e.dma_start(out=wsl[slot][:, 0:used], in_=wsc[l][k, :, 0:used]),
                          reads=[("wsc", l, k)], writes=[("w", slot)], dma=True)

                def stats():
                    p = next_ps()
                    for fc in range(8):
                        q = ctr["sq"] % 2
                        ctr["sq"] += 1
                        S.add("act", lambda e, q=q, fc=fc: e.activation(out=sq[q][:], in_=xs[:, fc, :], func=AF.Square),
                              reads=[("xs", fc)], writes=[("sq", q)])
                        for h in range(2):
                            S.add("pe", lambda e, q=q, fc=fc, h=h, p=p: e.matmul(
                                ps[p][:, h * 512:(h + 1) * 512], lhsT=onesf[:], rhs=sq[q][:, h * 512:(h + 1) * 512],
                                start=(fc == 0), stop=(fc == 7)),
                                reads=[("sq", q), "onesf"], writes=[("ps", p)])
                    S.add("act", lambda e, p=p: e.activation(out=rstd[:], in_=ps[p][:], func=AF.Sqrt, bias=EPS, scale=1.0 / D),
                          reads=[("ps", p)], writes=["rstd"])
                    S.add("dve", lambda e: e.reciprocal(out=rstd[:], in_=rstd[:]), reads=["rstd"], writes=["rstd"])

                def rmsnorm_to_hT(gkey):
                    stats()
                    g = GI[gkey]
                    for fc in range(8):
                        S.add("dve", lambda e, fc=fc, g=g: e.scalar_tensor_tensor(
                            out=hT[:, fc, :], in0=xs[:, fc, :], scalar=gv[:, g, fc:fc + 1], in1=rstd[:],
                            op0=ALU.mult, op1=ALU.mult),
                            reads=[("xs", fc), "rstd", ("gv", g)], writes=[("hT", fc)])

                def proj_fm(p, wslot, wcol, act_tile, act_key, nk, kbase, wstride):
                    for kc in range(nk):
                        for h in range(2):
                            S.add("pe", lambda e, h=h, kc=kc: e.matmul(
                                ps[p][:, h * 512:(h + 1) * 512],
                                lhsT=wsl[wslot][:, kc * wstride + wcol: kc * wstride + wcol + 128],
                                rhs=act_tile[:, kbase + kc, h * 512:(h + 1) * 512],
                                start=(kc == 0), stop=(kc == nk - 1)),
                                reads=[("w", wslot), (act_key, kbase + kc)], writes=[("ps", p)])

                xsrc = xT if (mode == "h0" or l == 0) else xcur
                for st in range(TOK // ST):
                    t0 = st * ST
                    for fc in range(8):
                        S.add("sp", lambda e, fc=fc, t0=t0: e.dma_start(out=xs[:, fc, :], in_=xsrc[fc * 128:(fc + 1) * 128, t0:t0 + ST]),
                              reads=["xcur"], writes=[("xs", fc)], dma=True)
                    if mode == "h0":
                        rmsnorm_to_hT("gmix0")
                        for fc in range(8):
                            S.add("sp", lambda e, fc=fc, t0=t0: e.dma_start(out=hl_t[fc // 2].ap()[(fc % 2) * 128:(fc % 2 + 1) * 128, t0:t0 + ST], in_=hT[:, fc, :]),
                                  reads=[("hT", fc)], writes=["hloc"], dma=True)
                        continue
                    yrows = yall[:, :]
                    for kc in range(24):
                        S.add("pool", lambda e, kc=kc, st=st: e.indirect_dma_start(
                            out=big[:, kc, :], out_offset=None, in_=yrows,
                            in_offset=bass.IndirectOffsetOnAxis(ap=yix[:, kc * 2 + st: kc * 2 + st + 1], axis=0)),
                            reads=[("yall", jj) for jj in range(12)] + ["yix"], writes=[("big", kc)], dma=True)
                    rmsnorm_to_hT("gmix%d" % l)
                    for n in range(8):
                        slot = next_w()
                        pieces = []
                        for b in range(3):
                            pieces.append((wview(Wl["w_br"][b], n * 128, 128), b * 1024, 8, 128))
                            pieces.append((wview(Wl["w_g"], b * D + n * 128, 128), (3 + b) * 1024, 8, 128))
                        load_w(slot, pieces)
                        for b in range(3):
                            pg = next_ps()
                            proj_fm(pg, slot, (3 + b) * 1024, hT, "hT", 8, 0, 128)
                            q = ctr["sg"] % 2
                            ctr["sg"] += 1
                            S.add("act", lambda e, q=q, pg=pg: e.activation(out=sgt[q][:], in_=ps[pg][:], func=AF.Sigmoid),
                                  reads=[("ps", pg)], writes=[("sg", q)])
                            pb_ = next_ps()
                            proj_fm(pb_, slot, b * 1024, big, "big", 8, b * 8, 128)
                            if b == 0:
                                S.add("dve", lambda e, q=q, pb_=pb_: e.tensor_tensor(out=mac[:], in0=ps[pb_][:], in1=sgt[q][:], op=ALU.mult),
                                      reads=[("ps", pb_), ("sg", q)], writes=["mac"])
                            else:
                                S.add("dve", lambda e, q=q, pb_=pb_: e.tensor_tensor(out=sgt[q][:], in0=ps[pb_][:], in1=sgt[q][:], op=ALU.mult),
                                      reads=[("ps", pb_), ("sg", q)], writes=[("sg", q)])
                                if b == 1:
                                    S.add("dve", lambda e, q=q: e.tensor_tensor(out=mac[:], in0=mac[:], in1=sgt[q][:], op=ALU.add),
                                          reads=["mac", ("sg", q)], writes=["mac"])
                                else:
                                    S.add("dve", lambda e, q=q, n=n: e.tensor_tensor(out=mT[:, n, :], in0=mac[:], in1=sgt[q][:], op=ALU.add),
                                          reads=["mac", ("sg", q)], writes=[("mT", n), "mac"])
                    slot = next_w()
                    load_w(slot, [(wview(Wl["w_out"], 0, 1024), 0, 8, 1024)])
                    for n in range(8):
                        p = next_ps()
                        proj_fm(p, slot, n * 128, mT, "mT", 8, 0, 1024)
                        S.add("dve", lambda e, n=n, p=p: e.tensor_tensor(out=xs[:, n, :], in0=xs[:, n, :], in1=ps[p][:], op=ALU.add),
                              reads=[("xs", n), ("ps", p)], writes=[("xs", n)])
                    rmsnorm_to_hT("gmlp%d" % l)
                    for hf in range(2):
                        for fg in range(4):
                            slot = next_w()
                            c0 = hf * 2048 + fg * 512
                            load_w(slot, [(wview(Wl["w_up"], c0, 512), 0, 8, 512)])
                            for j in range(4):
                                f = fg * 4 + j
                                p = next_ps()
                                proj_fm(p, slot, j * 128, hT, "hT", 8, 0, 512)
                                q = ctr["sg"] % 2
                                ctr["sg"] += 1
                                S.add("act", lambda e, q=q, p=p: e.activation(out=sgt[q][:], in_=ps[p][:], func=AF.Relu),
                                      reads=[("ps", p)], writes=[("sg", q)])
                                S.add("dve", lambda e, q=q, f=f: e.tensor_tensor(out=big[:, f, :], in0=sgt[q][:], in1=sgt[q][:], op=ALU.mult),
                                      reads=[("sg", q)], writes=[("big", f)])
                        for ng in range(4):
                            slot = next_w()
                            src = Wl["w_down"][hf * 2048:(hf + 1) * 2048, :].rearrange("(kc p) n -> p kc n", p=128)[:, :, ng * 256:(ng + 1) * 256]
                            load_w(slot, [(src, 0, 16, 256)])
                            for j in range(2):
                                n = ng * 2 + j
                                p = next_ps()
                                proj_fm(p, slot, j * 128, big, "big", 16, 0, 256)
                                S.add("dve", lambda e, n=n, p=p: e.tensor_tensor(out=xs[:, n, :], in0=xs[:, n, :], in1=ps[p][:], op=ALU.add),
                                      reads=[("xs", n), ("ps", p)], writes=[("xs", n)])
                    if not last:
                        for fc in range(8):
                            S.add("sp", lambda e, fc=fc, t0=t0: e.dma_start(out=xcur[fc * 128:(fc + 1) * 128, t0:t0 + ST], in_=xs[:, fc, :]),
                                  reads=[("xs", fc)], writes=["xcur"], dma=True)
                        rmsnorm_to_hT("gmix%d" % (l + 1))
                        for fc in range(8):
                            S.add("sp", lambda e, fc=fc, t0=t0: e.dma_start(out=hl_t[fc // 2].ap()[(fc % 2) * 128:(fc % 2 + 1) * 128, t0:t0 + ST], in_=hT[:, fc, :]),
                                  reads=[("hT", fc)] + [("ha", jj) for jj in range(4)], writes=["hloc"], dma=True)
                    else:
                        stats()
                        g = GI["gfin"]
                        for fc in range(8):
                            q = ctr["sg"] % 2
                            ctr["sg"] += 1
                            S.add("dve", lambda e, fc=fc, g=g, q=q: e.scalar_tensor_tensor(
                                out=sgt[q][:], in0=xs[:, fc, :], scalar=gv[:, g, fc:fc + 1], in1=rstd[:],
                                op0=ALU.mult, op1=ALU.mult),
                                reads=[("xs", fc), "rstd", ("gv", g)], writes=[("sg", q)])
                            S.add("sp", lambda e, fc=fc, q=q, t0=t0: e.dma_start(out=outT[fc * 128:(fc + 1) * 128, t0:t0 + ST], in_=sgt[q][:]),
                                  reads=[("sg", q)], dma=True)

        def emit_a2(tag, l):
            wA = W[l]["wA"]
            with contextlib.ExitStack() as es:
                def sb(name, shape, dt):
                    return es.enter_context(nc.sbuf_tensor(tag + name, list(shape), dt))
                wa = sb("wa", [128, 8, A_N], BF16)
                hhs = [sb("hh%d" % i, [128, 8, 4096], BF16) for i in range(2)]
                ost = [sb("ost%d" % i, [128, ST], BF16) for i in range(4)]
                ostf = sb("ostf", [2, ST], F32)
                for kc in range(8):
                    S.add("pool", lambda e, kc=kc: e.dma_start(out=wa[:, kc, :], in_=wA[kc * 128:(kc + 1) * 128, :]),
                          writes=[("wa", kc)], dma=True)
                warr = [("wa", kc) for kc in range(8)]

                def evac(p, m, pairs_fn, f32=False):
                    if f32:
                        S.add("dve", lambda e: e.tensor_copy(out=ostf[0:m, :], in_=ps[p][0:m, :]), reads=[("ps", p)], writes=["ostf"])
                        for (da, sa) in pairs_fn(ostf):
                            S.add("sp", lambda e, da=da, sa=sa: e.dma_start(out=da, in_=sa), reads=["ostf"], writes=["mixin"], dma=True)
                        return
                    o = ctr["ost"] % 4
                    ctr["ost"] += 1
                    if o % 2 == 0:
                        S.add("dve", lambda e: e.tensor_copy(out=ost[o][0:m, :], in_=ps[p][0:m, :]), reads=[("ps", p)], writes=[("ost", o)])
                    else:
                        S.add("act", lambda e: e.activation(out=ost[o][0:m, :], in_=ps[p][0:m, :], func=AF.Copy), reads=[("ps", p)], writes=[("ost", o)])
                    for (da, sa) in pairs_fn(ost[o]):
                        S.add("sp", lambda e, da=da, sa=sa: e.dma_start(out=da, in_=sa), reads=[("ost", o)], writes=["mixin"], dma=True)

                fm = []
                for j in range(2):
                    fm.append((A_Q + j * 128, 128, ctx["m_qT"], j * 128, False))
                    fm.append((A_K + j * 128, 128, ctx["m_kT"], j * 128, False))
                fm.append((A_IF, 2, ctx["m_if"], 0, True))
                for w in range(3):
                    for j in range(2):
                        fm.append((A_SC + w * 256 + j * 128, 128, ctx["scin"][w], j * 128, False))
                for j in range(2):
                    fm.append((A_SBQ + j * 128, 128, ctx["sbqk"][0], j * 128, False))
                    fm.append((A_SBK + j * 128, 128, ctx["sbqk"][1], j * 128, False))
                for half in range(2):
                    hh = hhs[half]
                    for r2 in range(2):
                        r = half * 2 + r2
                        for kc in range(8):
                            S.add("sp", lambda e, r=r, r2=r2, kc=kc, hh=hh: e.dma_start(
                                out=hh[:, kc, r2 * TOK:(r2 + 1) * TOK],
                                in_=ha_t[kc // 2].ap()[r * 256 + (kc % 2) * 128: r * 256 + (kc % 2 + 1) * 128, :]),
                                reads=[("ha", jj) for jj in range(4)], writes=[("hh", half, kc)], dma=True)
                for half in range(2):
                    hh = hhs[half]
                    for (c0, m, dst, row0, f32) in fm:
                        for tt in range(4):
                            tok = half * 4096 + tt * ST
                            p = next_ps()
                            for kc in range(8):
                                for h in range(2):
                                    S.add("pe", lambda e, p=p, h=h, kc=kc, c0=c0, m=m, tt=tt, hh=hh: e.matmul(
                                        ps[p][0:m, h * 512:(h + 1) * 512], lhsT=wa[:, kc, c0:c0 + m],
                                        rhs=hh[:, kc, tt * ST + h * 512: tt * ST + (h + 1) * 512],
                                        start=(kc == 0), stop=(kc == 7)),
                                        reads=[("wa", kc), ("hh", half, kc)], writes=[("ps", p)])
                            evac(p, m, lambda stg, dst=dst, row0=row0, m=m, tok=tok: [(dst[row0:row0 + m, tok:tok + ST], stg[0:m, :])], f32=f32)
                    for tt in range(32):
                        tok = half * 4096 + tt * 128
                        p = next_ps()
                        for kc in range(8):
                            for h in range(2):
                                cc0 = A_K + h * 512
                                S.add("pe", lambda e, p=p, h=h, kc=kc, cc0=cc0, tt=tt, hh=hh: e.matmul(
                                    ps[p][:, h * 512:(h + 1) * 512], lhsT=hh[:, kc, tt * 128:(tt + 1) * 128],
                                    rhs=wa[:, kc, cc0:cc0 + 512], start=(kc == 0), stop=(kc == 7)),
                                    reads=[("wa", kc), ("hh", half, kc)], writes=[("ps", p)])
                        evac(p, 128, lambda stg, tok=tok: [
                            (ctx["m_k"][tok:tok + 128, :], stg[:, 0:256]), (ctx["m_v"][tok:tok + 128, :], stg[:, 256:512]),
                            (ctx["m_o"][tok:tok + 128, :], stg[:, 512:768]), (ctx["sbv"][tok:tok + 128, :], stg[:, 768:1024])])

        if "h0" in phases:
            emit_ts("h0_", 0, "h0")
        barrier()
        if "ag" in phases:
            gather_h()
        for l in range(DEPTH):
            if "a2" in phases:
                emit_a2("a%d_" % l, l)
            barrier()
            if "mix" in phases:
                cur_layer["l"] = l
                build_mix(ctx=ctx, tag="L%d" % l)
            barrier()
            issued.clear()
            if "c" in phases:
                emit_ts("c%d_" % l, l, "c")
            if l + 1 < DEPTH:
                barrier()
                if "ag" in phases:
                    gather_h()
        S.finalize()
    return nc


_CACHE = {}


def _get(name, fn):
    if name not in _CACHE:
        _CACHE[name] = fn()
    return _CACHE[name]


def _run(nc, in_maps):
    res = run_bass_kernel_spmd(nc, in_maps, core_ids=list(range(NCORE)))
    return res.results


def _mix_inputs(E, l, b_if, ml_norm_g, conv_w):
    maps = []
    for c in range(NCORE):
        b, g = divmod(c, 4)
        src = [E[b * 4 + q] for q in range(4)]
        s = slice(g * 256, (g + 1) * 256)
        d = {
            "m_qT": np.ascontiguousarray(np.concatenate([r["e_qT"][g] for r in src], axis=1)),
            "m_kT": np.ascontiguousarray(np.concatenate([r["e_kT"][g] for r in src], axis=1)),
            "m_k": np.ascontiguousarray(np.concatenate([r["e_k"][g] for r in src], axis=0)),
            "m_v": np.ascontiguousarray(np.concatenate([r["e_v"][g] for r in src], axis=0)),
            "m_o": np.ascontiguousarray(np.concatenate([r["e_o"][g] for r in src], axis=0)),
            "m_if": np.ascontiguousarray(np.stack([np.concatenate([r["e_if"][g] for r in src]),
                                                   np.concatenate([r["e_if"][4 + g] for r in src])])),
            "bif": np.ascontiguousarray(np.broadcast_to(
                np.array([b_if[l, g], b_if[l, 4 + g]], np.float32), (128, 2))),
            "mlg": np.ascontiguousarray(np.broadcast_to(ml_norm_g[l, s].astype(np.float32), (64, 256))),
            "scin": np.ascontiguousarray(np.concatenate([r["e_sc"][g] for r in src], axis=2)),
            "convw": np.ascontiguousarray(conv_w[l][:, s].astype(np.float32).reshape(3, 2, 128).transpose(2, 1, 0)),
            "sbqk": np.ascontiguousarray(np.concatenate([r["e_sbqk"][g] for r in src], axis=2)),
            "sbv": np.ascontiguousarray(np.concatenate([r["e_sbv"][g] for r in src], axis=0)),
        }
        maps.append(d)
    return maps


def _y_to_ts(Y):
    out = []
    for c in range(NCORE):
        b, q = divmod(c, 4)
        ts = slice(q * TOK, (q + 1) * TOK)
        rows = []
        for br in range(3):
            for g in range(4):
                rows.append(Y[b * 4 + g]["yT"][br * 256:(br + 1) * 256, ts])
        out.append(np.ascontiguousarray(np.concatenate(rows, axis=0)))
    return out


def kernel_unfused(x, norm_mix_g, w_in, b_if, ml_norm_g, conv_w, w_ml_proj, w_sc_proj, w_sb_proj, w_out,
           norm_mlp_g, w_up, w_down, norm_final_g):
    f32 = np.float32
    x = np.asarray(x, f32)
    B_, S_, D_ = x.shape
    xf = x.reshape(NCORE, TOK, D_)
    xT = [np.ascontiguousarray(xf[c].T) for c in range(NCORE)]
    depth = w_in.shape[0]

    def cw(a):
        return np.ascontiguousarray(np.asarray(a, f32))

    nc_a = _get("ts_a", lambda: build_ts(False, True, False))
    nc_mix = _get("mix", lambda: build_mix())
    nc_ca = _get("ts_ca", lambda: build_ts(True, True, False))
    nc_cf = _get("ts_cf", lambda: build_ts(True, False, True))

    wmix0 = cw(w_in[0][:, :OFF_G])
    g0 = _g8(norm_mix_g[0])
    E = _run(nc_a, [{"xT": xT[c], "gmix_a": g0, "w_mix": wmix0} for c in range(NCORE)])
    out = None
    for l in range(depth):
        Y = _run(nc_mix, _mix_inputs(E, l, np.asarray(b_if, f32), np.asarray(ml_norm_g, f32), np.asarray(conv_w, f32)))
        yts = _y_to_ts(Y)
        base = {
            "gmix_c": _g8(norm_mix_g[l]), "w_g": cw(w_in[l][:, OFF_G:]),
            "w_ml": cw(w_ml_proj[l]), "w_sc": cw(w_sc_proj[l]), "w_sb": cw(w_sb_proj[l]), "w_out": cw(w_out[l]),
            "gmlp": _g8(norm_mlp_g[l]), "w_up": cw(w_up[l]), "w_down": cw(w_down[l]),
        }
        if l + 1 < depth:
            base["gmix_a"] = _g8(norm_mix_g[l + 1])
            base["w_mix"] = cw(w_in[l + 1][:, :OFF_G])
            E = _run(nc_ca, [dict(base, xT=xT[c], yT=yts[c]) for c in range(NCORE)])
            xT = [np.ascontiguousarray(E[c]["xoT"]) for c in range(NCORE)]
        else:
            base["gfin"] = _g8(norm_final_g)
            R = _run(nc_cf, [dict(base, xT=xT[c], yT=yts[c]) for c in range(NCORE)])
            out = np.stack([R[c]["outT"].T for c in range(NCORE)], axis=0)
    return np.ascontiguousarray(out.reshape(B_, S_, D_).astype(f32))


def _wa_cols(w, g):
    s = slice(g * 256, (g + 1) * 256)
    segs = [w[:, OFF_Q:OFF_Q + 1024][:, s], w[:, OFF_K:OFF_K + 1024][:, s], w[:, OFF_V:OFF_V + 1024][:, s],
            w[:, OFF_O:OFF_O + 1024][:, s], w[:, OFF_SBV:OFF_SBV + 1024][:, s],
            w[:, OFF_IF + g:OFF_IF + g + 1], w[:, OFF_IF + 4 + g:OFF_IF + 4 + g + 1],
            w[:, OFF_SC:OFF_SC + 1024][:, s], w[:, OFF_SC + 1024:OFF_SC + 2048][:, s], w[:, OFF_SC + 2048:OFF_SC + 3072][:, s],
            w[:, OFF_SBQ:OFF_SBQ + 1024][:, s], w[:, OFF_SBK:OFF_SBK + 1024][:, s]]
    return np.ascontiguousarray(np.concatenate(segs, axis=1))


def _yidx(q):
    idx = np.zeros((128, 48), np.int32)
    p = np.arange(128)
    for kc in range(24):
        br, rem = divmod(kc, 8)
        g, rr = divmod(rem, 2)
        row = g * 768 + br * 256 + rr * 128 + p
        for st in range(2):
            idx[:, kc * 2 + st] = row * 8 + q * 2 + st
    return idx


def kernel(x, norm_mix_g, w_in, b_if, ml_norm_g, conv_w, w_ml_proj, w_sc_proj, w_sb_proj, w_out,
           norm_mlp_g, w_up, w_down, norm_final_g):
    f32 = np.float32
    x = np.asarray(x, f32)
    B_, S_, D_ = x.shape
    xf = x.reshape(NCORE, TOK, D_)

    def cw(a):
        return np.ascontiguousarray(np.asarray(a, f32))

    nc = _get("fused", build_fused)
    shared = {"gfin": _g8(norm_final_g)}
    for l in range(DEPTH):
        t = "%d" % l
        shared.update({
            "gmix" + t: _g8(norm_mix_g[l]), "w_g" + t: cw(w_in[l][:, OFF_G:]),
            "w_ml" + t: cw(w_ml_proj[l]), "w_sc" + t: cw(w_sc_proj[l]), "w_sb" + t: cw(w_sb_proj[l]),
            "w_out" + t: cw(w_out[l]), "gmlp" + t: _g8(norm_mlp_g[l]), "w_up" + t: cw(w_up[l]), "w_down" + t: cw(w_down[l])})
    pergroup = []
    for g in range(4):
        d = {"yidx": _yidx(g)}
        s = slice(g * 256, (g + 1) * 256)
        for l in range(DEPTH):
            t = "%d" % l
            d["wA" + t] = _wa_cols(np.asarray(w_in[l], f32), g)
            d["bif" + t] = np.ascontiguousarray(np.broadcast_to(
                np.array([b_if[l, g], b_if[l, 4 + g]], f32), (128, 2)))
            d["mlg" + t] = np.ascontiguousarray(np.broadcast_to(np.asarray(ml_norm_g[l, s], f32), (64, 256)))
            d["convw" + t] = np.ascontiguousarray(np.asarray(conv_w[l], f32)[:, s].reshape(3, 2, 128).transpose(2, 1, 0))
        pergroup.append(d)
    in_maps = []
    for c in range(NCORE):
        d = dict(shared)
        d.update(pergroup[c % 4])
        d["xT"] = np.ascontiguousarray(xf[c].T)
        in_maps.append(d)
    R = _run(nc, in_maps)
    out = np.stack([R[c]["outT"].T for c in range(NCORE)], axis=0)
    return np.ascontiguousarray(out.reshape(B_, S_, D_).astype(f32))
```
